# Optimizing a Trainium2 kernel written in Bass

```python
import math
import jax
import jax.numpy as jnp
from jax import lax
import numpy as np


D_MODEL = 1024
BATCH = 8
SEQ = 4096
DEPTH = 2
DEC_BATCH = 8
DEC_SEQ = 8192
PAST_LEN = 128

GRID_W = 64
D_MIX = D_MODEL
ATT_WIDTH = D_MIX // 2
HEAD_DIM = 64
N_HEADS = ATT_WIDTH // HEAD_DIM
N_KV_HEADS = 2
KV_GROUP = N_HEADS // N_KV_HEADS
AXIS_DIM = HEAD_DIM // 2
ROPE_THETA = 10000.0
Q_BLOCK = 128
GLA_WIDTH = D_MIX - ATT_WIDTH
GLA_HEADS = 4
GLA_DV = GLA_WIDTH // GLA_HEADS
GLA_DK = GLA_DV // 2
GLA_KEY_WIDTH = GLA_HEADS * GLA_DK
GATE_RANK = 16
GATE_TAU = 16.0
CHUNK = 64
D_FF = 2816
EPS = 1e-6

SPLIT_SIZES = (ATT_WIDTH, N_KV_HEADS * HEAD_DIM, N_KV_HEADS * HEAD_DIM,
               GLA_KEY_WIDTH, GLA_KEY_WIDTH, GLA_WIDTH, GLA_WIDTH, GATE_RANK, GATE_RANK)
D_IN_PROJ = ATT_WIDTH + 2 * N_KV_HEADS * HEAD_DIM + 2 * GLA_KEY_WIDTH + 2 * GLA_WIDTH + 2 * GATE_RANK

kernel_name = 'hymba_axial_gqa_bigla_macaron_encoder'


def rmsnorm(x, gain):
    xf = x.astype(jnp.float32)
    y = xf * lax.rsqrt(jnp.mean(xf * xf, axis=-1, keepdims=True) + EPS)
    return (y * gain.astype(jnp.float32)).astype(x.dtype)


def swiglu(x, w_gate_up, w_down):
    g, u = jnp.split(x @ w_gate_up, 2, axis=-1)
    return (jax.nn.silu(g) * u) @ w_down


def split_columns(u):
    parts = []
    off = 0
    for w in SPLIT_SIZES:
        parts.append(u[..., off:off + w])
        off += w
    return parts


def axial_rope_tables(seq_len):
    rows = seq_len // GRID_W
    row = jnp.repeat(jnp.arange(rows, dtype=jnp.float32), GRID_W)
    col = jnp.tile(jnp.arange(GRID_W, dtype=jnp.float32), rows)
    inv_freq = 1.0 / (ROPE_THETA ** (jnp.arange(0, AXIS_DIM, 2, dtype=jnp.float32) / AXIS_DIM))
    ang_r = row[:, None] * inv_freq[None, :]
    ang_c = col[:, None] * inv_freq[None, :]
    ang = jnp.concatenate([ang_r, ang_r, ang_c, ang_c], axis=-1)
    return jnp.cos(ang), jnp.sin(ang)


def _rotate_half(t):
    a, b = jnp.split(t, 2, axis=-1)
    return jnp.concatenate([-b, a], axis=-1)


def rotate_half_axial(x):
    xr, xc = jnp.split(x, 2, axis=-1)
    return jnp.concatenate([_rotate_half(xr), _rotate_half(xc)], axis=-1)


def axial_gqa_attention(q, k, v, q_gain, k_gain, cos, sin):
    bsz, seq, _ = q.shape
    out_dtype = q.dtype
    q = rmsnorm(q.reshape(bsz, seq, N_KV_HEADS, KV_GROUP, HEAD_DIM), q_gain).astype(jnp.float32)
    k = rmsnorm(k.reshape(bsz, seq, N_KV_HEADS, HEAD_DIM), k_gain).astype(jnp.float32)
    v = v.reshape(bsz, seq, N_KV_HEADS, HEAD_DIM).astype(jnp.float32)
    q = (q * cos[None, :, None, None, :] + rotate_half_axial(q) * sin[None, :, None, None, :]) * (HEAD_DIM ** -0.5)
    k = k * cos[None, :, None, :] + rotate_half_axial(k) * sin[None, :, None, :]
    n_blocks = seq // Q_BLOCK
    q_blocks = jnp.moveaxis(q.reshape(bsz, n_blocks, Q_BLOCK, N_KV_HEADS, KV_GROUP, HEAD_DIM), 1, 0)

    def attend_block(qb):
        s = jnp.einsum('bqkgd,bskd->bkgqs', qb, k)
        p = jax.nn.softmax(s, axis=-1)
        return jnp.einsum('bkgqs,bskd->bqkgd', p, v)

    o = lax.map(attend_block, q_blocks)
    o = jnp.moveaxis(o, 0, 1).reshape(bsz, seq, ATT_WIDTH)
    return o.astype(out_dtype)


def gla_chunked(q, k, v, log_a, inclusive):
    bsz, nh, seq, dk = q.shape
    dv = v.shape[-1]
    n = seq // CHUNK
    q = q.reshape(bsz, nh, n, CHUNK, dk)
    k = k.reshape(bsz, nh, n, CHUNK, dk)
    v = v.reshape(bsz, nh, n, CHUNK, dv)
    b = jnp.cumsum(log_a.reshape(bsz, nh, n, CHUNK, dk), axis=-2)
    b_last = b[..., -1:, :]
    b_ref = b[..., CHUNK // 2:CHUNK // 2 + 1, :]
    qr = q * jnp.exp(b - b_ref)
    kr = k * jnp.exp(b_ref - b)
    a_intra = jnp.einsum('bhnid,bhnjd->bhnij', qr, kr)
    mask = jnp.tril(jnp.ones((CHUNK, CHUNK), dtype=bool), k=0 if inclusive else -1)
    a_intra = jnp.where(mask, a_intra, 0.0)
    o_intra = jnp.einsum('bhnij,bhnje->bhnie', a_intra, v)
    d_state = jnp.einsum('bhncd,bhnce->bhnde', k * jnp.exp(b_last - b), v)
    chunk_decay = jnp.exp(b_last[..., 0, :])

    def step(state, xs):
        dec, ds = xs
        return dec[..., None] * state + ds, state

    s0 = jnp.zeros((bsz, nh, dk, dv), jnp.float32)
    _, s_before = lax.scan(step, s0, (jnp.moveaxis(chunk_decay, 2, 0), jnp.moveaxis(d_state, 2, 0)))
    s_before = jnp.moveaxis(s_before, 0, 2)
    o_inter = jnp.einsum('bhncd,bhnde->bhnce', q * jnp.exp(b), s_before)
    return (o_intra + o_inter).reshape(bsz, nh, seq, dv)


def gla_mixer(q, k, v, r, gf_low, gb_low, w_gf, b_gf, w_gb, b_gb, gla_gain):
    bsz, seq, _ = q.shape
    out_dtype = q.dtype

    def to_heads(t, d):
        return t.reshape(bsz, seq, GLA_HEADS, d).transpose(0, 2, 1, 3).astype(jnp.float32)

    qh = to_heads(q, GLA_DK) * (GLA_DK ** -0.5)
    kh = to_heads(k, GLA_DK)
    vh = to_heads(v, GLA_DV)
    log_af = jax.nn.log_sigmoid((gf_low @ w_gf + b_gf).astype(jnp.float32)) / GATE_TAU
    log_ab = jax.nn.log_sigmoid((gb_low @ w_gb + b_gb).astype(jnp.float32)) / GATE_TAU
    o_f = gla_chunked(qh, kh, vh, to_heads(log_af, GLA_DK), True)

    def flip(t):
        return jnp.flip(t, axis=2)

    o_b = flip(gla_chunked(flip(qh), flip(kh), flip(vh), flip(to_heads(log_ab, GLA_DK)), False))
    o = rmsnorm(o_f + o_b, gla_gain)
    o = o.transpose(0, 2, 1, 3).reshape(bsz, seq, GLA_WIDTH)
    return (o * jax.nn.silu(r.astype(jnp.float32))).astype(out_dtype)


def encoder_layer(x, cos, sin, n_ffn1, w_ffn1_gu, w_ffn1_down, n_mix, w_in, q_gain, k_gain,
                  w_gf, b_gf, w_gb, b_gb, gla_gain, w_out, n_ffn2, w_ffn2_gu, w_ffn2_down, n_out):
    h = x + 0.5 * swiglu(rmsnorm(x, n_ffn1), w_ffn1_gu, w_ffn1_down)
    u = rmsnorm(h, n_mix) @ w_in
    q_a, k_a, v_a, q_l, k_l, v_l, r_l, gf_low, gb_low = split_columns(u)
    o_att = axial_gqa_attention(q_a, k_a, v_a, q_gain, k_gain, cos, sin)
    o_gla = gla_mixer(q_l, k_l, v_l, r_l, gf_low, gb_low, w_gf, b_gf, w_gb, b_gb, gla_gain)
    h = h + jnp.concatenate([o_att, o_gla], axis=-1) @ w_out
    h = h + 0.5 * swiglu(rmsnorm(h, n_ffn2), w_ffn2_gu, w_ffn2_down)
    return rmsnorm(h, n_out)


def trunk(x, norm_ffn1, w_ffn1_gu, w_ffn1_down, norm_mix, w_in, q_norm, k_norm,
          w_gate_f, b_gate_f, w_gate_b, b_gate_b, gla_norm, w_out,
          norm_ffn2, w_ffn2_gu, w_ffn2_down, norm_out):
    cos, sin = axial_rope_tables(x.shape[1])
    h = x
    for l in range(DEPTH):
        h = encoder_layer(h, cos, sin, norm_ffn1[l], w_ffn1_gu[l], w_ffn1_down[l], norm_mix[l], w_in[l],
                          q_norm[l], k_norm[l], w_gate_f[l], b_gate_f[l], w_gate_b[l], b_gate_b[l],
                          gla_norm[l], w_out[l], norm_ffn2[l], w_ffn2_gu[l], w_ffn2_down[l], norm_out[l])
    return h


def setup_inputs(seed: int = 0) -> dict:
    key = jax.random.key(seed)
    ks = jax.random.split(key, 20)

    def w(k, shape, fan_in):
        return jax.random.normal(k, shape, jnp.float32) * (fan_in ** -0.5)

    def gain(k, shape):
        return 1.0 + 0.05 * jax.random.normal(k, shape, jnp.float32)

    return {
        'x_prompt': jax.random.normal(ks[0], (BATCH, SEQ, D_MODEL), jnp.float32),
        'x_sample': jax.random.normal(ks[1], (DEC_BATCH, DEC_SEQ, D_MODEL), jnp.float32),
        'norm_ffn1': gain(ks[2], (DEPTH, D_MODEL)),
        'w_ffn1_gu': w(ks[3], (DEPTH, D_MODEL, 2 * D_FF), D_MODEL),
        'w_ffn1_down': w(ks[4], (DEPTH, D_FF, D_MODEL), D_FF),
        'norm_mix': gain(ks[5], (DEPTH, D_MODEL)),
        'w_in': w(ks[6], (DEPTH, D_MODEL, D_IN_PROJ), D_MODEL),
        'q_norm': gain(ks[7], (DEPTH, HEAD_DIM)),
        'k_norm': gain(ks[8], (DEPTH, HEAD_DIM)),
        'w_gate_f': w(ks[9], (DEPTH, GATE_RANK, GLA_KEY_WIDTH), GATE_RANK),
        'b_gate_f': 0.1 * jax.random.normal(ks[10], (DEPTH, GLA_KEY_WIDTH), jnp.float32),
        'w_gate_b': w(ks[11], (DEPTH, GATE_RANK, GLA_KEY_WIDTH), GATE_RANK),
        'b_gate_b': 0.1 * jax.random.normal(ks[12], (DEPTH, GLA_KEY_WIDTH), jnp.float32),
        'gla_norm': gain(ks[13], (DEPTH, GLA_DV)),
        'w_out': w(ks[14], (DEPTH, D_MIX, D_MODEL), D_MIX),
        'norm_ffn2': gain(ks[15], (DEPTH, D_MODEL)),
        'w_ffn2_gu': w(ks[16], (DEPTH, D_MODEL, 2 * D_FF), D_MODEL),
        'w_ffn2_down': w(ks[17], (DEPTH, D_FF, D_MODEL), D_FF),
        'norm_out': gain(ks[18], (DEPTH, D_MODEL)),
    }


def reference(x_prompt, x_sample, norm_ffn1, w_ffn1_gu, w_ffn1_down, norm_mix, w_in, q_norm, k_norm,
              w_gate_f, b_gate_f, w_gate_b, b_gate_b, gla_norm, w_out,
              norm_ffn2, w_ffn2_gu, w_ffn2_down, norm_out):
    y_prompt = trunk(x_prompt, norm_ffn1, w_ffn1_gu, w_ffn1_down, norm_mix, w_in, q_norm, k_norm,
                     w_gate_f, b_gate_f, w_gate_b, b_gate_b, gla_norm, w_out,
                     norm_ffn2, w_ffn2_gu, w_ffn2_down, norm_out)
    y_sample = trunk(x_sample, norm_ffn1, w_ffn1_gu, w_ffn1_down, norm_mix, w_in, q_norm, k_norm,
                     w_gate_f, b_gate_f, w_gate_b, b_gate_b, gla_norm, w_out,
                     norm_ffn2, w_ffn2_gu, w_ffn2_down, norm_out)
    return (y_prompt, y_sample)
```

```python
import numpy as np
import concourse.bass as bass
import concourse.mybir as mybir
from concourse.bass_utils import run_bass_kernel_spmd

F32 = mybir.dt.float32
BF16 = mybir.dt.bfloat16
AF = mybir.ActivationFunctionType
ALU = mybir.AluOpType

D = 1024
KC = 8
DFF = 2816
JC = 22
DIN = 2336
EPS = 1e-6
TB = 512
NQ = 12


class Op:
    __slots__ = ("eng", "fn", "dma", "deps", "signal", "sem", "val", "prev_val")

    def __init__(self, eng, fn, dma):
        self.eng = eng
        self.fn = fn
        self.dma = dma
        self.deps = []
        self.signal = False
        self.sem = None
        self.val = 0
        self.prev_val = 0


class Prog:
    ENGS = ["pe", "act", "dve", "pool", "sp"]

    def __init__(self, nc):
        self.nc = nc
        self.sem = {e: nc.alloc_semaphore("s_" + e) for e in ["pe", "act", "dve", "pool"]}
        self.cnt = {e: 0 for e in self.sem}
        self.dsem = {q: [nc.alloc_semaphore("d_%s_%d" % (q, i)) for i in range(NQ)] for q in ["sp", "act"]}
        self.dcnt = {q: 0 for q in self.dsem}
        self.dlast = {q: [0] * NQ for q in self.dsem}
        self.waited = {}
        self.reset()
        allsems = list(self.sem.values()) + [s for q in self.dsem for s in self.dsem[q]]
        with nc.Block() as block:
            def clr(e):
                for s in allsems:
                    e.sem_clear(s)
            block.sync(clr)

    def reset(self):
        self.ops = []
        self.writers = {}
        self.readers = {}

    def op(self, eng, fn, reads=(), writes=(), dma=False):
        o = Op(eng, fn, dma)
        idx = len(self.ops)
        deps = {}
        for k in reads:
            for w in self.writers.get(k, ()):
                deps[w] = "RAW"
        for k in writes:
            rs = self.readers.get(k, ())
            for w in self.writers.get(k, ()):
                deps.setdefault(w, "WAW")
            for r in rs:
                deps.setdefault(r, "WAR")
        deps.pop(idx, None)
        o.deps = list(deps.items())
        def keep(lst):
            if dma:
                return lst
            return [i for i in lst if self.ops[i].dma or self.ops[i].eng != eng]
        for k in reads:
            self.readers[k] = keep(self.readers.get(k, [])) + [idx]
        for k in writes:
            if self.readers.get(k):
                self.writers[k] = [idx]
                self.readers[k] = []
            else:
                self.writers[k] = keep(self.writers.get(k, [])) + [idx]
        self.ops.append(o)
        return o

    def _needs_sem(self, p, o, kind):
        if p.dma:
            return True
        if p.eng != o.eng or o.dma:
            return True
        if p.eng == "pe":
            return False
        return True

    def flush(self, name=None):
        nc = self.nc
        ops = self.ops
        if not ops:
            return
        for o in ops:
            for d, kind in o.deps:
                p = ops[d]
                if not p.dma and self._needs_sem(p, o, kind):
                    p.signal = True
        for o in ops:
            if o.dma:
                q = o.eng
                k = self.dcnt[q]
                self.dcnt[q] += 1
                o.sem = (q, k % NQ)
                o.val = 16 * (k // NQ + 1)
                o.prev_val = 16 * (k // NQ)
                self.dlast[q][k % NQ] = o.val
            elif o.signal:
                self.cnt[o.eng] += 1
                o.sem = o.eng
                o.val = self.cnt[o.eng]

        def semobj(key):
            if isinstance(key, tuple):
                return self.dsem[key[0]][key[1]]
            return self.sem[key]

        bname = {"pe": "tensor", "act": "scalar", "dve": "vector", "pool": "gpsimd", "sp": "sync"}
        with nc.Block() as block:
            for ename in self.ENGS:
                elist = [o for o in ops if o.eng == ename]

                def body(e, elist=elist, ename=ename):
                    for o in elist:
                        waits = {}
                        for d, kind in o.deps:
                            p = ops[d]
                            if p.sem is None or not self._needs_sem(p, o, kind):
                                continue
                            if waits.get(p.sem, 0) < p.val:
                                waits[p.sem] = p.val
                        if o.dma and o.prev_val > 0:
                            if waits.get(o.sem, 0) < o.prev_val:
                                waits[o.sem] = o.prev_val
                        for sk, val in waits.items():
                            if self.waited.get((ename, sk), 0) < val:
                                e.wait_ge(semobj(sk), val)
                                self.waited[(ename, sk)] = val
                        ins = o.fn(e)
                        if o.sem is not None:
                            ins.then_inc(semobj(o.sem), 16 if o.dma else 1)
                    if ename in self.dsem:
                        for i in range(NQ):
                            v = self.dlast[ename][i]
                            if v > 0 and self.waited.get((ename, (ename, i)), 0) < v:
                                e.wait_ge(self.dsem[ename][i], v)
                                self.waited[(ename, (ename, i))] = v

                getattr(block, bname[ename])(body)
        self.reset()


def _prod(s):
    r = 1
    for v in s:
        r *= v
    return r


_RE = {2: None, 3: "p (a b) -> p a b", 4: "p (a b c) -> p a b c"}


class Arena:
    def __init__(self, nc, nbytes):
        self.n = nbytes
        self.t = nc.alloc_sbuf_tensor("arena", [128, nbytes // 2], BF16)

    def view(self, off, dtype, shape):
        n = _prod(shape[1:])
        esz = 4 if dtype == F32 else 2
        assert off % 4 == 0 and off + n * esz <= self.n, (off, n * esz, self.n)
        ap = self.t[0:shape[0], off // 2: off // 2 + n * esz // 2]
        if dtype != BF16:
            ap = ap.bitcast(dtype)
        if len(shape) == 3:
            ap = ap.rearrange("p (a b) -> p a b", a=shape[1])
        elif len(shape) == 4:
            ap = ap.rearrange("p (a b c) -> p a b c", a=shape[1], b=shape[2])
        return ap


def build(seqs, depth=2, dbg=False, stop_after=None):
    nc = bass.Bass("TRN2", target_bir_lowering=False)
    P = Prog(nc)
    L = depth
    SMAX = max(S for _, S in seqs)
    NCHMAX = SMAX // 128
    K = 1024

    def din(name, shape, dt=F32):
        return nc.dram_tensor(name, list(shape), dt, kind="ExternalInput").ap()

    def dscr(name, shape, dt):
        kind = "ExternalOutput" if dbg else "Internal"
        return nc.dram_tensor(name, list(shape), dt, kind=kind).ap()

    xin = {n: din("x_" + n, [S, D]) for n, S in seqs}
    yout = {n: nc.dram_tensor("y_" + n, [S, D], F32, kind="ExternalOutput").ap() for n, S in seqs}
    W = {}
    for nm, shp in [("norm_ffn1", [L, D]), ("w_ffn1_gu", [L, D, 2 * DFF]), ("w_ffn1_down", [L, DFF, D]),
                    ("norm_mix", [L, D]), ("w_in", [L, D, DIN]), ("q_norm", [L, 64]), ("k_norm", [L, 64]),
                    ("w_gate_f", [L, 16, 256]), ("b_gate_f", [L, 256]), ("w_gate_b", [L, 16, 256]),
                    ("b_gate_b", [L, 256]), ("gla_norm", [L, 128]), ("w_out", [L, D, D]),
                    ("norm_ffn2", [L, D]), ("w_ffn2_gu", [L, D, 2 * DFF]), ("w_ffn2_down", [L, DFF, D]),
                    ("norm_out", [L, D])]:
        W[nm] = din(nm, shp)
    c_ident = din("c_ident", [128, 128])
    c_rot = din("c_rot", [128, 128])
    c_blk = din("c_blk", [128, 128])
    c_m2 = din("c_m2", [128, 256])
    c_reset = din("c_reset", [128, 512])
    c_cos = din("c_cos", [128, SMAX])
    c_sin = din("c_sin", [128, SMAX])

    WGU = [[dscr("WGU%d%d" % (l, f), [JC, 128, 2 * 8 * 128], BF16) for f in range(2)] for l in range(L)]
    WDN = [[dscr("WDN%d%d" % (l, f), [8, 128, 22 * 128], BF16) for f in range(2)] for l in range(L)]
    WINL = [dscr("WINL%d" % l, [18, 128, 8 * 128], BF16) for l in range(L)]
    WINR = [dscr("WINR%d" % l, [128, 8 * 640], BF16) for l in range(L)]
    WOUT = [dscr("WOUT%d" % l, [8, 128, 8 * 128], BF16) for l in range(L)]
    HS = dscr("HS", [8, 128, SMAX], F32)
    QS = dscr("QS", [4, 128, SMAX], BF16)
    GQ = dscr("GQ", [4, 128, SMAX], BF16)
    GK = dscr("GK", [4, 128, SMAX], BF16)
    GKT = dscr("GKT", [4, NCHMAX, 128, 128], BF16)
    GV = dscr("GV", [NCHMAX, 128, 512], BF16)
    GR = dscr("GR", [4, 128, SMAX], F32)
    OAS = dscr("OAS", [4, 128, SMAX], BF16)
    OG = dscr("OG", [4, 128, SMAX], BF16)

    sb = nc.alloc_sbuf_tensor
    ID32 = sb("ID32", [128, 128], F32)
    ID16 = sb("ID16", [128, 128], BF16)
    ROT32 = sb("ROT32", [128, 128], F32)
    ONES16 = sb("ONES16", [128, 128], BF16)
    BLK16 = sb("BLK16", [128, 128], BF16)
    M2 = sb("M2", [128, 256], F32)
    RESET = sb("RESET", [128, 512], F32)
    SV = sb("SV", [128, 128], F32)
    WG16 = sb("WG16", [32, L * 4 * 128], BF16)
    EPSC = sb("EPSC", [128, 1], F32)
    ps = [nc.alloc_psum_tensor("ps%d" % i, [128, 512], F32) for i in range(8)]
    ARENA_BYTES = 192 * 1024
    A = Arena(nc, ARENA_BYTES)

    OFF_N, OFF_QK, OFF_GLA, OFF_BG = 0, 32 * L, 34 * L, 35 * L

    def gcol(which, l, kc):
        c = OFF_N + which * L * 8 + l * 8 + kc
        return SV[:, c:c + 1]

    state = {"ps": 0}

    def psn():
        i = state["ps"] % 8
        state["ps"] += 1
        return ps[i], ("ps", i)

    def dma(out, in_, reads=(), writes=(), q="sp"):
        return P.op(q, lambda e: e.dma_start(out=out, in_=in_), reads=reads, writes=writes, dma=True)

    KiB = 1024

    def prep():
        stg32 = [A.view(i * 66 * KiB, F32, [128, 11264]) for i in range(2)]
        stg16 = [A.view(i * 66 * KiB + 44 * KiB, BF16, [128, 11264]) for i in range(2)]
        small = A.view(132 * KiB, F32, [128, 128])
        wg32 = A.view(133 * KiB, F32, [32, L * 4 * 128])
        tmpc = A.view(140 * KiB, F32, [128, 128])
        dma(ID32[:], c_ident, writes=["ID32"])
        dma(ROT32[:], c_rot, writes=["ROT32"])
        dma(M2[:], c_m2, writes=["M2"])
        dma(RESET[:], c_reset, writes=["RESET"])
        dma(tmpc, c_blk, writes=["tmpc"])
        P.op("dve", lambda e: e.tensor_copy(out=BLK16[:], in_=tmpc), reads=["tmpc"], writes=["BLK16"])
        P.op("dve", lambda e: e.tensor_copy(out=ID16[:], in_=ID32[:]), reads=["ID32"], writes=["ID16"])
        P.op("pool", lambda e: e.memset(ONES16[:], 1.0), writes=["ONES16"])
        P.op("pool", lambda e: e.memset(EPSC[:], EPS), writes=["EPSC"])
        P.op("pool", lambda e: e.memset(small, 0.0), writes=["small"])
        for wi, nm in enumerate(["norm_ffn1", "norm_mix", "norm_ffn2", "norm_out"]):
            r0 = OFF_N + wi * L * 8
            dma(small[r0:r0 + L * 8, :], W[nm].rearrange("l (kc p) -> (l kc) p", p=128), reads=["small0"], writes=["small"])
        for wi, nm in enumerate(["q_norm", "k_norm"]):
            r0 = OFF_QK + wi * L
            dma(small[r0:r0 + L, 0:64], W[nm], reads=["small0"], writes=["small"])
            dma(small[r0:r0 + L, 64:128], W[nm], reads=["small0"], writes=["small"])
        dma(small[OFF_GLA:OFF_GLA + L, :], W["gla_norm"], reads=["small0"], writes=["small"])
        dma(small[OFF_BG:OFF_BG + 4 * L, 0:64], W["b_gate_f"].rearrange("l (h n) -> (l h) n", n=64), reads=["small0"], writes=["small"])
        dma(small[OFF_BG:OFF_BG + 4 * L, 64:128], W["b_gate_b"].rearrange("l (h n) -> (l h) n", n=64), reads=["small0"], writes=["small"])
        pt, pk = psn()
        P.op("pe", lambda e: e.transpose(out=pt[:, 0:128], in_=small, identity=ID32[:]), reads=["small", "ID32"], writes=[pk])
        P.op("dve", lambda e: e.tensor_copy(out=SV[:], in_=pt[:, 0:128]), reads=[pk], writes=["SV"])
        P.op("dve", lambda e: e.tensor_scalar(out=SV[:, OFF_BG:OFF_BG + 4 * L], in0=SV[:, OFF_BG:OFF_BG + 4 * L],
                                              scalar1=-1.0, scalar2=None, op0=ALU.mult), reads=["SV"], writes=["SV"])
        P.op("pool", lambda e: e.memset(wg32, 0.0), writes=["wg32"])
        wgv = wg32.rearrange("p (l h n) -> p l h n", l=L, h=4)
        for l in range(L):
            dma(wgv[0:16, l, :, 0:64], W["w_gate_f"][l].rearrange("r (h n) -> r h n", n=64), reads=["wg320"], writes=["wg32"])
            dma(wgv[16:32, l, :, 64:128], W["w_gate_b"][l].rearrange("r (h n) -> r h n", n=64), reads=["wg320"], writes=["wg32"])
        P.op("dve", lambda e: e.tensor_copy(out=WG16[:], in_=wg32), reads=["wg32"], writes=["WG16"])

        st = {"i": 0}
        engs = ["dve", "act", "pool"]

        def cast(out, in_, r, w):
            en = engs[st["i"] % 3]
            st["i"] += 1
            if en == "act":
                P.op("act", lambda e: e.activation(out=out, in_=in_, func=AF.Copy), reads=r, writes=w)
            else:
                P.op(en, lambda e: e.tensor_copy(out=out, in_=in_), reads=r, writes=w)

        pi = {"i": 0}

        def plain(src, nkc, c0, ncols, dst_fn):
            i = pi["i"] % 2
            pi["i"] += 1
            s32 = stg32[i][:, 0:nkc * ncols].rearrange("p (kc n) -> p kc n", kc=nkc)
            nm = ncols // 128
            half = (nkc + 1) // 2
            srcv = src[:, c0:c0 + ncols].rearrange("(kc p) n -> p kc n", p=128)
            dma(s32[:, 0:half, :], srcv[:, 0:half, :], writes=[("s32", i)])
            dma(s32[:, half:nkc, :], srcv[:, half:nkc, :], writes=[("s32", i)])
            s16 = stg16[i][:, 0:nkc * ncols].rearrange("p (m kc c) -> p m kc c", m=nm, kc=nkc)
            s32p = stg32[i][:, 0:nkc * ncols].rearrange("p (kc m c) -> p m kc c", kc=nkc, m=nm)
            step = max(1, nm // 3)
            for m0 in range(0, nm, step):
                m1 = min(nm, m0 + step)
                cast(s16[:, m0:m1], s32p[:, m0:m1], [("s32", i)], [("s16", i)])
            dst_fn(stg16[i][:, 0:nkc * ncols].rearrange("p (m n) -> p m n", m=nm), ("s16", i))

        for l in range(L):
            for f in range(2):
                src = W["w_ffn%d_gu" % (f + 1)][l]
                for t in range(2):
                    for jg in range(2):
                        def dst_fn(s16v, key, l=l, f=f, t=t, jg=jg):
                            dv = WGU[l][f][11 * jg:11 * jg + 11].rearrange("j p (t n) -> p j t n", t=2)[:, :, t, :]
                            dma(dv, s16v, reads=[key], writes=["WGU"])
                        plain(src, 8, t * DFF + jg * 1408, 1408, dst_fn)
                src = W["w_ffn%d_down" % (f + 1)][l]
                for mh in range(2):
                    def dst_fn(s16v, key, l=l, f=f, mh=mh):
                        dv = WDN[l][f][4 * mh:4 * mh + 4].rearrange("m p n -> p m n")
                        dma(dv, s16v, reads=[key], writes=["WDN"])
                    plain(src, 22, mh * 512, 512, dst_fn)
            def dst_fn(s16v, key, l=l):
                dma(WOUT[l].rearrange("m p n -> p m n"), s16v, reads=[key], writes=["WOUT"])
            plain(W["w_out"][l], 8, 0, 1024, dst_fn)
            src = W["w_in"][l]
            i = pi["i"] % 2
            pi["i"] += 1
            sA = stg32[i][:, 0:8 * 1280].rearrange("p (kc n) -> p kc n", kc=8)
            srcv = src[:, 0:1280].rearrange("(kc p) n -> p kc n", p=128)
            dma(sA[:, 0:4, :], srcv[:, 0:4, :], writes=[("s32", i)])
            dma(sA[:, 4:8, :], srcv[:, 4:8, :], writes=[("s32", i)])
            T16 = A.view(152 * KiB, BF16, [128, 18 * 1024]).rearrange("p (t kc c) -> p t kc c", t=18, kc=8)
            k32, k16 = [("s32", i)], ["T16"]
            cast(T16[:, 1:5, :, 0:64], sA[:, :, 0:256].rearrange("p kc (c n) -> p c kc n", n=64), k32, k16)
            cast(T16[:, 1:5, :, 64:128], sA[:, :, 256:512].rearrange("p kc (c n) -> p c kc n", n=64), k32, k16)
            cast(T16[:, 5, :, :], sA[:, :, 512:640], k32, k16)
            for h in range(4):
                for half in range(2):
                    cast(T16[:, 6 + 2 * h, :, 64 * half:64 * half + 64], sA[:, :, 768 + 64 * h:768 + 64 * h + 64], k32, k16)
                    cast(T16[:, 7 + 2 * h, :, 64 * half:64 * half + 64], sA[:, :, 1024 + 64 * h:1024 + 64 * h + 64], k32, k16)
            R16 = A.view(141 * KiB, BF16, [128, 8, 640])
            cast(R16[:, :, 0:128], sA[:, :, 640:768], k32, ["R16"])
            j = pi["i"] % 2
            pi["i"] += 1
            sB = stg32[j][:, 0:8 * 1056].rearrange("p (kc n) -> p kc n", kc=8)
            srcv = src[:, 1280:2336].rearrange("(kc p) n -> p kc n", p=128)
            dma(sB[:, 0:4, :], srcv[:, 0:4, :], writes=[("s32", j)])
            dma(sB[:, 4:8, :], srcv[:, 4:8, :], writes=[("s32", j)])
            kb = [("s32", j)]
            cast(R16[:, :, 128:640], sB[:, :, 0:512], kb, ["R16"])
            cast(T16[:, 14:18, :, :], sB[:, :, 512:1024].rearrange("p kc (h n) -> p h kc n", n=128), kb, k16)
            P.op("pool", lambda e, T16=T16: e.memset(T16[:, 0, :, 32:128], 0.0), writes=k16)
            cast(T16[:, 0, :, 0:32], sB[:, :, 1024:1056], kb, k16)
            dma(WINL[l].rearrange("t p n -> p t n"), A.view(152 * KiB, BF16, [128, 18, 1024]),
                reads=k16, writes=["WINL"])
            dma(WINR[l], R16.rearrange("p kc n -> p (kc n)"), reads=["R16"], writes=["WINR"])
        P.flush()

    prep()
    if stop_after == "prep":
        return nc

    o = 0
    def take(nbytes):
        nonlocal o
        r = o
        o += nbytes
        return r
    O_KT = take(2 * SMAX)
    O_VO = take(NCHMAX * 2 * 128 * 2)
    O_DEC = take(4 * NCHMAX * 4)
    O_MAIN = o
    KT = A.view(O_KT, BF16, [128, SMAX])
    VO = A.view(O_VO, BF16, [128, NCHMAX, 2, 128])
    DEC = A.view(O_DEC, F32, [128, 4, NCHMAX, 1])
    XT = A.view(take(16 * KiB), F32, [128, 8, 512])
    XN = A.view(take(8 * KiB), BF16, [128, 8, 512])
    O_BIG = take(22 * KiB)
    ACTT = A.view(O_BIG, BF16, [128, 22, 512])
    XTOK = A.view(O_BIG, F32, [128, 4, 1024])
    SQ = A.view(O_BIG + 16 * KiB, BF16, [128, 4, 512])
    RS = A.view(take(2 * KiB), F32, [128, 512])
    TMPN = A.view(take(2 * KiB), F32, [128, 512])
    SG = A.view(take(4 * KiB), F32, [128, 2, 512])
    NW = 8
    WR = [A.view(take(4 * KiB), BF16, [128, 2048]) for _ in range(NW)]
    OC = A.view(take(8 * KiB), BF16, [128, 8, 512])
    CS = A.view(take(4 * KiB), F32, [128, 2, 512])
    U32 = A.view(take(2 * KiB), F32, [128, 512])
    QN32 = A.view(take(2 * KiB), F32, [128, 512])
    T132 = A.view(take(2 * KiB), F32, [128, 512])
    T232 = A.view(take(2 * KiB), F32, [128, 512])
    QF = A.view(take(2 * KiB), BF16, [128, 2, 512])
    SQ2 = A.view(take(1 * KiB), BF16, [128, 512])
    GLOW = A.view(take(1 * KiB), BF16, [128, 512])
    L32 = A.view(take(2 * KiB), F32, [128, 512])
    PP32 = A.view(take(2 * KiB), F32, [128, 512])
    Z32 = A.view(take(2 * KiB), F32, [128, 512])
    TZ32 = A.view(take(2 * KiB), F32, [128, 512])
    EQ32 = A.view(take(2 * KiB), F32, [128, 512])
    EK32 = A.view(take(2 * KiB), F32, [128, 512])
    QD16 = A.view(take(2 * KiB), BF16, [128, 2, 512])
    KD16 = A.view(take(2 * KiB), BF16, [128, 2, 512])
    KDS16 = A.view(take(2 * KiB), BF16, [128, 2, 512])
    KDT16 = A.view(take(2 * KiB), BF16, [128, 2, 512])
    SR32 = A.view(take(4 * KiB), F32, [128, 2, 512])
    VLT = A.view(take(4 * KiB), BF16, [128, 4, 512])
    assert o <= ARENA_BYTES, o
    O_END_MAIN = o

    carry = []

    def run_carry():
        while carry:
            carry.pop(0)()

    def bigkeys(j0, j1):
        return [("big", j) for j in range(j0, j1)]

    class WS:
        def __init__(self):
            self.plan = []
            self.loaded = 0
            self.next = 0

        def add(self, ap2d, n):
            self.plan.append((ap2d, n))

        def _load(self, i):
            ap2d, n = self.plan[i]
            s = i % NW
            dma(WR[s][:, 0:n], ap2d, writes=[("wr", s)])

        def get(self):
            i = self.next
            self.next += 1
            while self.loaded < min(len(self.plan), i + 4):
                self._load(self.loaded)
                self.loaded += 1
            assert i < self.loaded
            s = i % NW
            return WR[s], ("wr", s)

    def plan_ffn(ws, l, f):
        for j in range(JC):
            ws.add(WGU[l][f][j], 2048)
        for m in range(8):
            ws.add(WDN[l][f][m][:, 0:1408], 1408)
            ws.add(WDN[l][f][m][:, 1408:2816], 1408)

    def plan_front(ws, l):
        plan_ffn(ws, l, 0)
        def pl_(pc):
            ws.add(WINL[l][2 * pc:2 * pc + 2].rearrange("t p n -> p t n"), 2048)
        wr_ = WINR[l].rearrange("p (kc n) -> p kc n", kc=8)
        pl_(0); pl_(7); pl_(1); pl_(8); pl_(2)
        ws.add(wr_[:, :, 0:128], 1024)
        pl_(3)
        ws.add(wr_[:, 0:4, 128:640], 2048)
        ws.add(wr_[:, 4:8, 128:640], 2048)
        pl_(4); pl_(5); pl_(6)

    def plan_back(ws, l):
        for m in range(8):
            ws.add(WOUT[l][m], 1024)
        plan_ffn(ws, l, 1)

    def _load(self, i):
        ap, n = self.plan[i]
        s = i % NW
        dst = WR[s][:, 0:n]
        if len(ap.shape) == 3:
            dst = dst.rearrange("p (a b) -> p a b", a=ap.shape[1])
        dma(dst, ap, writes=[("wr", s)])
    WS._load = _load

    def rmsnorm(which, l, dst16=None):
        pn, pk = psn()
        for kc in range(8):
            if kc % 4 in (0, 2):
                P.op("act", lambda e, kc=kc: e.activation(out=XN[:, kc, :], in_=XT[:, kc, :], func=AF.Square),
                     reads=[("XT", kc)], writes=[("XN", kc)])
            else:
                en_ = "dve" if kc % 4 == 1 else "pool"
                P.op(en_, lambda e, kc=kc: e.tensor_tensor(out=XN[:, kc, :], in0=XT[:, kc, :], in1=XT[:, kc, :], op=ALU.mult),
                     reads=[("XT", kc)], writes=[("XN", kc)])
        for kc in range(8):
            P.op("pe", lambda e, kc=kc: e.matmul(pn[:], lhsT=ONES16[:], rhs=XN[:, kc, :], start=(kc == 0), stop=(kc == 7)),
                 reads=[("XN", kc), "ONES16"], writes=[pk])
        P.op("act", lambda e: e.activation(out=TMPN, in_=pn[:], func=AF.Ln, scale=1.0 / D, bias=EPSC[:]), reads=[pk], writes=["TMPN"])
        P.op("act", lambda e: e.activation(out=RS, in_=TMPN, func=AF.Exp, scale=-0.5), reads=["TMPN"], writes=["RS"])
        for kc in range(8):
            if dst16 is not None:
                P.op("dve", lambda e, kc=kc: e.scalar_tensor_tensor(out=XN[:, kc, :], in0=XT[:, kc, :], scalar=gcol(which, l, kc),
                                                                    in1=RS, op0=ALU.mult, op1=ALU.mult),
                     reads=[("XT", kc), "RS", "SV"], writes=[("XN", kc)])
            else:
                P.op("dve", lambda e, kc=kc: e.scalar_tensor_tensor(out=XT[:, kc, :], in0=XT[:, kc, :], scalar=gcol(which, l, kc),
                                                                    in1=RS, op0=ALU.mult, op1=ALU.mult),
                     reads=[("XT", kc), "RS", "SV"], writes=[("XT", kc)])

    def ffn(ws):
        for j in range(JC):
            w, wk = ws.get()
            wv = w[:, 0:2048].rearrange("p (t kc c) -> p t kc c", t=2, kc=8)
            pg, pgk = psn()
            pu, puk = psn()
            for kc in range(8):
                P.op("pe", lambda e, kc=kc, wv=wv, pg=pg: e.matmul(pg[:], lhsT=wv[:, 0, kc, :], rhs=XN[:, kc, :], start=(kc == 0), stop=(kc == 7)),
                     reads=[wk, ("XN", kc)], writes=[pgk])
            for kc in range(8):
                P.op("pe", lambda e, kc=kc, wv=wv, pu=pu: e.matmul(pu[:], lhsT=wv[:, 1, kc, :], rhs=XN[:, kc, :], start=(kc == 0), stop=(kc == 7)),
                     reads=[wk, ("XN", kc)], writes=[puk])
            r = j % 2
            P.op("act", lambda e, r=r, pg=pg: e.activation(out=SG[:, r, :], in_=pg[:], func=AF.Silu), reads=[pgk], writes=[("SG", r)])
            P.op("dve", lambda e, r=r, pu=pu, j=j: e.tensor_tensor(out=ACTT[:, j, :], in0=SG[:, r, :], in1=pu[:], op=ALU.mult),
                 reads=[("SG", r), puk], writes=[("big", j)])
        for m in range(8):
            wa, wak = ws.get()
            wb, wbk = ws.get()
            pd, pdk = psn()
            for kc in range(22):
                wsrc = wa if kc < 11 else wb
                wkk = wak if kc < 11 else wbk
                kk = kc if kc < 11 else kc - 11
                P.op("pe", lambda e, kc=kc, kk=kk, wsrc=wsrc, pd=pd: e.matmul(pd[:], lhsT=wsrc[:, kk * 128:(kk + 1) * 128], rhs=ACTT[:, kc, :],
                                                                             start=(kc == 0), stop=(kc == 21)),
                     reads=[wkk, ("big", kc)], writes=[pdk])
            P.op("dve", lambda e, m=m, pd=pd: e.scalar_tensor_tensor(out=XT[:, m, :], in0=pd[:], scalar=0.5, in1=XT[:, m, :],
                                                                     op0=ALU.mult, op1=ALU.add),
                 reads=[pdk, ("XT", m)], writes=[("XT", m)])

    def load_x_dma(x_ap, t0):
        xv = x_ap[t0:t0 + 512, :].rearrange("(g p) n -> p g n", p=128)
        dma(XTOK[:, 0:2, :], xv[:, 0:2, :], writes=bigkeys(0, 8))
        dma(XTOK[:, 2:4, :], xv[:, 2:4, :], writes=bigkeys(8, 16))

    def load_x(x_ap, t0):
        for kc in range(8):
            pt, pk = psn()
            for g in range(4):
                P.op("pe", lambda e, g=g, kc=kc, pt=pt: e.transpose(out=pt[:, g * 128:(g + 1) * 128], in_=XTOK[:, g, kc * 128:(kc + 1) * 128], identity=ID32[:]),
                     reads=bigkeys(4 * g, 4 * g + 4) + ["ID32"], writes=[pk])
            if kc % 2 == 0:
                P.op("act", lambda e, kc=kc, pt=pt: e.activation(out=XT[:, kc, :], in_=pt[:], func=AF.Copy), reads=[pk], writes=[("XT", kc)])
            else:
                P.op("dve", lambda e, kc=kc, pt=pt: e.tensor_copy(out=XT[:, kc, :], in_=pt[:]), reads=[pk], writes=[("XT", kc)])

    def store_y(y_ap, t0):
        yv = y_ap[t0:t0 + 512, :].rearrange("(g p) n -> p g n", p=128)
        for g in range(4):
            for hh in range(2):
                pt, pk = psn()
                for k4 in range(4):
                    kc = hh * 4 + k4
                    P.op("pe", lambda e, g=g, kc=kc, k4=k4, pt=pt: e.transpose(out=pt[:, k4 * 128:(k4 + 1) * 128], in_=XT[:, kc, g * 128:(g + 1) * 128], identity=ID32[:]),
                         reads=[("XT", kc), "ID32"], writes=[pk])
                if hh == 0:
                    P.op("act", lambda e, g=g, pt=pt: e.activation(out=XTOK[:, g, 0:512], in_=pt[:], func=AF.Copy), reads=[pk], writes=bigkeys(4 * g, 4 * g + 2))
                else:
                    P.op("dve", lambda e, g=g, pt=pt: e.tensor_copy(out=XTOK[:, g, 512:1024], in_=pt[:]), reads=[pk], writes=bigkeys(4 * g + 2, 4 * g + 4))
        dma(yv[:, 0:2, :], XTOK[:, 0:2, :], reads=bigkeys(0, 8), writes=["Y"])
        dma(yv[:, 2:4, :], XTOK[:, 2:4, :], reads=bigkeys(8, 16), writes=["Y"])

    def front(ws, l, b, hook=None):
        t0 = b * 512
        rmsnorm(0, l, XN)
        ffn(ws)
        dma(HS.rearrange("kc p t -> p kc t")[:, :, t0:t0 + 512], XT, reads=[("XT", kc) for kc in range(8)], writes=[("HS", b)])
        rmsnorm(1, l, XN)
        if hook is not None:
            hook()
        dma(CS[:, 0, :], c_cos[:, t0:t0 + 512], writes=["CS"])
        dma(CS[:, 1, :], c_sin[:, t0:t0 + 512], writes=["CS"])
        xnk = [("XN", kc) for kc in range(8)]

        def proj(w, wk, ti, M=128):
            pp, ppk = psn()
            wv = w[:, 0:2048].rearrange("p (t kc c) -> p t kc c", t=2, kc=8)
            for kc in range(8):
                P.op("pe", lambda e, kc=kc: e.matmul(pp[0:M, :], lhsT=wv[:, ti, kc, 0:M], rhs=XN[:, kc, :], start=(kc == 0), stop=(kc == 7)),
                     reads=[wk, ("XN", kc)], writes=[ppk])
            return pp, ppk

        def qk_post(pp, ppk, gc, dest, dkeys):
            P.op("act", lambda e: e.activation(out=U32, in_=pp[:], func=AF.Copy), reads=[ppk], writes=["U32"])
            P.op("act", lambda e: e.activation(out=SQ2, in_=pp[:], func=AF.Square), reads=[ppk], writes=["SQ2"])
            p2, p2k = psn()
            P.op("pe", lambda e: e.matmul(p2[:], lhsT=BLK16[:], rhs=SQ2, start=True, stop=True), reads=["SQ2", "BLK16"], writes=[p2k])
            P.op("act", lambda e: e.activation(out=TMPN, in_=p2[:], func=AF.Ln, scale=1.0 / 64, bias=EPSC[:]), reads=[p2k], writes=["TMPN"])
            P.op("act", lambda e: e.activation(out=RS, in_=TMPN, func=AF.Exp, scale=-0.5), reads=["TMPN"], writes=["RS"])
            P.op("dve", lambda e: e.scalar_tensor_tensor(out=QN32, in0=U32, scalar=SV[:, gc:gc + 1], in1=RS, op0=ALU.mult, op1=ALU.mult),
                 reads=["U32", "RS", "SV"], writes=["QN32"])
            p3, p3k = psn()
            P.op("pe", lambda e: e.matmul(p3[:], lhsT=ROT32[:], rhs=QN32, start=True, stop=True), reads=["QN32", "ROT32"], writes=[p3k])
            P.op("pool", lambda e: e.tensor_tensor(out=T132, in0=QN32, in1=CS[:, 0, :], op=ALU.mult), reads=["QN32", "CS"], writes=["T132"])
            P.op("dve", lambda e: e.tensor_tensor(out=T232, in0=p3[:], in1=CS[:, 1, :], op=ALU.mult), reads=[p3k, "CS"], writes=["T232"])
            P.op("dve", lambda e: e.tensor_tensor(out=dest, in0=T132, in1=T232, op=ALU.add), reads=["T132", "T232"], writes=dkeys)

        def qk_s1(pp, ppk):
            P.op("act", lambda e: e.activation(out=U32, in_=pp[:], func=AF.Copy), reads=[ppk], writes=["U32"])
            P.op("act", lambda e: e.activation(out=SQ2, in_=pp[:], func=AF.Square), reads=[ppk], writes=["SQ2"])

        def qk_p2():
            p2, p2k = psn()
            P.op("pe", lambda e: e.matmul(p2[:], lhsT=BLK16[:], rhs=SQ2, start=True, stop=True), reads=["SQ2", "BLK16"], writes=[p2k])
            return p2, p2k

        def qk_s2(p2, p2k, gc):
            P.op("act", lambda e: e.activation(out=TMPN, in_=p2[:], func=AF.Ln, scale=1.0 / 64, bias=EPSC[:]), reads=[p2k], writes=["TMPN"])
            P.op("act", lambda e: e.activation(out=RS, in_=TMPN, func=AF.Exp, scale=-0.5), reads=["TMPN"], writes=["RS"])
            P.op("dve", lambda e: e.scalar_tensor_tensor(out=QN32, in0=U32, scalar=SV[:, gc:gc + 1], in1=RS, op0=ALU.mult, op1=ALU.mult),
                 reads=["U32", "RS", "SV"], writes=["QN32"])

        def qk_p3():
            p3, p3k = psn()
            P.op("pe", lambda e: e.matmul(p3[:], lhsT=ROT32[:], rhs=QN32, start=True, stop=True), reads=["QN32", "ROT32"], writes=[p3k])
            return p3, p3k

        def qk_s3(p3, p3k, dest, dkeys):
            P.op("pool", lambda e: e.tensor_tensor(out=T132, in0=QN32, in1=CS[:, 0, :], op=ALU.mult), reads=["QN32", "CS"], writes=["T132"])
            P.op("dve", lambda e: e.tensor_tensor(out=T232, in0=p3[:], in1=CS[:, 1, :], op=ALU.mult), reads=[p3k, "CS"], writes=["T232"])
            P.op("dve", lambda e: e.tensor_tensor(out=dest, in0=T132, in1=T232, op=ALU.add), reads=["T132", "T232"], writes=dkeys)

        def qa_dest(c):
            r = c % 2
            return QF[:, r, :], [("QF", r)]

        def qa_store(c):
            r = c % 2
            dma(QS[c][:, t0:t0 + 512], QF[:, r, :], reads=[("QF", r)], writes=[("QS", b)])

        def rl(w_, wk_, ti, h):
            pp, ppk = proj(w_, wk_, ti)
            r = h % 2
            P.op("act", lambda e, r=r, pp=pp: e.activation(out=SR32[:, r, :], in_=pp[:], func=AF.Silu), reads=[ppk], writes=[("SR", r)])
            dma(GR[h][:, t0:t0 + 512], SR32[:, r, :], reads=[("SR", r)], writes=[("GR", b)])

        def va(w_, wk_):
            wva = w_[:, 0:1024].rearrange("p (kc n) -> p kc n", kc=8)
            pv, pvk = psn()
            for g in range(4):
                for kc in range(8):
                    P.op("pe", lambda e, g=g, kc=kc: e.matmul(pv[:, g * 128:(g + 1) * 128], lhsT=XN[:, kc, g * 128:(g + 1) * 128], rhs=wva[:, kc, :],
                                                              start=(kc == 0), stop=(kc == 7)),
                         reads=[wk_, ("XN", kc)], writes=[pvk])
            P.op("act", lambda e: e.activation(out=VO[:, 4 * b:4 * b + 4, :, 0:64], in_=pv[:].rearrange("p (g k n) -> p g k n", g=4, k=2), func=AF.Copy),
                 reads=[pvk], writes=[("VO", b)])

        def vl(g, wl0, wl0k, wl1, wl1k):
            pl, plk = psn()
            for kc in range(8):
                wsrc = wl0 if kc < 4 else wl1
                wkk = wl0k if kc < 4 else wl1k
                wvv = wsrc[:, 0:2048].rearrange("p (kc n) -> p kc n", kc=4)
                P.op("pe", lambda e, kc=kc, wvv=wvv: e.matmul(pl[:], lhsT=XN[:, kc, g * 128:(g + 1) * 128], rhs=wvv[:, kc % 4, :],
                                                               start=(kc == 0), stop=(kc == 7)),
                     reads=[wkk, ("XN", kc)], writes=[plk])
            if g % 2 == 0:
                P.op("act", lambda e: e.activation(out=VLT[:, g, :], in_=pl[:], func=AF.Copy), reads=[plk], writes=[("VLT", g)])
            else:
                P.op("dve", lambda e: e.tensor_copy(out=VLT[:, g, :], in_=pl[:]), reads=[plk], writes=[("VLT", g)])
            if g == 3:
                dma(GV[4 * b:4 * b + 4].rearrange("c p n -> p c n"), VLT, reads=[("VLT", g_) for g_ in range(4)], writes=[("GV", b)])

        v4 = lambda t, lo, hi: t[lo:hi, :].rearrange("p (c n) -> p c n", c=4)
        wg = WG16[:].rearrange("p (l h n) -> p l h n", l=L, h=4)

        def gla_a(w_, wk_, h):
            pq, pqk = proj(w_, wk_, 0)
            pkk_, pkkk = proj(w_, wk_, 1)
            px, pxk = psn()
            P.op("pe", lambda e: e.matmul(px[:], lhsT=wg[0:32, l, h, :], rhs=GLOW[0:32, :], start=True, stop=True),
                 reads=["GLOW", "WG16"], writes=[pxk])
            bcol = OFF_BG + 4 * l + h
            P.op("act", lambda e: e.activation(out=TZ32, in_=px[:], func=AF.Exp, scale=-1.0, bias=SV[:, bcol:bcol + 1]),
                 reads=[pxk, "SV"], writes=["TZ32"])
            P.op("act", lambda e: e.activation(out=L32, in_=TZ32, func=AF.Ln, bias=1.0), reads=["TZ32"], writes=["L32"])
            P.op("dve", lambda e: e.tensor_tensor_scan(out=PP32, data0=RESET[:], data1=L32, initial=0.0, op0=ALU.mult, op1=ALU.add),
                 reads=["L32", "RESET"], writes=["PP32"])
            P.op("pool", lambda e: e.tensor_copy(out=Z32[0:64, :], in_=PP32[0:64, :]), reads=["PP32"], writes=["Z32"])
            P.op("pool", lambda e: e.tensor_tensor(out=TZ32[64:128, :], in0=L32[64:128, :], in1=PP32[64:128, :], op=ALU.subtract),
                 reads=["L32", "PP32"], writes=["TZ32"])
            P.op("dve", lambda e: e.tensor_tensor(out=v4(Z32, 64, 128), in0=v4(TZ32, 64, 128),
                                                  in1=v4(PP32, 64, 128)[:, :, 127:128].to_broadcast([64, 4, 128]), op=ALU.add),
                 reads=["TZ32", "PP32"], writes=["Z32"])
            P.op("act", lambda e: e.activation(out=EQ32, in_=Z32, func=AF.Exp, scale=-1.0 / 16), reads=["Z32"], writes=["EQ32"])
            P.op("act", lambda e: e.activation(out=EK32, in_=Z32, func=AF.Exp, scale=1.0 / 16), reads=["Z32"], writes=["EK32"])
            P.op("pool", lambda e: e.tensor_copy(out=DEC[0:64, h, 4 * b:4 * b + 4, :], in_=v4(EQ32, 0, 64)[:, :, 127:128]),
                 reads=["EQ32"], writes=[("DEC", h, b)])
            P.op("pool", lambda e: e.tensor_copy(out=DEC[64:128, h, 4 * b:4 * b + 4, :], in_=v4(EQ32, 64, 128)[:, :, 0:1]),
                 reads=["EQ32"], writes=[("DEC", h, b)])
            r = h % 2
            P.op("dve", lambda e: e.scalar_tensor_tensor(out=QD16[:, r, :], in0=pq[:], scalar=0.125, in1=EQ32, op0=ALU.mult, op1=ALU.mult),
                 reads=[pqk, "EQ32"], writes=[("QD", r)])
            P.op("dve", lambda e: e.tensor_tensor(out=KD16[:, r, :], in0=pkk_[:], in1=EK32, op=ALU.mult),
                 reads=[pkkk, "EK32"], writes=[("KD", r)])
            P.op("dve", lambda e: e.tensor_tensor(out=KDS16[:, r, :].rearrange("p (c n) -> p c n", c=4),
                                                  in0=KD16[:, r, :].rearrange("p (c n) -> p c n", c=4),
                                                  in1=DEC[:, h, 4 * b:4 * b + 4, :].to_broadcast([128, 4, 128]), op=ALU.mult),
                 reads=[("KD", r), ("DEC", h, b)], writes=[("KDS", r)])
            dma(GQ[h][:, t0:t0 + 512], QD16[:, r, :], reads=[("QD", r)], writes=[("GQ", b)])
            dma(GK[h][:, t0:t0 + 512], KD16[:, r, :], reads=[("KD", r)], writes=[("GK", b)])

        def gla_b(h):
            r = h % 2
            ptt, ptk = psn()
            ptb = ptt[:].bitcast(BF16)
            for ch in range(4):
                P.op("pe", lambda e, ch=ch: e.transpose(out=ptb[:, ch * 128:(ch + 1) * 128], in_=KDS16[:, r, ch * 128:(ch + 1) * 128], identity=ID16[:]),
                     reads=[("KDS", r), "ID16"], writes=[ptk])
            P.op("act", lambda e: e.activation(out=KDT16[:, r, :], in_=ptb[:, 0:512], func=AF.Copy), reads=[ptk], writes=[("KDT", r)])
            dma(GKT[h][4 * b:4 * b + 4].rearrange("c p n -> p c n"), KDT16[:, r, :].rearrange("p (c n) -> p c n", c=4),
                reads=[("KDT", r)], writes=[("GKT", b)])

        gq = OFF_QK + 0 * L + l
        gk = OFF_QK + 1 * L + l
        w0, w0k = ws.get()
        pp, ppk = proj(w0, w0k, 0, M=32)
        P.op("act", lambda e, pp=pp: e.activation(out=GLOW[0:32, :], in_=pp[0:32, :], func=AF.Copy), reads=[ppk], writes=["GLOW"])
        ppa, ppak = proj(w0, w0k, 1)
        qk_s1(ppa, ppak)
        w7, w7k = ws.get()
        rl(w7, w7k, 0, 0)
        p2, p2k = qk_p2()
        qk_s2(p2, p2k, gq)
        w1, w1k = ws.get()
        ppb, ppbk = proj(w1, w1k, 0)
        rl(w7, w7k, 1, 1)
        p3, p3k = qk_p3()
        qk_s3(p3, p3k, *qa_dest(0))
        qa_store(0)
        qk_s1(ppb, ppbk)
        ppc, ppck = proj(w1, w1k, 1)
        p2, p2k = qk_p2()
        qk_s2(p2, p2k, gq)
        w8, w8k = ws.get()
        rl(w8, w8k, 0, 2)
        p3, p3k = qk_p3()
        qk_s3(p3, p3k, *qa_dest(1))
        qa_store(1)
        qk_s1(ppc, ppck)
        w2, w2k = ws.get()
        ppd, ppdk = proj(w2, w2k, 0)
        p2, p2k = qk_p2()
        qk_s2(p2, p2k, gq)
        rl(w8, w8k, 1, 3)
        p3, p3k = qk_p3()
        qk_s3(p3, p3k, *qa_dest(2))
        qa_store(2)
        qk_s1(ppd, ppdk)
        ppe, ppek = proj(w2, w2k, 1)
        p2, p2k = qk_p2()
        qk_s2(p2, p2k, gq)
        wv_, wvk_ = ws.get()
        va(wv_, wvk_)
        p3, p3k = qk_p3()
        qk_s3(p3, p3k, *qa_dest(3))
        qa_store(3)
        qk_s1(ppe, ppek)
        w3, w3k = ws.get()
        gla_a(w3, w3k, 0)
        p2, p2k = qk_p2()
        qk_s2(p2, p2k, gk)
        wl0, wl0k = ws.get()
        wl1, wl1k = ws.get()
        vl(0, wl0, wl0k, wl1, wl1k)
        p3, p3k = qk_p3()
        qk_s3(p3, p3k, KT[:, t0:t0 + 512], [("KT", b)])
        vl(1, wl0, wl0k, wl1, wl1k)
        w4, w4k = ws.get()
        gla_a(w4, w4k, 1)
        gla_b(0)
        vl(2, wl0, wl0k, wl1, wl1k)
        vl(3, wl0, wl0k, wl1, wl1k)
        w5, w5k = ws.get()
        gla_a(w5, w5k, 2)
        gla_b(1)
        w6, w6k = ws.get()
        gla_a(w6, w6k, 3)
        gla_b(2)
        carry.append(lambda: gla_b(3))

    def back_loads(b, oc=True, xt=True):
        t0 = b * 512
        if oc:
            dma(OC[:, 0:4, :], OAS.rearrange("a p t -> p a t")[:, :, t0:t0 + 512], writes=[("OC", 0)])
            dma(OC[:, 4:8, :], OG.rearrange("a p t -> p a t")[:, :, t0:t0 + 512], writes=[("OC", 1)])
        if xt:
            dma(XT, HS.rearrange("kc p t -> p kc t")[:, :, t0:t0 + 512], reads=[("HS", b)], writes=[("XT", kc) for kc in range(8)])

    def back(ws, l, b, hook=None):
        t0 = b * 512
        for m in range(8):
            w, wk = ws.get()
            wv = w[:, 0:1024].rearrange("p (kc c) -> p kc c", kc=8)
            pd, pdk = psn()
            for kc in range(8):
                P.op("pe", lambda e, kc=kc, wv=wv, pd=pd: e.matmul(pd[:], lhsT=wv[:, kc, :], rhs=OC[:, kc, :], start=(kc == 0), stop=(kc == 7)),
                     reads=[wk, ("OC", kc // 4)], writes=[pdk])
            P.op("dve", lambda e, m=m, pd=pd: e.tensor_tensor(out=XT[:, m, :], in0=pd[:], in1=XT[:, m, :], op=ALU.add),
                 reads=[pdk, ("XT", m)], writes=[("XT", m)])
        run_carry()
        if hook is not None:
            hook()
        rmsnorm(2, l, XN)
        ffn(ws)
        rmsnorm(3, l, None)

    def attention(S):
        NB, NCH = S // 512, S // 128
        o2 = O_MAIN
        QTt = A.view(o2, BF16, [128, 2, 512]); o2 += 2 * KiB
        PT = A.view(o2, BF16, [128, 8, 512]); o2 += 8 * KiB
        ACCS = A.view(o2, F32, [128, 2, 512]); o2 += 4 * KiB
        RCP = A.view(o2, F32, [128, 2, 512]); o2 += 4 * KiB
        OAb = A.view(o2, BF16, [128, 2, 512]); o2 += 2 * KiB
        iters = [(qb, c) for qb in range(NB) for c in range(4)]
        steps = [(k, sc) for k in range(len(iters)) for sc in range(NCH)]
        pt_slot = {}
        pstate = {"pti": 0}

        def acc_of(k):
            r = k % 2
            return [(ps[2 * r], ("ps", 2 * r)), (ps[2 * r + 1], ("ps", 2 * r + 1))]

        def st_of(i):
            sp_ = 4 + 2 * (i % 2)
            return [(ps[sp_], ("ps", sp_)), (ps[sp_ + 1], ("ps", sp_ + 1))]

        def emit_mm1(i):
            k, sc = steps[i]
            qb, c = iters[k]
            r = k % 2
            if sc == 0:
                dma(QTt[:, r, :], QS[c][:, qb * 512:qb * 512 + 512], writes=[("QTt", r)])
            st = st_of(i)
            for hf in range(2):
                lo, hi = 64 * hf, 64 * hf + 64
                P.op("pe", lambda e, hf=hf, lo=lo, hi=hi, sc=sc, r=r, st=st: e.matmul(st[hf][0][:], lhsT=KT[lo:hi, sc * 128:(sc + 1) * 128],
                                                                                    rhs=QTt[lo:hi, r, :], start=True, stop=True),
                     reads=["KTall", ("QTt", r)], writes=[st[hf][1]])

        def emit_exp(i):
            st = st_of(i)
            sl = []
            for hf in range(2):
                s_ = pstate["pti"] % 8
                pstate["pti"] += 1
                sl.append(s_)
                P.op("act", lambda e, hf=hf, s_=s_, st=st: e.activation(out=PT[:, s_, :], in_=st[hf][0][:], func=AF.Exp, scale=0.125),
                     reads=[st[hf][1]], writes=[("PT", s_)])
            pt_slot[i] = sl

        def emit_mm2(i):
            k, sc = steps[i]
            acc = acc_of(k)
            for hf in range(2):
                s_ = pt_slot[i][hf]
                P.op("pe", lambda e, hf=hf, s_=s_, sc=sc, acc=acc: e.matmul(acc[hf][0][:], lhsT=VO[:, sc, hf, :], rhs=PT[:, s_, :],
                                                                           start=(sc == 0), stop=(sc == NCH - 1)),
                     reads=["VOall", ("PT", s_)], writes=[acc[hf][1]])

        def fin_a(k):
            acc = acc_of(k)
            for hf in range(2):
                ap_, ak = acc[hf]
                P.op("dve", lambda e, hf=hf, ap_=ap_: e.tensor_copy(out=ACCS[:, hf, :], in_=ap_[:]), reads=[ak], writes=[("ACCS", hf)])
                P.op("dve", lambda e, hf=hf: e.reciprocal(out=RCP[64:128, hf, :], in_=ACCS[64:128, hf, :]), reads=[("ACCS", hf)], writes=[("RCP", hf)])

        def fin_b(k):
            qb, c = iters[k]
            acc = acc_of(k)
            for hf in range(2):
                head = c + 4 * hf
                ap_, ak = acc[hf]
                P.op("pe", lambda e, hf=hf, ap_=ap_: e.matmul(ap_[0:64, :], lhsT=ID32[64:128, 64:128], rhs=RCP[64:128, hf, :], start=True, stop=True),
                     reads=[("RCP", hf), "ID32"], writes=[ak])
                P.op("dve", lambda e, hf=hf, ap_=ap_: e.tensor_tensor(out=OAb[0:64, hf, :], in0=ACCS[0:64, hf, :], in1=ap_[0:64, :], op=ALU.mult),
                     reads=[("ACCS", hf), ak], writes=[("OAb", hf)])
                po_ = (head % 2) * 64
                dma(OAS[head // 2][po_:po_ + 64, qb * 512:qb * 512 + 512], OAb[0:64, hf, :], reads=[("OAb", hf)], writes=["OAS"])

        nst = len(steps)
        emit_mm1(0)
        pending_fin = None
        for i in range(nst):
            k, sc = steps[i]
            if i + 1 < nst:
                emit_mm1(i + 1)
            emit_exp(i)
            emit_mm2(i)
            if pending_fin is not None and sc == min(8, NCH - 1):
                fin_b(pending_fin)
                pending_fin = None
            if sc == NCH - 1:
                fin_a(k)
                pending_fin = k
        if pending_fin is not None:
            fin_b(pending_fin)

    def gla(S, l):
        NB, NCH = S // 512, S // 128
        o2 = O_MAIN
        if 6 * NCH * 128 <= O_DEC:
            DSB = A.view(0, F32, [128, NCH, 128])
            STATE16 = A.view(4 * NCH * 128, BF16, [128, NCH, 128])
        else:
            DSB = A.view(o2, F32, [128, NCH, 128]); o2 += 4 * NCH * 128
            STATE16 = A.view(o2, BF16, [128, NCH, 128]); o2 += 2 * NCH * 128
        GQh = A.view(o2, BF16, [128, S]); o2 += 2 * S
        GKh = A.view(o2, BF16, [128, S]); o2 += 2 * S
        GKTh = A.view(o2, BF16, [128, NCH, 128]); o2 += 2 * S
        GVh = A.view(o2, BF16, [128, NCH, 128]); o2 += 2 * S
        SALL = A.view(o2, F32, [128, NCH, 128]); o2 += 4 * NCH * 128
        AT2 = A.view(o2, BF16, [128, 3, 2, 512]); o2 += 6 * KiB
        OSQ = A.view(o2, BF16, [128, 2, 512]); o2 += 2 * KiB
        SRb = A.view(o2, F32, [128, 3, 512]); o2 += 6 * KiB
        T1 = A.view(o2, F32, [128, 512]); o2 += 2 * KiB
        OGb = A.view(o2, BF16, [128, 2, 512]); o2 += 2 * KiB
        TN2 = A.view(o2, F32, [128, 512]); o2 += 2 * KiB
        RS2 = A.view(o2, F32, [128, 512]); o2 += 2 * KiB
        assert o2 <= ARENA_BYTES, o2
        for h in range(4):
            cst = min(16, NCH)
            for c0 in range(0, NCH, cst):
                dma(GKTh[:, c0:c0 + cst, :], GKT[h][c0:c0 + cst].rearrange("c p n -> p c n"), writes=[("GKTh", c0 // cst)])
                dma(GVh[:, c0:c0 + cst, :], GV[c0:c0 + cst, :, h * 128:(h + 1) * 128].rearrange("c p n -> p c n"), writes=[("GVh", c0 // cst)])
            step = min(2048, S)
            for c0 in range(0, S, step):
                dma(GQh[:, c0:c0 + step], GQ[h][:, c0:c0 + step], writes=[("GQh", c0 // step)])
                dma(GKh[:, c0:c0 + step], GK[h][:, c0:c0 + step], writes=[("GKh", c0 // step)])
            P.op("pool", lambda e: e.memset(SALL[0:64, 0, :], 0.0), writes=[("SA", 0)])
            P.op("pool", lambda e: e.memset(SALL[64:128, NCH - 1, :], 0.0), writes=[("SB", NCH - 1)])
            for cb in range(NB):
                pd, pdk = psn()
                for ci in range(4):
                    c = 4 * cb + ci
                    P.op("pe", lambda e, c=c, ci=ci, pd=pd: e.matmul(pd[:, ci * 128:(ci + 1) * 128], lhsT=GKTh[:, c, :], rhs=GVh[:, c, :], start=True, stop=True),
                         reads=[("GKTh", c // cst), ("GVh", c // cst)], writes=[pdk])
                P.op("act", lambda e, cb=cb, pd=pd: e.activation(out=DSB[64:128, 4 * cb:4 * cb + 4, :], in_=pd[64:128, :].rearrange("p (c n) -> p c n", c=4), func=AF.Copy),
                     reads=[pdk], writes=[("DSB", cb)])
                for ci in range(4):
                    c = 4 * cb + ci
                    if c + 1 < NCH:
                        P.op("dve", lambda e, c=c, ci=ci, pd=pd, h=h: e.scalar_tensor_tensor(out=SALL[0:64, c + 1, :], in0=SALL[0:64, c, :], scalar=DEC[0:64, h, c, :],
                                                                                          in1=pd[0:64, ci * 128:(ci + 1) * 128], op0=ALU.mult, op1=ALU.add),
                             reads=[("SA", c), pdk, "DECall"], writes=[("SA", c + 1)])
            for c in range(NCH - 1, 0, -1):
                P.op("pool", lambda e, c=c, h=h: e.tensor_scalar(out=SALL[64:128, c - 1, :], in0=SALL[64:128, c, :], scalar1=DEC[64:128, h, c, :],
                                                                 scalar2=None, op0=ALU.mult),
                     reads=[("SB", c), "DECall"], writes=[("SB", c - 1)])
                P.op("pool", lambda e, c=c: e.tensor_tensor(out=SALL[64:128, c - 1, :], in0=SALL[64:128, c - 1, :], in1=DSB[64:128, c, :], op=ALU.add),
                     reads=[("SB", c - 1), ("DSB", c // 4)], writes=[("SB", c - 1)])
            cst2 = min(16, NCH)
            for c0 in range(0, NCH, cst2):
                en_ = "act" if (c0 // cst2) % 2 == 0 else "pool"
                rk = [("SA", c) for c in range(c0, c0 + cst2)] + [("SB", c) for c in range(c0, c0 + cst2)]
                wk_ = [("ST16", c) for c in range(c0, c0 + cst2)]
                if en_ == "act":
                    P.op("act", lambda e, c0=c0: e.activation(out=STATE16[:, c0:c0 + cst2, :], in_=SALL[:, c0:c0 + cst2, :], func=AF.Copy), reads=rk, writes=wk_)
                else:
                    P.op("pool", lambda e, c0=c0: e.tensor_copy(out=STATE16[:, c0:c0 + cst2, :], in_=SALL[:, c0:c0 + cst2, :]), reads=rk, writes=wk_)
            m2f = M2[:, 0:128].rearrange("p (o n) -> p o n", o=1).to_broadcast([128, 4, 128])
            m2b = M2[:, 128:256].rearrange("p (o n) -> p o n", o=1).to_broadcast([128, 4, 128])
            v4_ = lambda t: t.rearrange("p (c n) -> p c n", c=4)
            gc = OFF_GLA + l
            pos = {}

            def st3_A(cb):
                paF, pafk = psn()
                paB, pabk = psn()
                r3 = cb % 3
                r2 = cb % 2
                dma(SRb[:, r3, :], GR[h][:, cb * 512:cb * 512 + 512], writes=[("SRb", r3)])
                for ci in range(4):
                    c = 4 * cb + ci
                    cs = slice(c * 128, (c + 1) * 128)
                    osl = slice(ci * 128, (ci + 1) * 128)
                    qk_ = [("GKh", (c * 128) // step), ("GQh", (c * 128) // step)]
                    P.op("pe", lambda e, cs=cs, osl=osl: e.matmul(paF[:, osl], lhsT=GKh[0:64, cs], rhs=GQh[0:64, cs], start=True, stop=True),
                         reads=qk_, writes=[pafk])
                    P.op("pe", lambda e, cs=cs, osl=osl: e.matmul(paB[:, osl], lhsT=GKh[64:128, cs], rhs=GQh[64:128, cs], start=True, stop=True),
                         reads=qk_, writes=[pabk])
                P.op("dve", lambda e: e.tensor_tensor(out=v4_(AT2[:, r3, 0, :]), in0=v4_(paF[:]), in1=m2f, op=ALU.mult),
                     reads=[pafk, "M2"], writes=[("AT2", r3, 0)])
                P.op("dve", lambda e: e.tensor_tensor(out=v4_(AT2[:, r3, 1, :]), in0=v4_(paB[:]), in1=m2b, op=ALU.mult),
                     reads=[pabk, "M2"], writes=[("AT2", r3, 1)])

            def st3_O(cb):
                po, pok = psn()
                pos[cb] = (po, pok)
                r3 = cb % 3
                r2 = cb % 2
                for ci in range(4):
                    c = 4 * cb + ci
                    cs = slice(c * 128, (c + 1) * 128)
                    osl = slice(ci * 128, (ci + 1) * 128)
                    P.op("pe", lambda e, c=c, osl=osl: e.matmul(po[:, osl], lhsT=GVh[:, c, :], rhs=AT2[:, r3, 0, osl], start=True, stop=False),
                         reads=[("GVh", c // cst), ("AT2", r3, 0)], writes=[pok])
                    P.op("pe", lambda e, c=c, osl=osl: e.matmul(po[:, osl], lhsT=GVh[:, c, :], rhs=AT2[:, r3, 1, osl], start=False, stop=False),
                         reads=[("GVh", c // cst), ("AT2", r3, 1)], writes=[pok])
                    P.op("pe", lambda e, c=c, cs=cs, osl=osl: e.matmul(po[:, osl], lhsT=STATE16[:, c, :], rhs=GQh[:, cs], start=False, stop=True),
                         reads=[("ST16", c), ("GQh", (c * 128) // step)], writes=[pok])
                P.op("act", lambda e: e.activation(out=OSQ[:, r2, :], in_=po[:], func=AF.Square), reads=[pok], writes=[("OSQ", r2)])

            def st3_N(cb):
                po, pok = pos.pop(cb)
                r2 = cb % 2
                pn, pnk = psn()
                P.op("pe", lambda e: e.matmul(pn[:], lhsT=ONES16[:], rhs=OSQ[:, r2, :], start=True, stop=True), reads=[("OSQ", r2), "ONES16"], writes=[pnk])
                P.op("act", lambda e: e.activation(out=TN2, in_=pn[:], func=AF.Ln, scale=1.0 / 128, bias=EPSC[:]), reads=[pnk], writes=["TN2"])
                P.op("act", lambda e: e.activation(out=RS2, in_=TN2, func=AF.Exp, scale=-0.5), reads=["TN2"], writes=["RS2"])
                P.op("dve", lambda e: e.scalar_tensor_tensor(out=T1, in0=po[:], scalar=SV[:, gc:gc + 1], in1=RS2, op0=ALU.mult, op1=ALU.mult),
                     reads=[pok, "RS2", "SV"], writes=["T1"])
                r3 = cb % 3
                P.op("pool", lambda e: e.tensor_tensor(out=OGb[:, r2, :], in0=T1, in1=SRb[:, r3, :], op=ALU.mult),
                     reads=["T1", ("SRb", r3)], writes=[("OGb", r2)])
                dma(OG[h][:, cb * 512:cb * 512 + 512], OGb[:, r2, :], reads=[("OGb", r2)], writes=["OG"])

            for stp in range(NB + 2):
                if stp < NB:
                    st3_A(stp)
                if 0 <= stp - 1 < NB:
                    st3_O(stp - 1)
                if 0 <= stp - 2 < NB:
                    st3_N(stp - 2)

    for name, S in seqs:
        NB = S // 512
        P.op("pool", lambda e: e.memset(VO[:, :, :, 64:128], 1.0), writes=["VOones"])
        ws = WS()
        for b in range(NB):
            plan_front(ws, 0)
        load_x_dma(xin[name], 0)
        for b in range(NB):
            load_x(xin[name], b * 512)
            run_carry()
            hk = (lambda b=b: load_x_dma(xin[name], (b + 1) * 512)) if b + 1 < NB else None
            front(ws, 0, b, hook=hk)
        run_carry()
        P.flush()
        if stop_after == "front":
            return nc
        for l in range(L):
            attention(S)
            P.flush()
            if stop_after == "att":
                return nc
            gla(S, l)
            P.flush()
            if stop_after == "gla":
                return nc
            ws = WS()
            for b in range(NB):
                plan_back(ws, l)
                if l + 1 < L:
                    plan_front(ws, l + 1)
            if l + 1 == L:
                pass
            else:
                P.op("pool", lambda e: e.memset(VO[:, :, :, 64:128], 1.0), writes=["VOones"])
            back_loads(0)
            for b in range(NB):
                if l + 1 < L:
                    back(ws, l, b)
                    hk = (lambda b=b: back_loads(b + 1)) if b + 1 < NB else None
                    front(ws, l + 1, b, hook=hk)
                else:
                    hk = (lambda b=b: back_loads(b + 1, oc=True, xt=False)) if b + 1 < NB else None
                    back(ws, l, b, hook=hk)
                    store_y(yout[name], b * 512)
                    if b + 1 < NB:
                        back_loads(b + 1, oc=False, xt=True)
            run_carry()
            P.flush()
    return nc


def make_consts(SMAX):
    ident = np.eye(128, dtype=np.float32)
    rot = np.zeros((128, 128), np.float32)
    for hb in (0, 64):
        for i in range(16):
            rot[hb + 16 + i, hb + i] = -1.0
            rot[hb + i, hb + 16 + i] = 1.0
            rot[hb + 48 + i, hb + 32 + i] = -1.0
            rot[hb + 32 + i, hb + 48 + i] = 1.0
    blk = np.zeros((128, 128), np.float32)
    blk[0:64, 0:64] = 1.0
    blk[64:128, 64:128] = 1.0
    j = np.arange(128)[:, None]
    i = np.arange(128)[None, :]
    m2 = np.concatenate([(j <= i), (j > i)], axis=1).astype(np.float32)
    reset = np.ones((128, 512), np.float32)
    reset[:, ::128] = 0.0
    t = np.arange(SMAX)
    row = (t // 64).astype(np.float32)
    col = (t % 64).astype(np.float32)
    inv_freq = (1.0 / (np.float32(10000.0) ** (np.arange(0, 32, 2, dtype=np.float32) / np.float32(32)))).astype(np.float32)
    ang_r = row[:, None] * inv_freq[None, :]
    ang_c = col[:, None] * inv_freq[None, :]
    ang = np.concatenate([ang_r, ang_r, ang_c, ang_c], axis=-1).astype(np.float32)
    cos = np.cos(ang).astype(np.float32).T
    sin = np.sin(ang).astype(np.float32).T
    cosT = np.ascontiguousarray(np.concatenate([cos, cos], axis=0))
    sinT = np.ascontiguousarray(np.concatenate([sin, sin], axis=0))
    return {"c_ident": ident, "c_rot": rot, "c_blk": blk, "c_m2": m2, "c_reset": reset, "c_cos": cosT, "c_sin": sinT}


WNAMES = ["norm_ffn1", "w_ffn1_gu", "w_ffn1_down", "norm_mix", "w_in", "q_norm", "k_norm", "w_gate_f", "b_gate_f",
          "w_gate_b", "b_gate_b", "gla_norm", "w_out", "norm_ffn2", "w_ffn2_gu", "w_ffn2_down", "norm_out"]


def kernel(**inputs):
    xp = np.asarray(inputs["x_prompt"], dtype=np.float32)
    xs = np.asarray(inputs["x_sample"], dtype=np.float32)
    nb = xp.shape[0]
    SP, SS = xp.shape[1], xs.shape[1]
    depth = np.asarray(inputs["w_in"]).shape[0]
    nc = build([("p", SP), ("s", SS)], depth=depth)
    consts = make_consts(max(SP, SS))
    wts = {k: np.ascontiguousarray(np.asarray(inputs[k], dtype=np.float32)) for k in WNAMES}
    in_maps = []
    for i in range(nb):
        m = {"x_p": np.ascontiguousarray(xp[i]), "x_s": np.ascontiguousarray(xs[i])}
        m.update(wts)
        m.update(consts)
        in_maps.append(m)
    res = run_bass_kernel_spmd(nc, in_maps, core_ids=list(range(nb)))
    yp = np.stack([np.asarray(r["y_p"], dtype=np.float32) for r in res.results], axis=0)
    ys = np.stack([np.asarray(r["y_s"], dtype=np.float32) for r in res.results], axis=0)
    return (yp, ys)
```

```python
import numpy as np
import concourse.bass as bass
import concourse.mybir as mybir
from concourse.bass_utils import run_bass_kernel_spmd

F32 = mybir.dt.float32
BF16 = mybir.dt.bfloat16
AF = mybir.ActivationFunctionType
ALU = mybir.AluOpType

D = 1024
KC = 8
DFF = 2816
JC = 22
DIN = 2336
EPS = 1e-6
TB = 512
NQ = 12


class Op:
    __slots__ = ("eng", "fn", "dma", "deps", "signal", "sem", "val", "prev_val")

    def __init__(self, eng, fn, dma):
        self.eng = eng
        self.fn = fn
        self.dma = dma
        self.deps = []
        self.signal = False
        self.sem = None
        self.val = 0
        self.prev_val = 0


class Prog:
    ENGS = ["pe", "act", "dve", "pool", "sp"]

    def __init__(self, nc):
        self.nc = nc
        self.sem = {e: nc.alloc_semaphore("s_" + e) for e in ["pe", "act", "dve", "pool"]}
        self.cnt = {e: 0 for e in self.sem}
        self.dsem = {q: [nc.alloc_semaphore("d_%s_%d" % (q, i)) for i in range(NQ)] for q in ["sp", "act"]}
        self.dcnt = {q: 0 for q in self.dsem}
        self.dlast = {q: [0] * NQ for q in self.dsem}
        self.waited = {}
        self.reset()
        allsems = list(self.sem.values()) + [s for q in self.dsem for s in self.dsem[q]]
        with nc.Block() as block:
            def clr(e):
                for s in allsems:
                    e.sem_clear(s)
            block.sync(clr)

    def reset(self):
        self.ops = []
        self.writers = {}
        self.readers = {}

    def op(self, eng, fn, reads=(), writes=(), dma=False):
        o = Op(eng, fn, dma)
        idx = len(self.ops)
        deps = {}
        for k in reads:
            for w in self.writers.get(k, ()):
                deps[w] = "RAW"
        for k in writes:
            rs = self.readers.get(k, ())
            for w in self.writers.get(k, ()):
                deps.setdefault(w, "WAW")
            for r in rs:
                deps.setdefault(r, "WAR")
        deps.pop(idx, None)
        o.deps = list(deps.items())
        def keep(lst):
            if dma:
                return lst
            return [i for i in lst if self.ops[i].dma or self.ops[i].eng != eng]
        for k in reads:
            self.readers[k] = keep(self.readers.get(k, [])) + [idx]
        for k in writes:
            if self.readers.get(k):
                self.writers[k] = [idx]
                self.readers[k] = []
            else:
                self.writers[k] = keep(self.writers.get(k, [])) + [idx]
        self.ops.append(o)
        return o

    def _needs_sem(self, p, o, kind):
        if p.dma:
            return True
        if p.eng != o.eng or o.dma:
            return True
        if p.eng == "pe":
            return False
        return True

    def flush(self, name=None):
        nc = self.nc
        ops = self.ops
        if not ops:
            return
        for o in ops:
            for d, kind in o.deps:
                p = ops[d]
                if not p.dma and self._needs_sem(p, o, kind):
                    p.signal = True
        for o in ops:
            if o.dma:
                q = o.eng
                k = self.dcnt[q]
                self.dcnt[q] += 1
                o.sem = (q, k % NQ)
                o.val = 16 * (k // NQ + 1)
                o.prev_val = 16 * (k // NQ)
                self.dlast[q][k % NQ] = o.val
            elif o.signal:
                self.cnt[o.eng] += 1
                o.sem = o.eng
                o.val = self.cnt[o.eng]

        def semobj(key):
            if isinstance(key, tuple):
                return self.dsem[key[0]][key[1]]
            return self.sem[key]

        bname = {"pe": "tensor", "act": "scalar", "dve": "vector", "pool": "gpsimd", "sp": "sync"}
        with nc.Block() as block:
            for ename in self.ENGS:
                elist = [o for o in ops if o.eng == ename]

                def body(e, elist=elist, ename=ename):
                    for o in elist:
                        waits = {}
                        for d, kind in o.deps:
                            p = ops[d]
                            if p.sem is None or not self._needs_sem(p, o, kind):
                                continue
                            if waits.get(p.sem, 0) < p.val:
                                waits[p.sem] = p.val
                        if o.dma and o.prev_val > 0:
                            if waits.get(o.sem, 0) < o.prev_val:
                                waits[o.sem] = o.prev_val
                        for sk, val in waits.items():
                            if self.waited.get((ename, sk), 0) < val:
                                e.wait_ge(semobj(sk), val)
                                self.waited[(ename, sk)] = val
                        ins = o.fn(e)
                        if o.sem is not None:
                            ins.then_inc(semobj(o.sem), 16 if o.dma else 1)
                    if ename in self.dsem:
                        for i in range(NQ):
                            v = self.dlast[ename][i]
                            if v > 0 and self.waited.get((ename, (ename, i)), 0) < v:
                                e.wait_ge(self.dsem[ename][i], v)
                                self.waited[(ename, (ename, i))] = v

                getattr(block, bname[ename])(body)
        self.reset()


def _prod(s):
    r = 1
    for v in s:
        r *= v
    return r


_RE = {2: None, 3: "p (a b) -> p a b", 4: "p (a b c) -> p a b c"}


class Arena:
    def __init__(self, nc, nbytes):
        self.n = nbytes
        self.t = nc.alloc_sbuf_tensor("arena", [128, nbytes // 2], BF16)

    def view(self, off, dtype, shape):
        n = _prod(shape[1:])
        esz = 4 if dtype == F32 else 2
        assert off % 4 == 0 and off + n * esz <= self.n, (off, n * esz, self.n)
        ap = self.t[0:shape[0], off // 2: off // 2 + n * esz // 2]
        if dtype != BF16:
            ap = ap.bitcast(dtype)
        if len(shape) == 3:
            ap = ap.rearrange("p (a b) -> p a b", a=shape[1])
        elif len(shape) == 4:
            ap = ap.rearrange("p (a b c) -> p a b c", a=shape[1], b=shape[2])
        return ap


def build(seqs, depth=2, dbg=False, stop_after=None):
    nc = bass.Bass("TRN2", target_bir_lowering=False)
    P = Prog(nc)
    L = depth
    SMAX = max(S for _, S in seqs)
    NCHMAX = SMAX // 128
    K = 1024

    def din(name, shape, dt=F32):
        return nc.dram_tensor(name, list(shape), dt, kind="ExternalInput").ap()

    def dscr(name, shape, dt):
        kind = "ExternalOutput" if dbg else "Internal"
        return nc.dram_tensor(name, list(shape), dt, kind=kind).ap()

    xin = {n: din("x_" + n, [S, D]) for n, S in seqs}
    yout = {n: nc.dram_tensor("y_" + n, [S, D], F32, kind="ExternalOutput").ap() for n, S in seqs}
    W = {}
    for nm, shp in [("norm_ffn1", [L, D]), ("w_ffn1_gu", [L, D, 2 * DFF]), ("w_ffn1_down", [L, DFF, D]),
                    ("norm_mix", [L, D]), ("w_in", [L, D, DIN]), ("q_norm", [L, 64]), ("k_norm", [L, 64]),
                    ("w_gate_f", [L, 16, 256]), ("b_gate_f", [L, 256]), ("w_gate_b", [L, 16, 256]),
                    ("b_gate_b", [L, 256]), ("gla_norm", [L, 128]), ("w_out", [L, D, D]),
                    ("norm_ffn2", [L, D]), ("w_ffn2_gu", [L, D, 2 * DFF]), ("w_ffn2_down", [L, DFF, D]),
                    ("norm_out", [L, D])]:
        W[nm] = din(nm, shp)
    c_ident = din("c_ident", [128, 128])
    c_rot = din("c_rot", [128, 128])
    c_blk = din("c_blk", [128, 128])
    c_m2 = din("c_m2", [128, 256])
    c_reset = din("c_reset", [128, 512])
    c_cos = din("c_cos", [128, SMAX])
    c_sin = din("c_sin", [128, SMAX])

    WGU = [[dscr("WGU%d%d" % (l, f), [JC, 128, 2 * 8 * 128], BF16) for f in range(2)] for l in range(L)]
    WDN = [[dscr("WDN%d%d" % (l, f), [8, 128, 22 * 128], BF16) for f in range(2)] for l in range(L)]
    WINL = [dscr("WINL%d" % l, [18, 128, 8 * 128], BF16) for l in range(L)]
    WINR = [dscr("WINR%d" % l, [128, 8 * 640], BF16) for l in range(L)]
    WOUT = [dscr("WOUT%d" % l, [8, 128, 8 * 128], BF16) for l in range(L)]
    HS = dscr("HS", [8, 128, SMAX], F32)
    QS = dscr("QS", [4, 128, SMAX], BF16)
    GQ = dscr("GQ", [4, 128, SMAX], BF16)
    GK = dscr("GK", [4, 128, SMAX], BF16)
    GKT = dscr("GKT", [4, NCHMAX, 128, 128], BF16)
    GV = dscr("GV", [NCHMAX, 128, 512], BF16)
    GR = dscr("GR", [4, 128, SMAX], F32)
    OAS = dscr("OAS", [4, 128, SMAX], BF16)
    OG = dscr("OG", [4, 128, SMAX], BF16)

    sb = nc.alloc_sbuf_tensor
    ID32 = sb("ID32", [128, 128], F32)
    ID16 = sb("ID16", [128, 128], BF16)
    ROT32 = sb("ROT32", [128, 128], F32)
    ONES16 = sb("ONES16", [128, 128], BF16)
    BLK16 = sb("BLK16", [128, 128], BF16)
    M2 = sb("M2", [128, 256], F32)
    RESET = sb("RESET", [128, 512], F32)
    SV = sb("SV", [128, 128], F32)
    WG16 = sb("WG16", [32, L * 4 * 128], BF16)
    EPSC = sb("EPSC", [128, 1], F32)
    ps = [nc.alloc_psum_tensor("ps%d" % i, [128, 512], F32) for i in range(8)]
    ARENA_BYTES = 192 * 1024
    A = Arena(nc, ARENA_BYTES)

    OFF_N, OFF_QK, OFF_GLA, OFF_BG = 0, 32 * L, 34 * L, 35 * L

    def gcol(which, l, kc):
        c = OFF_N + which * L * 8 + l * 8 + kc
        return SV[:, c:c + 1]

    state = {"ps": 0}

    def psn():
        i = state["ps"] % 8
        state["ps"] += 1
        return ps[i], ("ps", i)

    def dma(out, in_, reads=(), writes=(), q="sp"):
        return P.op(q, lambda e: e.dma_start(out=out, in_=in_), reads=reads, writes=writes, dma=True)

    KiB = 1024

    def prep():
        stg32 = [A.view(i * 66 * KiB, F32, [128, 11264]) for i in range(2)]
        stg16 = [A.view(i * 66 * KiB + 44 * KiB, BF16, [128, 11264]) for i in range(2)]
        small = A.view(132 * KiB, F32, [128, 128])
        wg32 = A.view(133 * KiB, F32, [32, L * 4 * 128])
        tmpc = A.view(140 * KiB, F32, [128, 128])
        dma(ID32[:], c_ident, writes=["ID32"])
        dma(ROT32[:], c_rot, writes=["ROT32"])
        dma(M2[:], c_m2, writes=["M2"])
        dma(RESET[:], c_reset, writes=["RESET"])
        dma(tmpc, c_blk, writes=["tmpc"])
        P.op("dve", lambda e: e.tensor_copy(out=BLK16[:], in_=tmpc), reads=["tmpc"], writes=["BLK16"])
        P.op("dve", lambda e: e.tensor_copy(out=ID16[:], in_=ID32[:]), reads=["ID32"], writes=["ID16"])
        P.op("pool", lambda e: e.memset(ONES16[:], 1.0), writes=["ONES16"])
        P.op("pool", lambda e: e.memset(EPSC[:], EPS), writes=["EPSC"])
        P.op("pool", lambda e: e.memset(small, 0.0), writes=["small"])
        for wi, nm in enumerate(["norm_ffn1", "norm_mix", "norm_ffn2", "norm_out"]):
            r0 = OFF_N + wi * L * 8
            dma(small[r0:r0 + L * 8, :], W[nm].rearrange("l (kc p) -> (l kc) p", p=128), reads=["small0"], writes=["small"])
        for wi, nm in enumerate(["q_norm", "k_norm"]):
            r0 = OFF_QK + wi * L
            dma(small[r0:r0 + L, 0:64], W[nm], reads=["small0"], writes=["small"])
            dma(small[r0:r0 + L, 64:128], W[nm], reads=["small0"], writes=["small"])
        dma(small[OFF_GLA:OFF_GLA + L, :], W["gla_norm"], reads=["small0"], writes=["small"])
        dma(small[OFF_BG:OFF_BG + 4 * L, 0:64], W["b_gate_f"].rearrange("l (h n) -> (l h) n", n=64), reads=["small0"], writes=["small"])
        dma(small[OFF_BG:OFF_BG + 4 * L, 64:128], W["b_gate_b"].rearrange("l (h n) -> (l h) n", n=64), reads=["small0"], writes=["small"])
        pt, pk = psn()
        P.op("pe", lambda e: e.transpose(out=pt[:, 0:128], in_=small, identity=ID32[:]), reads=["small", "ID32"], writes=[pk])
        P.op("dve", lambda e: e.tensor_copy(out=SV[:], in_=pt[:, 0:128]), reads=[pk], writes=["SV"])
        P.op("dve", lambda e: e.tensor_scalar(out=SV[:, OFF_BG:OFF_BG + 4 * L], in0=SV[:, OFF_BG:OFF_BG + 4 * L],
                                              scalar1=-1.0, scalar2=None, op0=ALU.mult), reads=["SV"], writes=["SV"])
        P.op("pool", lambda e: e.memset(wg32, 0.0), writes=["wg32"])
        wgv = wg32.rearrange("p (l h n) -> p l h n", l=L, h=4)
        for l in range(L):
            dma(wgv[0:16, l, :, 0:64], W["w_gate_f"][l].rearrange("r (h n) -> r h n", n=64), reads=["wg320"], writes=["wg32"])
            dma(wgv[16:32, l, :, 64:128], W["w_gate_b"][l].rearrange("r (h n) -> r h n", n=64), reads=["wg320"], writes=["wg32"])
        P.op("dve", lambda e: e.tensor_copy(out=WG16[:], in_=wg32), reads=["wg32"], writes=["WG16"])

        st = {"i": 0}
        engs = ["dve", "act", "pool"]

        def cast(out, in_, r, w):
            en = engs[st["i"] % 3]
            st["i"] += 1
            if en == "act":
                P.op("act", lambda e: e.activation(out=out, in_=in_, func=AF.Copy), reads=r, writes=w)
            else:
                P.op(en, lambda e: e.tensor_copy(out=out, in_=in_), reads=r, writes=w)

        pi = {"i": 0}

        def plain(src, nkc, c0, ncols, dst_fn):
            i = pi["i"] % 2
            pi["i"] += 1
            s32 = stg32[i][:, 0:nkc * ncols].rearrange("p (kc n) -> p kc n", kc=nkc)
            nm = ncols // 128
            half = (nkc + 1) // 2
            srcv = src[:, c0:c0 + ncols].rearrange("(kc p) n -> p kc n", p=128)
            dma(s32[:, 0:half, :], srcv[:, 0:half, :], writes=[("s32", i)])
            dma(s32[:, half:nkc, :], srcv[:, half:nkc, :], writes=[("s32", i)])
            s16 = stg16[i][:, 0:nkc * ncols].rearrange("p (m kc c) -> p m kc c", m=nm, kc=nkc)
            s32p = stg32[i][:, 0:nkc * ncols].rearrange("p (kc m c) -> p m kc c", kc=nkc, m=nm)
            step = max(1, nm // 3)
            for m0 in range(0, nm, step):
                m1 = min(nm, m0 + step)
                cast(s16[:, m0:m1], s32p[:, m0:m1], [("s32", i)], [("s16", i)])
            dst_fn(stg16[i][:, 0:nkc * ncols].rearrange("p (m n) -> p m n", m=nm), ("s16", i))

        for l in range(L):
            for f in range(2):
                src = W["w_ffn%d_gu" % (f + 1)][l]
                for t in range(2):
                    for jg in range(2):
                        def dst_fn(s16v, key, l=l, f=f, t=t, jg=jg):
                            dv = WGU[l][f][11 * jg:11 * jg + 11].rearrange("j p (t n) -> p j t n", t=2)[:, :, t, :]
                            dma(dv, s16v, reads=[key], writes=["WGU"])
                        plain(src, 8, t * DFF + jg * 1408, 1408, dst_fn)
                src = W["w_ffn%d_down" % (f + 1)][l]
                for mh in range(2):
                    def dst_fn(s16v, key, l=l, f=f, mh=mh):
                        dv = WDN[l][f][4 * mh:4 * mh + 4].rearrange("m p n -> p m n")
                        dma(dv, s16v, reads=[key], writes=["WDN"])
                    plain(src, 22, mh * 512, 512, dst_fn)
            def dst_fn(s16v, key, l=l):
                dma(WOUT[l].rearrange("m p n -> p m n"), s16v, reads=[key], writes=["WOUT"])
            plain(W["w_out"][l], 8, 0, 1024, dst_fn)
            src = W["w_in"][l]
            i = pi["i"] % 2
            pi["i"] += 1
            sA = stg32[i][:, 0:8 * 1280].rearrange("p (kc n) -> p kc n", kc=8)
            srcv = src[:, 0:1280].rearrange("(kc p) n -> p kc n", p=128)
            dma(sA[:, 0:4, :], srcv[:, 0:4, :], writes=[("s32", i)])
            dma(sA[:, 4:8, :], srcv[:, 4:8, :], writes=[("s32", i)])
            T16 = A.view(152 * KiB, BF16, [128, 18 * 1024]).rearrange("p (t kc c) -> p t kc c", t=18, kc=8)
            k32, k16 = [("s32", i)], ["T16"]
            cast(T16[:, 1:5, :, 0:64], sA[:, :, 0:256].rearrange("p kc (c n) -> p c kc n", n=64), k32, k16)
            cast(T16[:, 1:5, :, 64:128], sA[:, :, 256:512].rearrange("p kc (c n) -> p c kc n", n=64), k32, k16)
            cast(T16[:, 5, :, :], sA[:, :, 512:640], k32, k16)
            for h in range(4):
                for half in range(2):
                    cast(T16[:, 6 + 2 * h, :, 64 * half:64 * half + 64], sA[:, :, 768 + 64 * h:768 + 64 * h + 64], k32, k16)
                    cast(T16[:, 7 + 2 * h, :, 64 * half:64 * half + 64], sA[:, :, 1024 + 64 * h:1024 + 64 * h + 64], k32, k16)
            R16 = A.view(141 * KiB, BF16, [128, 8, 640])
            cast(R16[:, :, 0:128], sA[:, :, 640:768], k32, ["R16"])
            j = pi["i"] % 2
            pi["i"] += 1
            sB = stg32[j][:, 0:8 * 1056].rearrange("p (kc n) -> p kc n", kc=8)
            srcv = src[:, 1280:2336].rearrange("(kc p) n -> p kc n", p=128)
            dma(sB[:, 0:4, :], srcv[:, 0:4, :], writes=[("s32", j)])
            dma(sB[:, 4:8, :], srcv[:, 4:8, :], writes=[("s32", j)])
            kb = [("s32", j)]
            cast(R16[:, :, 128:640], sB[:, :, 0:512], kb, ["R16"])
            cast(T16[:, 14:18, :, :], sB[:, :, 512:1024].rearrange("p kc (h n) -> p h kc n", n=128), kb, k16)
            P.op("pool", lambda e, T16=T16: e.memset(T16[:, 0, :, 32:128], 0.0), writes=k16)
            cast(T16[:, 0, :, 0:32], sB[:, :, 1024:1056], kb, k16)
            dma(WINL[l].rearrange("t p n -> p t n"), A.view(152 * KiB, BF16, [128, 18, 1024]),
                reads=k16, writes=["WINL"])
            dma(WINR[l], R16.rearrange("p kc n -> p (kc n)"), reads=["R16"], writes=["WINR"])
        P.flush()

    prep()
    if stop_after == "prep":
        return nc

    o = 0
    def take(nbytes):
        nonlocal o
        r = o
        o += nbytes
        return r
    O_KT = take(2 * SMAX)
    O_VO = take(NCHMAX * 2 * 128 * 2)
    O_DEC = take(4 * NCHMAX * 4)
    O_MAIN = o
    KT = A.view(O_KT, BF16, [128, SMAX])
    VO = A.view(O_VO, BF16, [128, NCHMAX, 2, 128])
    DEC = A.view(O_DEC, F32, [128, 4, NCHMAX, 1])
    XT = A.view(take(16 * KiB), F32, [128, 8, 512])
    XN = A.view(take(8 * KiB), BF16, [128, 8, 512])
    O_BIG = take(22 * KiB)
    ACTT = A.view(O_BIG, BF16, [128, 22, 512])
    XTOK = A.view(O_BIG, F32, [128, 4, 1024])
    SQ = A.view(O_BIG + 16 * KiB, BF16, [128, 4, 512])
    RS = A.view(take(2 * KiB), F32, [128, 512])
    TMPN = A.view(take(2 * KiB), F32, [128, 512])
    SG = A.view(take(4 * KiB), F32, [128, 2, 512])
    NW = 8
    WR = [A.view(take(4 * KiB), BF16, [128, 2048]) for _ in range(NW)]
    OC = A.view(take(8 * KiB), BF16, [128, 8, 512])
    CS = A.view(take(4 * KiB), F32, [128, 2, 512])
    U32 = A.view(take(2 * KiB), F32, [128, 512])
    QN32 = A.view(take(2 * KiB), F32, [128, 512])
    T132 = A.view(take(2 * KiB), F32, [128, 512])
    T232 = A.view(take(2 * KiB), F32, [128, 512])
    QF = A.view(take(2 * KiB), BF16, [128, 2, 512])
    SQ2 = A.view(take(1 * KiB), BF16, [128, 512])
    GLOW = A.view(take(1 * KiB), BF16, [128, 512])
    L32 = A.view(take(2 * KiB), F32, [128, 512])
    PP32 = A.view(take(2 * KiB), F32, [128, 512])
    Z32 = A.view(take(2 * KiB), F32, [128, 512])
    TZ32 = A.view(take(2 * KiB), F32, [128, 512])
    EQ32 = A.view(take(2 * KiB), F32, [128, 512])
    EK32 = A.view(take(2 * KiB), F32, [128, 512])
    QD16 = A.view(take(2 * KiB), BF16, [128, 2, 512])
    KD16 = A.view(take(2 * KiB), BF16, [128, 2, 512])
    KDS16 = A.view(take(2 * KiB), BF16, [128, 2, 512])
    KDT16 = A.view(take(2 * KiB), BF16, [128, 2, 512])
    SR32 = A.view(take(4 * KiB), F32, [128, 2, 512])
    VLT = A.view(take(4 * KiB), BF16, [128, 4, 512])
    assert o <= ARENA_BYTES, o
    O_END_MAIN = o

    carry = []

    def run_carry():
        while carry:
            carry.pop(0)()

    def bigkeys(j0, j1):
        return [("big", j) for j in range(j0, j1)]

    class WS:
        def __init__(self):
            self.plan = []
            self.loaded = 0
            self.next = 0

        def add(self, ap2d, n):
            self.plan.append((ap2d, n))

        def _load(self, i):
            ap2d, n = self.plan[i]
            s = i % NW
            dma(WR[s][:, 0:n], ap2d, writes=[("wr", s)])

        def get(self):
            i = self.next
            self.next += 1
            while self.loaded < min(len(self.plan), i + 4):
                self._load(self.loaded)
                self.loaded += 1
            assert i < self.loaded
            s = i % NW
            return WR[s], ("wr", s)

    def plan_ffn(ws, l, f):
        for j in range(JC):
            ws.add(WGU[l][f][j], 2048)
        for m in range(8):
            ws.add(WDN[l][f][m][:, 0:1408], 1408)
            ws.add(WDN[l][f][m][:, 1408:2816], 1408)

    def plan_front(ws, l):
        plan_ffn(ws, l, 0)
        def pl_(pc):
            ws.add(WINL[l][2 * pc:2 * pc + 2].rearrange("t p n -> p t n"), 2048)
        wr_ = WINR[l].rearrange("p (kc n) -> p kc n", kc=8)
        pl_(0); pl_(7); pl_(1); pl_(8); pl_(2)
        ws.add(wr_[:, :, 0:128], 1024)
        pl_(3)
        ws.add(wr_[:, 0:4, 128:640], 2048)
        ws.add(wr_[:, 4:8, 128:640], 2048)
        pl_(4); pl_(5); pl_(6)

    def plan_back(ws, l):
        for m in range(8):
            ws.add(WOUT[l][m], 1024)
        plan_ffn(ws, l, 1)

    def _load(self, i):
        ap, n = self.plan[i]
        s = i % NW
        dst = WR[s][:, 0:n]
        if len(ap.shape) == 3:
            dst = dst.rearrange("p (a b) -> p a b", a=ap.shape[1])
        dma(dst, ap, writes=[("wr", s)])
    WS._load = _load

    def rmsnorm(which, l, dst16=None):
        pn, pk = psn()
        for kc in range(8):
            if kc % 2 == 0:
                P.op("act", lambda e, kc=kc: e.activation(out=XN[:, kc, :], in_=XT[:, kc, :], func=AF.Square),
                     reads=[("XT", kc)], writes=[("XN", kc)])
            else:
                P.op("dve", lambda e, kc=kc: e.tensor_tensor(out=XN[:, kc, :], in0=XT[:, kc, :], in1=XT[:, kc, :], op=ALU.mult),
                     reads=[("XT", kc)], writes=[("XN", kc)])
        for kc in range(8):
            P.op("pe", lambda e, kc=kc: e.matmul(pn[:], lhsT=ONES16[:], rhs=XN[:, kc, :], start=(kc == 0), stop=(kc == 7)),
                 reads=[("XN", kc), "ONES16"], writes=[pk])
        P.op("act", lambda e: e.activation(out=TMPN, in_=pn[:], func=AF.Ln, scale=1.0 / D, bias=EPSC[:]), reads=[pk], writes=["TMPN"])
        P.op("act", lambda e: e.activation(out=RS, in_=TMPN, func=AF.Exp, scale=-0.5), reads=["TMPN"], writes=["RS"])
        for kc in range(8):
            if dst16 is not None:
                P.op("dve", lambda e, kc=kc: e.scalar_tensor_tensor(out=XN[:, kc, :], in0=XT[:, kc, :], scalar=gcol(which, l, kc),
                                                                    in1=RS, op0=ALU.mult, op1=ALU.mult),
                     reads=[("XT", kc), "RS", "SV"], writes=[("XN", kc)])
            else:
                P.op("dve", lambda e, kc=kc: e.scalar_tensor_tensor(out=XT[:, kc, :], in0=XT[:, kc, :], scalar=gcol(which, l, kc),
                                                                    in1=RS, op0=ALU.mult, op1=ALU.mult),
                     reads=[("XT", kc), "RS", "SV"], writes=[("XT", kc)])

    def ffn(ws):
        for j in range(JC):
            w, wk = ws.get()
            wv = w[:, 0:2048].rearrange("p (t kc c) -> p t kc c", t=2, kc=8)
            pg, pgk = psn()
            pu, puk = psn()
            for kc in range(8):
                P.op("pe", lambda e, kc=kc, wv=wv, pg=pg: e.matmul(pg[:], lhsT=wv[:, 0, kc, :], rhs=XN[:, kc, :], start=(kc == 0), stop=(kc == 7)),
                     reads=[wk, ("XN", kc)], writes=[pgk])
            for kc in range(8):
                P.op("pe", lambda e, kc=kc, wv=wv, pu=pu: e.matmul(pu[:], lhsT=wv[:, 1, kc, :], rhs=XN[:, kc, :], start=(kc == 0), stop=(kc == 7)),
                     reads=[wk, ("XN", kc)], writes=[puk])
            r = j % 2
            P.op("act", lambda e, r=r, pg=pg: e.activation(out=SG[:, r, :], in_=pg[:], func=AF.Silu), reads=[pgk], writes=[("SG", r)])
            P.op("dve", lambda e, r=r, pu=pu, j=j: e.tensor_tensor(out=ACTT[:, j, :], in0=SG[:, r, :], in1=pu[:], op=ALU.mult),
                 reads=[("SG", r), puk], writes=[("big", j)])
        for m in range(8):
            wa, wak = ws.get()
            wb, wbk = ws.get()
            pd, pdk = psn()
            for kc in range(22):
                wsrc = wa if kc < 11 else wb
                wkk = wak if kc < 11 else wbk
                kk = kc if kc < 11 else kc - 11
                P.op("pe", lambda e, kc=kc, kk=kk, wsrc=wsrc, pd=pd: e.matmul(pd[:], lhsT=wsrc[:, kk * 128:(kk + 1) * 128], rhs=ACTT[:, kc, :],
                                                                             start=(kc == 0), stop=(kc == 21)),
                     reads=[wkk, ("big", kc)], writes=[pdk])
            P.op("dve", lambda e, m=m, pd=pd: e.scalar_tensor_tensor(out=XT[:, m, :], in0=pd[:], scalar=0.5, in1=XT[:, m, :],
                                                                     op0=ALU.mult, op1=ALU.add),
                 reads=[pdk, ("XT", m)], writes=[("XT", m)])

    def load_x_dma(x_ap, t0):
        xv = x_ap[t0:t0 + 512, :].rearrange("(g p) n -> p g n", p=128)
        dma(XTOK[:, 0:2, :], xv[:, 0:2, :], writes=bigkeys(0, 8))
        dma(XTOK[:, 2:4, :], xv[:, 2:4, :], writes=bigkeys(8, 16))

    def load_x(x_ap, t0):
        for kc in range(8):
            pt, pk = psn()
            for g in range(4):
                P.op("pe", lambda e, g=g, kc=kc, pt=pt: e.transpose(out=pt[:, g * 128:(g + 1) * 128], in_=XTOK[:, g, kc * 128:(kc + 1) * 128], identity=ID32[:]),
                     reads=bigkeys(4 * g, 4 * g + 4) + ["ID32"], writes=[pk])
            if kc % 2 == 0:
                P.op("act", lambda e, kc=kc, pt=pt: e.activation(out=XT[:, kc, :], in_=pt[:], func=AF.Copy), reads=[pk], writes=[("XT", kc)])
            else:
                P.op("dve", lambda e, kc=kc, pt=pt: e.tensor_copy(out=XT[:, kc, :], in_=pt[:]), reads=[pk], writes=[("XT", kc)])

    def store_y(y_ap, t0):
        yv = y_ap[t0:t0 + 512, :].rearrange("(g p) n -> p g n", p=128)
        for g in range(4):
            for hh in range(2):
                pt, pk = psn()
                for k4 in range(4):
                    kc = hh * 4 + k4
                    P.op("pe", lambda e, g=g, kc=kc, k4=k4, pt=pt: e.transpose(out=pt[:, k4 * 128:(k4 + 1) * 128], in_=XT[:, kc, g * 128:(g + 1) * 128], identity=ID32[:]),
                         reads=[("XT", kc), "ID32"], writes=[pk])
                if hh == 0:
                    P.op("act", lambda e, g=g, pt=pt: e.activation(out=XTOK[:, g, 0:512], in_=pt[:], func=AF.Copy), reads=[pk], writes=bigkeys(4 * g, 4 * g + 2))
                else:
                    P.op("dve", lambda e, g=g, pt=pt: e.tensor_copy(out=XTOK[:, g, 512:1024], in_=pt[:]), reads=[pk], writes=bigkeys(4 * g + 2, 4 * g + 4))
        dma(yv[:, 0:2, :], XTOK[:, 0:2, :], reads=bigkeys(0, 8), writes=["Y"])
        dma(yv[:, 2:4, :], XTOK[:, 2:4, :], reads=bigkeys(8, 16), writes=["Y"])

    def front(ws, l, b, hook=None):
        t0 = b * 512
        rmsnorm(0, l, XN)
        ffn(ws)
        dma(HS.rearrange("kc p t -> p kc t")[:, :, t0:t0 + 512], XT, reads=[("XT", kc) for kc in range(8)], writes=[("HS", b)])
        rmsnorm(1, l, XN)
        if hook is not None:
            hook()
        dma(CS[:, 0, :], c_cos[:, t0:t0 + 512], writes=["CS"])
        dma(CS[:, 1, :], c_sin[:, t0:t0 + 512], writes=["CS"])
        xnk = [("XN", kc) for kc in range(8)]

        def proj(w, wk, ti, M=128):
            pp, ppk = psn()
            wv = w[:, 0:2048].rearrange("p (t kc c) -> p t kc c", t=2, kc=8)
            for kc in range(8):
                P.op("pe", lambda e, kc=kc: e.matmul(pp[0:M, :], lhsT=wv[:, ti, kc, 0:M], rhs=XN[:, kc, :], start=(kc == 0), stop=(kc == 7)),
                     reads=[wk, ("XN", kc)], writes=[ppk])
            return pp, ppk

        def qk_post(pp, ppk, gc, dest, dkeys):
            P.op("act", lambda e: e.activation(out=U32, in_=pp[:], func=AF.Copy), reads=[ppk], writes=["U32"])
            P.op("act", lambda e: e.activation(out=SQ2, in_=pp[:], func=AF.Square), reads=[ppk], writes=["SQ2"])
            p2, p2k = psn()
            P.op("pe", lambda e: e.matmul(p2[:], lhsT=BLK16[:], rhs=SQ2, start=True, stop=True), reads=["SQ2", "BLK16"], writes=[p2k])
            P.op("act", lambda e: e.activation(out=TMPN, in_=p2[:], func=AF.Ln, scale=1.0 / 64, bias=EPSC[:]), reads=[p2k], writes=["TMPN"])
            P.op("act", lambda e: e.activation(out=RS, in_=TMPN, func=AF.Exp, scale=-0.5), reads=["TMPN"], writes=["RS"])
            P.op("dve", lambda e: e.scalar_tensor_tensor(out=QN32, in0=U32, scalar=SV[:, gc:gc + 1], in1=RS, op0=ALU.mult, op1=ALU.mult),
                 reads=["U32", "RS", "SV"], writes=["QN32"])
            p3, p3k = psn()
            P.op("pe", lambda e: e.matmul(p3[:], lhsT=ROT32[:], rhs=QN32, start=True, stop=True), reads=["QN32", "ROT32"], writes=[p3k])
            P.op("pool", lambda e: e.tensor_tensor(out=T132, in0=QN32, in1=CS[:, 0, :], op=ALU.mult), reads=["QN32", "CS"], writes=["T132"])
            P.op("dve", lambda e: e.tensor_tensor(out=T232, in0=p3[:], in1=CS[:, 1, :], op=ALU.mult), reads=[p3k, "CS"], writes=["T232"])
            P.op("dve", lambda e: e.tensor_tensor(out=dest, in0=T132, in1=T232, op=ALU.add), reads=["T132", "T232"], writes=dkeys)

        def qk_s1(pp, ppk):
            P.op("act", lambda e: e.activation(out=U32, in_=pp[:], func=AF.Copy), reads=[ppk], writes=["U32"])
            P.op("act", lambda e: e.activation(out=SQ2, in_=pp[:], func=AF.Square), reads=[ppk], writes=["SQ2"])

        def qk_p2():
            p2, p2k = psn()
            P.op("pe", lambda e: e.matmul(p2[:], lhsT=BLK16[:], rhs=SQ2, start=True, stop=True), reads=["SQ2", "BLK16"], writes=[p2k])
            return p2, p2k

        def qk_s2(p2, p2k, gc):
            P.op("act", lambda e: e.activation(out=TMPN, in_=p2[:], func=AF.Ln, scale=1.0 / 64, bias=EPSC[:]), reads=[p2k], writes=["TMPN"])
            P.op("act", lambda e: e.activation(out=RS, in_=TMPN, func=AF.Exp, scale=-0.5), reads=["TMPN"], writes=["RS"])
            P.op("dve", lambda e: e.scalar_tensor_tensor(out=QN32, in0=U32, scalar=SV[:, gc:gc + 1], in1=RS, op0=ALU.mult, op1=ALU.mult),
                 reads=["U32", "RS", "SV"], writes=["QN32"])

        def qk_p3():
            p3, p3k = psn()
            P.op("pe", lambda e: e.matmul(p3[:], lhsT=ROT32[:], rhs=QN32, start=True, stop=True), reads=["QN32", "ROT32"], writes=[p3k])
            return p3, p3k

        def qk_s3(p3, p3k, dest, dkeys):
            P.op("dve", lambda e: e.tensor_tensor(out=T132, in0=QN32, in1=CS[:, 0, :], op=ALU.mult), reads=["QN32", "CS"], writes=["T132"])
            P.op("dve", lambda e: e.tensor_tensor(out=T232, in0=p3[:], in1=CS[:, 1, :], op=ALU.mult), reads=[p3k, "CS"], writes=["T232"])
            P.op("dve", lambda e: e.tensor_tensor(out=dest, in0=T132, in1=T232, op=ALU.add), reads=["T132", "T232"], writes=dkeys)

        def qa_dest(c):
            r = c % 2
            return QF[:, r, :], [("QF", r)]

        def qa_store(c):
            r = c % 2
            dma(QS[c][:, t0:t0 + 512], QF[:, r, :], reads=[("QF", r)], writes=[("QS", b)])

        def rl(w_, wk_, ti, h):
            pp, ppk = proj(w_, wk_, ti)
            r = h % 2
            P.op("act", lambda e, r=r, pp=pp: e.activation(out=SR32[:, r, :], in_=pp[:], func=AF.Silu), reads=[ppk], writes=[("SR", r)])
            dma(GR[h][:, t0:t0 + 512], SR32[:, r, :], reads=[("SR", r)], writes=[("GR", b)])

        def va(w_, wk_):
            wva = w_[:, 0:1024].rearrange("p (kc n) -> p kc n", kc=8)
            pv, pvk = psn()
            for g in range(4):
                for kc in range(8):
                    P.op("pe", lambda e, g=g, kc=kc: e.matmul(pv[:, g * 128:(g + 1) * 128], lhsT=XN[:, kc, g * 128:(g + 1) * 128], rhs=wva[:, kc, :],
                                                              start=(kc == 0), stop=(kc == 7)),
                         reads=[wk_, ("XN", kc)], writes=[pvk])
            P.op("act", lambda e: e.activation(out=VO[:, 4 * b:4 * b + 4, :, 0:64], in_=pv[:].rearrange("p (g k n) -> p g k n", g=4, k=2), func=AF.Copy),
                 reads=[pvk], writes=[("VO", b)])

        def vl(g, wl0, wl0k, wl1, wl1k):
            pl, plk = psn()
            for kc in range(8):
                wsrc = wl0 if kc < 4 else wl1
                wkk = wl0k if kc < 4 else wl1k
                wvv = wsrc[:, 0:2048].rearrange("p (kc n) -> p kc n", kc=4)
                P.op("pe", lambda e, kc=kc, wvv=wvv: e.matmul(pl[:], lhsT=XN[:, kc, g * 128:(g + 1) * 128], rhs=wvv[:, kc % 4, :],
                                                               start=(kc == 0), stop=(kc == 7)),
                     reads=[wkk, ("XN", kc)], writes=[plk])
            if g % 2 == 0:
                P.op("act", lambda e: e.activation(out=VLT[:, g, :], in_=pl[:], func=AF.Copy), reads=[plk], writes=[("VLT", g)])
            else:
                P.op("dve", lambda e: e.tensor_copy(out=VLT[:, g, :], in_=pl[:]), reads=[plk], writes=[("VLT", g)])
            if g == 3:
                dma(GV[4 * b:4 * b + 4].rearrange("c p n -> p c n"), VLT, reads=[("VLT", g_) for g_ in range(4)], writes=[("GV", b)])

        v4 = lambda t, lo, hi: t[lo:hi, :].rearrange("p (c n) -> p c n", c=4)
        wg = WG16[:].rearrange("p (l h n) -> p l h n", l=L, h=4)

        def gla_a(w_, wk_, h):
            pq, pqk = proj(w_, wk_, 0)
            pkk_, pkkk = proj(w_, wk_, 1)
            px, pxk = psn()
            P.op("pe", lambda e: e.matmul(px[:], lhsT=wg[0:32, l, h, :], rhs=GLOW[0:32, :], start=True, stop=True),
                 reads=["GLOW", "WG16"], writes=[pxk])
            bcol = OFF_BG + 4 * l + h
            P.op("act", lambda e: e.activation(out=TZ32, in_=px[:], func=AF.Exp, scale=-1.0, bias=SV[:, bcol:bcol + 1]),
                 reads=[pxk, "SV"], writes=["TZ32"])
            P.op("act", lambda e: e.activation(out=L32, in_=TZ32, func=AF.Ln, bias=1.0), reads=["TZ32"], writes=["L32"])
            P.op("dve", lambda e: e.tensor_tensor_scan(out=PP32, data0=RESET[:], data1=L32, initial=0.0, op0=ALU.mult, op1=ALU.add),
                 reads=["L32", "RESET"], writes=["PP32"])
            P.op("act", lambda e: e.activation(out=Z32[0:64, :], in_=PP32[0:64, :], func=AF.Copy), reads=["PP32"], writes=["Z32"])
            P.op("dve", lambda e: e.tensor_tensor(out=TZ32[64:128, :], in0=L32[64:128, :], in1=PP32[64:128, :], op=ALU.subtract),
                 reads=["L32", "PP32"], writes=["TZ32"])
            P.op("dve", lambda e: e.tensor_tensor(out=v4(Z32, 64, 128), in0=v4(TZ32, 64, 128),
                                                  in1=v4(PP32, 64, 128)[:, :, 127:128].to_broadcast([64, 4, 128]), op=ALU.add),
                 reads=["TZ32", "PP32"], writes=["Z32"])
            P.op("act", lambda e: e.activation(out=EQ32, in_=Z32, func=AF.Exp, scale=-1.0 / 16), reads=["Z32"], writes=["EQ32"])
            P.op("act", lambda e: e.activation(out=EK32, in_=Z32, func=AF.Exp, scale=1.0 / 16), reads=["Z32"], writes=["EK32"])
            P.op("dve", lambda e: e.tensor_copy(out=DEC[0:64, h, 4 * b:4 * b + 4, :], in_=v4(EQ32, 0, 64)[:, :, 127:128]),
                 reads=["EQ32"], writes=[("DEC", h, b)])
            P.op("dve", lambda e: e.tensor_copy(out=DEC[64:128, h, 4 * b:4 * b + 4, :], in_=v4(EQ32, 64, 128)[:, :, 0:1]),
                 reads=["EQ32"], writes=[("DEC", h, b)])
            r = h % 2
            P.op("dve", lambda e: e.scalar_tensor_tensor(out=QD16[:, r, :], in0=pq[:], scalar=0.125, in1=EQ32, op0=ALU.mult, op1=ALU.mult),
                 reads=[pqk, "EQ32"], writes=[("QD", r)])
            P.op("dve", lambda e: e.tensor_tensor(out=KD16[:, r, :], in0=pkk_[:], in1=EK32, op=ALU.mult),
                 reads=[pkkk, "EK32"], writes=[("KD", r)])
            P.op("dve", lambda e: e.tensor_tensor(out=KDS16[:, r, :].rearrange("p (c n) -> p c n", c=4),
                                                  in0=KD16[:, r, :].rearrange("p (c n) -> p c n", c=4),
                                                  in1=DEC[:, h, 4 * b:4 * b + 4, :].to_broadcast([128, 4, 128]), op=ALU.mult),
                 reads=[("KD", r), ("DEC", h, b)], writes=[("KDS", r)])
            dma(GQ[h][:, t0:t0 + 512], QD16[:, r, :], reads=[("QD", r)], writes=[("GQ", b)])
            dma(GK[h][:, t0:t0 + 512], KD16[:, r, :], reads=[("KD", r)], writes=[("GK", b)])

        def gla_b(h):
            r = h % 2
            ptt, ptk = psn()
            ptb = ptt[:].bitcast(BF16)
            for ch in range(4):
                P.op("pe", lambda e, ch=ch: e.transpose(out=ptb[:, ch * 128:(ch + 1) * 128], in_=KDS16[:, r, ch * 128:(ch + 1) * 128], identity=ID16[:]),
                     reads=[("KDS", r), "ID16"], writes=[ptk])
            P.op("act", lambda e: e.activation(out=KDT16[:, r, :], in_=ptb[:, 0:512], func=AF.Copy), reads=[ptk], writes=[("KDT", r)])
            dma(GKT[h][4 * b:4 * b + 4].rearrange("c p n -> p c n"), KDT16[:, r, :].rearrange("p (c n) -> p c n", c=4),
                reads=[("KDT", r)], writes=[("GKT", b)])

        gq = OFF_QK + 0 * L + l
        gk = OFF_QK + 1 * L + l
        w0, w0k = ws.get()
        pp, ppk = proj(w0, w0k, 0, M=32)
        P.op("act", lambda e, pp=pp: e.activation(out=GLOW[0:32, :], in_=pp[0:32, :], func=AF.Copy), reads=[ppk], writes=["GLOW"])
        ppa, ppak = proj(w0, w0k, 1)
        qk_s1(ppa, ppak)
        w7, w7k = ws.get()
        rl(w7, w7k, 0, 0)
        p2, p2k = qk_p2()
        qk_s2(p2, p2k, gq)
        w1, w1k = ws.get()
        ppb, ppbk = proj(w1, w1k, 0)
        rl(w7, w7k, 1, 1)
        p3, p3k = qk_p3()
        qk_s3(p3, p3k, *qa_dest(0))
        qa_store(0)
        qk_s1(ppb, ppbk)
        ppc, ppck = proj(w1, w1k, 1)
        p2, p2k = qk_p2()
        qk_s2(p2, p2k, gq)
        w8, w8k = ws.get()
        rl(w8, w8k, 0, 2)
        p3, p3k = qk_p3()
        qk_s3(p3, p3k, *qa_dest(1))
        qa_store(1)
        qk_s1(ppc, ppck)
        w2, w2k = ws.get()
        ppd, ppdk = proj(w2, w2k, 0)
        p2, p2k = qk_p2()
        qk_s2(p2, p2k, gq)
        rl(w8, w8k, 1, 3)
        p3, p3k = qk_p3()
        qk_s3(p3, p3k, *qa_dest(2))
        qa_store(2)
        qk_s1(ppd, ppdk)
        ppe, ppek = proj(w2, w2k, 1)
        p2, p2k = qk_p2()
        qk_s2(p2, p2k, gq)
        wv_, wvk_ = ws.get()
        va(wv_, wvk_)
        p3, p3k = qk_p3()
        qk_s3(p3, p3k, *qa_dest(3))
        qa_store(3)
        qk_s1(ppe, ppek)
        w3, w3k = ws.get()
        gla_a(w3, w3k, 0)
        p2, p2k = qk_p2()
        qk_s2(p2, p2k, gk)
        wl0, wl0k = ws.get()
        wl1, wl1k = ws.get()
        vl(0, wl0, wl0k, wl1, wl1k)
        p3, p3k = qk_p3()
        qk_s3(p3, p3k, KT[:, t0:t0 + 512], [("KT", b)])
        vl(1, wl0, wl0k, wl1, wl1k)
        w4, w4k = ws.get()
        gla_a(w4, w4k, 1)
        gla_b(0)
        vl(2, wl0, wl0k, wl1, wl1k)
        vl(3, wl0, wl0k, wl1, wl1k)
        w5, w5k = ws.get()
        gla_a(w5, w5k, 2)
        gla_b(1)
        w6, w6k = ws.get()
        gla_a(w6, w6k, 3)
        gla_b(2)
        carry.append(lambda: gla_b(3))

    def back_loads(b, oc=True, xt=True):
        t0 = b * 512
        if oc:
            dma(OC[:, 0:4, :], OAS.rearrange("a p t -> p a t")[:, :, t0:t0 + 512], writes=[("OC", 0)])
            dma(OC[:, 4:8, :], OG.rearrange("a p t -> p a t")[:, :, t0:t0 + 512], writes=[("OC", 1)])
        if xt:
            dma(XT, HS.rearrange("kc p t -> p kc t")[:, :, t0:t0 + 512], reads=[("HS", b)], writes=[("XT", kc) for kc in range(8)])

    def back(ws, l, b, hook=None):
        t0 = b * 512
        for m in range(8):
            w, wk = ws.get()
            wv = w[:, 0:1024].rearrange("p (kc c) -> p kc c", kc=8)
            pd, pdk = psn()
            for kc in range(8):
                P.op("pe", lambda e, kc=kc, wv=wv, pd=pd: e.matmul(pd[:], lhsT=wv[:, kc, :], rhs=OC[:, kc, :], start=(kc == 0), stop=(kc == 7)),
                     reads=[wk, ("OC", kc // 4)], writes=[pdk])
            P.op("dve", lambda e, m=m, pd=pd: e.tensor_tensor(out=XT[:, m, :], in0=pd[:], in1=XT[:, m, :], op=ALU.add),
                 reads=[pdk, ("XT", m)], writes=[("XT", m)])
        run_carry()
        if hook is not None:
            hook()
        rmsnorm(2, l, XN)
        ffn(ws)
        rmsnorm(3, l, None)

    def attention(S):
        NB, NCH = S // 512, S // 128
        o2 = O_MAIN
        QTt = A.view(o2, BF16, [128, 2, 512]); o2 += 2 * KiB
        PT = A.view(o2, BF16, [128, 8, 512]); o2 += 8 * KiB
        ACCS = A.view(o2, F32, [128, 2, 512]); o2 += 4 * KiB
        RCP = A.view(o2, F32, [128, 2, 512]); o2 += 4 * KiB
        OAb = A.view(o2, BF16, [128, 2, 512]); o2 += 2 * KiB
        iters = [(qb, c) for qb in range(NB) for c in range(4)]
        steps = [(k, sc) for k in range(len(iters)) for sc in range(NCH)]
        pt_slot = {}
        pstate = {"pti": 0}

        def acc_of(k):
            r = k % 2
            return [(ps[2 * r], ("ps", 2 * r)), (ps[2 * r + 1], ("ps", 2 * r + 1))]

        def st_of(i):
            sp_ = 4 + 2 * (i % 2)
            return [(ps[sp_], ("ps", sp_)), (ps[sp_ + 1], ("ps", sp_ + 1))]

        def emit_mm1(i):
            k, sc = steps[i]
            qb, c = iters[k]
            r = k % 2
            if sc == 0:
                dma(QTt[:, r, :], QS[c][:, qb * 512:qb * 512 + 512], writes=[("QTt", r)])
            st = st_of(i)
            for hf in range(2):
                lo, hi = 64 * hf, 64 * hf + 64
                P.op("pe", lambda e, hf=hf, lo=lo, hi=hi, sc=sc, r=r, st=st: e.matmul(st[hf][0][:], lhsT=KT[lo:hi, sc * 128:(sc + 1) * 128],
                                                                                    rhs=QTt[lo:hi, r, :], start=True, stop=True),
                     reads=["KTall", ("QTt", r)], writes=[st[hf][1]])

        def emit_exp(i):
            st = st_of(i)
            sl = []
            for hf in range(2):
                s_ = pstate["pti"] % 8
                pstate["pti"] += 1
                sl.append(s_)
                P.op("act", lambda e, hf=hf, s_=s_, st=st: e.activation(out=PT[:, s_, :], in_=st[hf][0][:], func=AF.Exp, scale=0.125),
                     reads=[st[hf][1]], writes=[("PT", s_)])
            pt_slot[i] = sl

        def emit_mm2(i):
            k, sc = steps[i]
            acc = acc_of(k)
            for hf in range(2):
                s_ = pt_slot[i][hf]
                P.op("pe", lambda e, hf=hf, s_=s_, sc=sc, acc=acc: e.matmul(acc[hf][0][:], lhsT=VO[:, sc, hf, :], rhs=PT[:, s_, :],
                                                                           start=(sc == 0), stop=(sc == NCH - 1)),
                     reads=["VOall", ("PT", s_)], writes=[acc[hf][1]])

        def fin_a(k):
            acc = acc_of(k)
            for hf in range(2):
                ap_, ak = acc[hf]
                P.op("dve", lambda e, hf=hf, ap_=ap_: e.tensor_copy(out=ACCS[:, hf, :], in_=ap_[:]), reads=[ak], writes=[("ACCS", hf)])
                P.op("dve", lambda e, hf=hf: e.reciprocal(out=RCP[64:128, hf, :], in_=ACCS[64:128, hf, :]), reads=[("ACCS", hf)], writes=[("RCP", hf)])

        def fin_b(k):
            qb, c = iters[k]
            acc = acc_of(k)
            for hf in range(2):
                head = c + 4 * hf
                ap_, ak = acc[hf]
                P.op("pe", lambda e, hf=hf, ap_=ap_: e.matmul(ap_[0:64, :], lhsT=ID32[64:128, 64:128], rhs=RCP[64:128, hf, :], start=True, stop=True),
                     reads=[("RCP", hf), "ID32"], writes=[ak])
                P.op("dve", lambda e, hf=hf, ap_=ap_: e.tensor_tensor(out=OAb[0:64, hf, :], in0=ACCS[0:64, hf, :], in1=ap_[0:64, :], op=ALU.mult),
                     reads=[("ACCS", hf), ak], writes=[("OAb", hf)])
                po_ = (head % 2) * 64
                dma(OAS[head // 2][po_:po_ + 64, qb * 512:qb * 512 + 512], OAb[0:64, hf, :], reads=[("OAb", hf)], writes=["OAS"])

        nst = len(steps)
        emit_mm1(0)
        pending_fin = None
        for i in range(nst):
            k, sc = steps[i]
            if i + 1 < nst:
                emit_mm1(i + 1)
            emit_exp(i)
            emit_mm2(i)
            if pending_fin is not None and sc == min(8, NCH - 1):
                fin_b(pending_fin)
                pending_fin = None
            if sc == NCH - 1:
                fin_a(k)
                pending_fin = k
        if pending_fin is not None:
            fin_b(pending_fin)

    def gla(S, l):
        NB, NCH = S // 512, S // 128
        o2 = O_MAIN
        if 6 * NCH * 128 <= O_DEC:
            DSB = A.view(0, F32, [128, NCH, 128])
            STATE16 = A.view(4 * NCH * 128, BF16, [128, NCH, 128])
        else:
            DSB = A.view(o2, F32, [128, NCH, 128]); o2 += 4 * NCH * 128
            STATE16 = A.view(o2, BF16, [128, NCH, 128]); o2 += 2 * NCH * 128
        GQh = A.view(o2, BF16, [128, S]); o2 += 2 * S
        GKh = A.view(o2, BF16, [128, S]); o2 += 2 * S
        GKTh = A.view(o2, BF16, [128, NCH, 128]); o2 += 2 * S
        GVh = A.view(o2, BF16, [128, NCH, 128]); o2 += 2 * S
        SALL = A.view(o2, F32, [128, NCH, 128]); o2 += 4 * NCH * 128
        AT2 = A.view(o2, BF16, [128, 3, 2, 512]); o2 += 6 * KiB
        OSQ = A.view(o2, BF16, [128, 2, 512]); o2 += 2 * KiB
        SRb = A.view(o2, F32, [128, 3, 512]); o2 += 6 * KiB
        T1 = A.view(o2, F32, [128, 512]); o2 += 2 * KiB
        OGb = A.view(o2, BF16, [128, 2, 512]); o2 += 2 * KiB
        TN2 = A.view(o2, F32, [128, 512]); o2 += 2 * KiB
        RS2 = A.view(o2, F32, [128, 512]); o2 += 2 * KiB
        assert o2 <= ARENA_BYTES, o2
        for h in range(4):
            cst = min(16, NCH)
            for c0 in range(0, NCH, cst):
                dma(GKTh[:, c0:c0 + cst, :], GKT[h][c0:c0 + cst].rearrange("c p n -> p c n"), writes=[("GKTh", c0 // cst)])
                dma(GVh[:, c0:c0 + cst, :], GV[c0:c0 + cst, :, h * 128:(h + 1) * 128].rearrange("c p n -> p c n"), writes=[("GVh", c0 // cst)])
            step = min(2048, S)
            for c0 in range(0, S, step):
                dma(GQh[:, c0:c0 + step], GQ[h][:, c0:c0 + step], writes=[("GQh", c0 // step)])
                dma(GKh[:, c0:c0 + step], GK[h][:, c0:c0 + step], writes=[("GKh", c0 // step)])
            P.op("pool", lambda e: e.memset(SALL[0:64, 0, :], 0.0), writes=[("SA", 0)])
            P.op("pool", lambda e: e.memset(SALL[64:128, NCH - 1, :], 0.0), writes=[("SB", NCH - 1)])
            for cb in range(NB):
                pd, pdk = psn()
                for ci in range(4):
                    c = 4 * cb + ci
                    P.op("pe", lambda e, c=c, ci=ci, pd=pd: e.matmul(pd[:, ci * 128:(ci + 1) * 128], lhsT=GKTh[:, c, :], rhs=GVh[:, c, :], start=True, stop=True),
                         reads=[("GKTh", c // cst), ("GVh", c // cst)], writes=[pdk])
                P.op("act", lambda e, cb=cb, pd=pd: e.activation(out=DSB[:, 4 * cb:4 * cb + 4, :], in_=pd[:].rearrange("p (c n) -> p c n", c=4), func=AF.Copy),
                     reads=[pdk], writes=[("DSB", cb)])

            def fwd_step(c):
                P.op("dve", lambda e, c=c, h=h: e.scalar_tensor_tensor(out=SALL[0:64, c + 1, :], in0=SALL[0:64, c, :], scalar=DEC[0:64, h, c, :],
                                                                     in1=DSB[0:64, c, :], op0=ALU.mult, op1=ALU.add),
                     reads=[("SA", c), ("DSB", c // 4), "DECall"], writes=[("SA", c + 1)])

            def bwd_step(c):
                P.op("dve", lambda e, c=c, h=h: e.scalar_tensor_tensor(out=SALL[64:128, c - 1, :], in0=SALL[64:128, c, :], scalar=DEC[64:128, h, c, :],
                                                                     in1=DSB[64:128, c, :], op0=ALU.mult, op1=ALU.add),
                     reads=[("SB", c), ("DSB", c // 4), "DECall"], writes=[("SB", c - 1)])

            LEAD = min(12, max(1, (NCH - 1) // 2))
            fl = list(range(0, NCH - 1))
            bl = list(range(NCH - 1, 0, -1))
            for k in range(len(fl) + LEAD):
                if k < len(fl):
                    fwd_step(fl[k])
                if 0 <= k - LEAD < len(bl):
                    bwd_step(bl[k - LEAD])
            cst2 = min(16, NCH)
            for c0 in range(0, NCH, cst2):
                en_ = "act"
                rk = [("SA", c) for c in range(c0, c0 + cst2)] + [("SB", c) for c in range(c0, c0 + cst2)]
                wk_ = [("ST16", c) for c in range(c0, c0 + cst2)]
                if en_ == "act":
                    P.op("act", lambda e, c0=c0: e.activation(out=STATE16[:, c0:c0 + cst2, :], in_=SALL[:, c0:c0 + cst2, :], func=AF.Copy), reads=rk, writes=wk_)
                else:
                    P.op("pool", lambda e, c0=c0: e.tensor_copy(out=STATE16[:, c0:c0 + cst2, :], in_=SALL[:, c0:c0 + cst2, :]), reads=rk, writes=wk_)
            m2f = M2[:, 0:128].rearrange("p (o n) -> p o n", o=1).to_broadcast([128, 4, 128])
            m2b = M2[:, 128:256].rearrange("p (o n) -> p o n", o=1).to_broadcast([128, 4, 128])
            v4_ = lambda t: t.rearrange("p (c n) -> p c n", c=4)
            gc = OFF_GLA + l
            pos = {}

            def st3_A(cb):
                paF, pafk = psn()
                paB, pabk = psn()
                r3 = cb % 3
                r2 = cb % 2
                dma(SRb[:, r3, :], GR[h][:, cb * 512:cb * 512 + 512], writes=[("SRb", r3)])
                for ci in range(4):
                    c = 4 * cb + ci
                    cs = slice(c * 128, (c + 1) * 128)
                    osl = slice(ci * 128, (ci + 1) * 128)
                    qk_ = [("GKh", (c * 128) // step), ("GQh", (c * 128) // step)]
                    P.op("pe", lambda e, cs=cs, osl=osl: e.matmul(paF[:, osl], lhsT=GKh[0:64, cs], rhs=GQh[0:64, cs], start=True, stop=True),
                         reads=qk_, writes=[pafk])
                    P.op("pe", lambda e, cs=cs, osl=osl: e.matmul(paB[:, osl], lhsT=GKh[64:128, cs], rhs=GQh[64:128, cs], start=True, stop=True),
                         reads=qk_, writes=[pabk])
                P.op("dve", lambda e: e.tensor_tensor(out=v4_(AT2[:, r3, 0, :]), in0=v4_(paF[:]), in1=m2f, op=ALU.mult),
                     reads=[pafk, "M2"], writes=[("AT2", r3, 0)])
                P.op("dve", lambda e: e.tensor_tensor(out=v4_(AT2[:, r3, 1, :]), in0=v4_(paB[:]), in1=m2b, op=ALU.mult),
                     reads=[pabk, "M2"], writes=[("AT2", r3, 1)])

            def st3_O(cb):
                po, pok = psn()
                pos[cb] = (po, pok)
                r3 = cb % 3
                r2 = cb % 2
                for ci in range(4):
                    c = 4 * cb + ci
                    cs = slice(c * 128, (c + 1) * 128)
                    osl = slice(ci * 128, (ci + 1) * 128)
                    P.op("pe", lambda e, c=c, osl=osl: e.matmul(po[:, osl], lhsT=GVh[:, c, :], rhs=AT2[:, r3, 0, osl], start=True, stop=False),
                         reads=[("GVh", c // cst), ("AT2", r3, 0)], writes=[pok])
                    P.op("pe", lambda e, c=c, osl=osl: e.matmul(po[:, osl], lhsT=GVh[:, c, :], rhs=AT2[:, r3, 1, osl], start=False, stop=False),
                         reads=[("GVh", c // cst), ("AT2", r3, 1)], writes=[pok])
                    P.op("pe", lambda e, c=c, cs=cs, osl=osl: e.matmul(po[:, osl], lhsT=STATE16[:, c, :], rhs=GQh[:, cs], start=False, stop=True),
                         reads=[("ST16", c), ("GQh", (c * 128) // step)], writes=[pok])
                P.op("act", lambda e: e.activation(out=OSQ[:, r2, :], in_=po[:], func=AF.Square), reads=[pok], writes=[("OSQ", r2)])

            def st3_N(cb):
                po, pok = pos.pop(cb)
                r2 = cb % 2
                pn, pnk = psn()
                P.op("pe", lambda e: e.matmul(pn[:], lhsT=ONES16[:], rhs=OSQ[:, r2, :], start=True, stop=True), reads=[("OSQ", r2), "ONES16"], writes=[pnk])
                P.op("act", lambda e: e.activation(out=TN2, in_=pn[:], func=AF.Ln, scale=1.0 / 128, bias=EPSC[:]), reads=[pnk], writes=["TN2"])
                P.op("act", lambda e: e.activation(out=RS2, in_=TN2, func=AF.Exp, scale=-0.5), reads=["TN2"], writes=["RS2"])
                P.op("dve", lambda e: e.scalar_tensor_tensor(out=T1, in0=po[:], scalar=SV[:, gc:gc + 1], in1=RS2, op0=ALU.mult, op1=ALU.mult),
                     reads=[pok, "RS2", "SV"], writes=["T1"])
                r3 = cb % 3
                P.op("dve", lambda e: e.tensor_tensor(out=OGb[:, r2, :], in0=T1, in1=SRb[:, r3, :], op=ALU.mult),
                     reads=["T1", ("SRb", r3)], writes=[("OGb", r2)])
                dma(OG[h][:, cb * 512:cb * 512 + 512], OGb[:, r2, :], reads=[("OGb", r2)], writes=["OG"])

            for stp in range(NB + 2):
                if stp < NB:
                    st3_A(stp)
                if 0 <= stp - 1 < NB:
                    st3_O(stp - 1)
                if 0 <= stp - 2 < NB:
                    st3_N(stp - 2)

    for name, S in seqs:
        NB = S // 512
        P.op("pool", lambda e: e.memset(VO[:, :, :, 64:128], 1.0), writes=["VOones"])
        ws = WS()
        for b in range(NB):
            plan_front(ws, 0)
        load_x_dma(xin[name], 0)
        for b in range(NB):
            load_x(xin[name], b * 512)
            run_carry()
            hk = (lambda b=b: load_x_dma(xin[name], (b + 1) * 512)) if b + 1 < NB else None
            front(ws, 0, b, hook=hk)
        run_carry()
        P.flush()
        if stop_after == "front":
            return nc
        for l in range(L):
            attention(S)
            P.flush()
            if stop_after == "att":
                return nc
            gla(S, l)
            P.flush()
            if stop_after == "gla":
                return nc
            ws = WS()
            for b in range(NB):
                plan_back(ws, l)
                if l + 1 < L:
                    plan_front(ws, l + 1)
            if l + 1 == L:
                pass
            else:
                P.op("pool", lambda e: e.memset(VO[:, :, :, 64:128], 1.0), writes=["VOones"])
            back_loads(0)
            for b in range(NB):
                if l + 1 < L:
                    back(ws, l, b)
                    hk = (lambda b=b: back_loads(b + 1)) if b + 1 < NB else None
                    front(ws, l + 1, b, hook=hk)
                else:
                    hk = (lambda b=b: back_loads(b + 1, oc=True, xt=False)) if b + 1 < NB else None
                    back(ws, l, b, hook=hk)
                    store_y(yout[name], b * 512)
                    if b + 1 < NB:
                        back_loads(b + 1, oc=False, xt=True)
            run_carry()
            P.flush()
    return nc


def make_consts(SMAX):
    ident = np.eye(128, dtype=np.float32)
    rot = np.zeros((128, 128), np.float32)
    for hb in (0, 64):
        for i in range(16):
            rot[hb + 16 + i, hb + i] = -1.0
            rot[hb + i, hb + 16 + i] = 1.0
            rot[hb + 48 + i, hb + 32 + i] = -1.0
            rot[hb + 32 + i, hb + 48 + i] = 1.0
    blk = np.zeros((128, 128), np.float32)
    blk[0:64, 0:64] = 1.0
    blk[64:128, 64:128] = 1.0
    j = np.arange(128)[:, None]
    i = np.arange(128)[None, :]
    m2 = np.concatenate([(j <= i), (j > i)], axis=1).astype(np.float32)
    reset = np.ones((128, 512), np.float32)
    reset[:, ::128] = 0.0
    t = np.arange(SMAX)
    row = (t // 64).astype(np.float32)
    col = (t % 64).astype(np.float32)
    inv_freq = (1.0 / (np.float32(10000.0) ** (np.arange(0, 32, 2, dtype=np.float32) / np.float32(32)))).astype(np.float32)
    ang_r = row[:, None] * inv_freq[None, :]
    ang_c = col[:, None] * inv_freq[None, :]
    ang = np.concatenate([ang_r, ang_r, ang_c, ang_c], axis=-1).astype(np.float32)
    cos = np.cos(ang).astype(np.float32).T
    sin = np.sin(ang).astype(np.float32).T
    cosT = np.ascontiguousarray(np.concatenate([cos, cos], axis=0))
    sinT = np.ascontiguousarray(np.concatenate([sin, sin], axis=0))
    return {"c_ident": ident, "c_rot": rot, "c_blk": blk, "c_m2": m2, "c_reset": reset, "c_cos": cosT, "c_sin": sinT}


WNAMES = ["norm_ffn1", "w_ffn1_gu", "w_ffn1_down", "norm_mix", "w_in", "q_norm", "k_norm", "w_gate_f", "b_gate_f",
          "w_gate_b", "b_gate_b", "gla_norm", "w_out", "norm_ffn2", "w_ffn2_gu", "w_ffn2_down", "norm_out"]


def kernel(**inputs):
    xp = np.asarray(inputs["x_prompt"], dtype=np.float32)
    xs = np.asarray(inputs["x_sample"], dtype=np.float32)
    nb = xp.shape[0]
    SP, SS = xp.shape[1], xs.shape[1]
    depth = np.asarray(inputs["w_in"]).shape[0]
    nc = build([("p", SP), ("s", SS)], depth=depth)
    consts = make_consts(max(SP, SS))
    wts = {k: np.ascontiguousarray(np.asarray(inputs[k], dtype=np.float32)) for k in WNAMES}
    in_maps = []
    for i in range(nb):
        m = {"x_p": np.ascontiguousarray(xp[i]), "x_s": np.ascontiguousarray(xs[i])}
        m.update(wts)
        m.update(consts)
        in_maps.append(m)
    res = run_bass_kernel_spmd(nc, in_maps, core_ids=list(range(nb)))
    yp = np.stack([np.asarray(r["y_p"], dtype=np.float32) for r in res.results], axis=0)
    ys = np.stack([np.asarray(r["y_s"], dtype=np.float32) for r in res.results], axis=0)
    return (yp, ys)
```

```python
import numpy as np
import concourse.bass as bass
import concourse.mybir as mybir
from concourse.bass_utils import run_bass_kernel_spmd

F32 = mybir.dt.float32
BF16 = mybir.dt.bfloat16
AF = mybir.ActivationFunctionType
ALU = mybir.AluOpType

D = 1024
KC = 8
DFF = 2816
JC = 22
DIN = 2336
EPS = 1e-6
TB = 512
NQ = 12


class Op:
    __slots__ = ("eng", "fn", "dma", "deps", "signal", "sem", "val", "prev_val")

    def __init__(self, eng, fn, dma):
        self.eng = eng
        self.fn = fn
        self.dma = dma
        self.deps = []
        self.signal = False
        self.sem = None
        self.val = 0
        self.prev_val = 0


class Prog:
    ENGS = ["pe", "act", "dve", "pool", "sp"]

    def __init__(self, nc):
        self.nc = nc
        self.sem = {e: nc.alloc_semaphore("s_" + e) for e in ["pe", "act", "dve", "pool"]}
        self.cnt = {e: 0 for e in self.sem}
        self.dsem = {q: [nc.alloc_semaphore("d_%s_%d" % (q, i)) for i in range(NQ)] for q in ["sp", "act"]}
        self.dcnt = {q: 0 for q in self.dsem}
        self.dlast = {q: [0] * NQ for q in self.dsem}
        self.waited = {}
        self.reset()
        allsems = list(self.sem.values()) + [s for q in self.dsem for s in self.dsem[q]]
        with nc.Block() as block:
            def clr(e):
                for s in allsems:
                    e.sem_clear(s)
            block.sync(clr)

    def reset(self):
        self.ops = []
        self.writers = {}
        self.readers = {}

    def op(self, eng, fn, reads=(), writes=(), dma=False):
        o = Op(eng, fn, dma)
        idx = len(self.ops)
        deps = {}
        for k in reads:
            for w in self.writers.get(k, ()):
                deps[w] = "RAW"
        for k in writes:
            rs = self.readers.get(k, ())
            for w in self.writers.get(k, ()):
                deps.setdefault(w, "WAW")
            for r in rs:
                deps.setdefault(r, "WAR")
        deps.pop(idx, None)
        o.deps = list(deps.items())
        def keep(lst):
            if dma:
                return lst
            return [i for i in lst if self.ops[i].dma or self.ops[i].eng != eng]
        for k in reads:
            self.readers[k] = keep(self.readers.get(k, [])) + [idx]
        for k in writes:
            if self.readers.get(k):
                self.writers[k] = [idx]
                self.readers[k] = []
            else:
                self.writers[k] = keep(self.writers.get(k, [])) + [idx]
        self.ops.append(o)
        return o

    def _needs_sem(self, p, o, kind):
        if p.dma:
            return True
        if p.eng != o.eng or o.dma:
            return True
        if p.eng == "pe":
            return False
        return True

    def flush(self, name=None):
        nc = self.nc
        ops = self.ops
        if not ops:
            return
        for o in ops:
            for d, kind in o.deps:
                p = ops[d]
                if not p.dma and self._needs_sem(p, o, kind):
                    p.signal = True
        for o in ops:
            if o.dma:
                q = o.eng
                k = self.dcnt[q]
                self.dcnt[q] += 1
                o.sem = (q, k % NQ)
                o.val = 16 * (k // NQ + 1)
                o.prev_val = 16 * (k // NQ)
                self.dlast[q][k % NQ] = o.val
            elif o.signal:
                self.cnt[o.eng] += 1
                o.sem = o.eng
                o.val = self.cnt[o.eng]

        def semobj(key):
            if isinstance(key, tuple):
                return self.dsem[key[0]][key[1]]
            return self.sem[key]

        bname = {"pe": "tensor", "act": "scalar", "dve": "vector", "pool": "gpsimd", "sp": "sync"}
        with nc.Block() as block:
            for ename in self.ENGS:
                elist = [o for o in ops if o.eng == ename]

                def body(e, elist=elist, ename=ename):
                    for o in elist:
                        waits = {}
                        for d, kind in o.deps:
                            p = ops[d]
                            if p.sem is None or not self._needs_sem(p, o, kind):
                                continue
                            if waits.get(p.sem, 0) < p.val:
                                waits[p.sem] = p.val
                        if o.dma and o.prev_val > 0:
                            if waits.get(o.sem, 0) < o.prev_val:
                                waits[o.sem] = o.prev_val
                        for sk, val in waits.items():
                            if self.waited.get((ename, sk), 0) < val:
                                e.wait_ge(semobj(sk), val)
                                self.waited[(ename, sk)] = val
                        ins = o.fn(e)
                        if o.sem is not None:
                            ins.then_inc(semobj(o.sem), 16 if o.dma else 1)
                    if ename in self.dsem:
                        for i in range(NQ):
                            v = self.dlast[ename][i]
                            if v > 0 and self.waited.get((ename, (ename, i)), 0) < v:
                                e.wait_ge(self.dsem[ename][i], v)
                                self.waited[(ename, (ename, i))] = v

                getattr(block, bname[ename])(body)
        self.reset()


def _prod(s):
    r = 1
    for v in s:
        r *= v
    return r


_RE = {2: None, 3: "p (a b) -> p a b", 4: "p (a b c) -> p a b c"}


class Arena:
    def __init__(self, nc, nbytes):
        self.n = nbytes
        self.t = nc.alloc_sbuf_tensor("arena", [128, nbytes // 2], BF16)

    def view(self, off, dtype, shape):
        n = _prod(shape[1:])
        esz = 4 if dtype == F32 else 2
        assert off % 4 == 0 and off + n * esz <= self.n, (off, n * esz, self.n)
        ap = self.t[0:shape[0], off // 2: off // 2 + n * esz // 2]
        if dtype != BF16:
            ap = ap.bitcast(dtype)
        if len(shape) == 3:
            ap = ap.rearrange("p (a b) -> p a b", a=shape[1])
        elif len(shape) == 4:
            ap = ap.rearrange("p (a b c) -> p a b c", a=shape[1], b=shape[2])
        return ap


def build(seqs, depth=2, dbg=False, stop_after=None):
    nc = bass.Bass("TRN2", target_bir_lowering=False)
    P = Prog(nc)
    L = depth
    SMAX = max(S for _, S in seqs)
    NCHMAX = SMAX // 128
    K = 1024

    def din(name, shape, dt=F32):
        return nc.dram_tensor(name, list(shape), dt, kind="ExternalInput").ap()

    def dscr(name, shape, dt):
        kind = "ExternalOutput" if dbg else "Internal"
        return nc.dram_tensor(name, list(shape), dt, kind=kind).ap()

    xin = {n: din("x_" + n, [S, D]) for n, S in seqs}
    yout = {n: nc.dram_tensor("y_" + n, [S, D], F32, kind="ExternalOutput").ap() for n, S in seqs}
    W = {}
    for nm, shp in [("norm_ffn1", [L, D]), ("w_ffn1_gu", [L, D, 2 * DFF]), ("w_ffn1_down", [L, DFF, D]),
                    ("norm_mix", [L, D]), ("w_in", [L, D, DIN]), ("q_norm", [L, 64]), ("k_norm", [L, 64]),
                    ("w_gate_f", [L, 16, 256]), ("b_gate_f", [L, 256]), ("w_gate_b", [L, 16, 256]),
                    ("b_gate_b", [L, 256]), ("gla_norm", [L, 128]), ("w_out", [L, D, D]),
                    ("norm_ffn2", [L, D]), ("w_ffn2_gu", [L, D, 2 * DFF]), ("w_ffn2_down", [L, DFF, D]),
                    ("norm_out", [L, D])]:
        W[nm] = din(nm, shp)
    c_ident = din("c_ident", [128, 128])
    c_rot = din("c_rot", [128, 128])
    c_blk = din("c_blk", [128, 128])
    c_m2 = din("c_m2", [128, 256])
    c_reset = din("c_reset", [128, 512])
    c_cos = din("c_cos", [128, SMAX])
    c_sin = din("c_sin", [128, SMAX])

    WGU = [[dscr("WGU%d%d" % (l, f), [JC, 128, 2 * 8 * 128], BF16) for f in range(2)] for l in range(L)]
    WDN = [[dscr("WDN%d%d" % (l, f), [8, 128, 22 * 128], BF16) for f in range(2)] for l in range(L)]
    WINL = [dscr("WINL%d" % l, [18, 128, 8 * 128], BF16) for l in range(L)]
    WINR = [dscr("WINR%d" % l, [128, 8 * 640], BF16) for l in range(L)]
    WOUT = [dscr("WOUT%d" % l, [8, 128, 8 * 128], BF16) for l in range(L)]
    HS = dscr("HS", [8, 128, SMAX], F32)
    QS = dscr("QS", [4, 128, SMAX], BF16)
    GQ = dscr("GQ", [4, 128, SMAX], BF16)
    GK = dscr("GK", [4, 128, SMAX], BF16)
    GKT = dscr("GKT", [4, NCHMAX, 128, 128], BF16)
    GV = dscr("GV", [NCHMAX, 128, 512], BF16)
    GR = dscr("GR", [4, 128, SMAX], F32)
    OAS = dscr("OAS", [4, 128, SMAX], BF16)
    OG = dscr("OG", [4, 128, SMAX], BF16)

    sb = nc.alloc_sbuf_tensor
    ID32 = sb("ID32", [128, 128], F32)
    ID16 = sb("ID16", [128, 128], BF16)
    ROT32 = sb("ROT32", [128, 128], F32)
    ONES16 = sb("ONES16", [128, 128], BF16)
    BLK16 = sb("BLK16", [128, 128], BF16)
    M2 = sb("M2", [128, 256], F32)
    RESET = sb("RESET", [128, 512], F32)
    SV = sb("SV", [128, 128], F32)
    WG16 = sb("WG16", [32, L * 4 * 128], BF16)
    EPSC = sb("EPSC", [128, 1], F32)
    ps = [nc.alloc_psum_tensor("ps%d" % i, [128, 512], F32) for i in range(8)]
    ARENA_BYTES = 192 * 1024
    A = Arena(nc, ARENA_BYTES)

    OFF_N, OFF_QK, OFF_GLA, OFF_BG = 0, 32 * L, 34 * L, 35 * L

    def gcol(which, l, kc):
        c = OFF_N + which * L * 8 + l * 8 + kc
        return SV[:, c:c + 1]

    state = {"ps": 0}

    def psn():
        i = state["ps"] % 8
        state["ps"] += 1
        return ps[i], ("ps", i)

    def dma(out, in_, reads=(), writes=(), q="sp"):
        return P.op(q, lambda e: e.dma_start(out=out, in_=in_), reads=reads, writes=writes, dma=True)

    KiB = 1024

    def prep():
        stg32 = [A.view(i * 66 * KiB, F32, [128, 11264]) for i in range(2)]
        stg16 = [A.view(i * 66 * KiB + 44 * KiB, BF16, [128, 11264]) for i in range(2)]
        small = A.view(132 * KiB, F32, [128, 128])
        wg32 = A.view(133 * KiB, F32, [32, L * 4 * 128])
        tmpc = A.view(140 * KiB, F32, [128, 128])
        dma(ID32[:], c_ident, writes=["ID32"])
        dma(ROT32[:], c_rot, writes=["ROT32"])
        dma(M2[:], c_m2, writes=["M2"])
        dma(RESET[:], c_reset, writes=["RESET"])
        dma(tmpc, c_blk, writes=["tmpc"])
        P.op("dve", lambda e: e.tensor_copy(out=BLK16[:], in_=tmpc), reads=["tmpc"], writes=["BLK16"])
        P.op("dve", lambda e: e.tensor_copy(out=ID16[:], in_=ID32[:]), reads=["ID32"], writes=["ID16"])
        P.op("pool", lambda e: e.memset(ONES16[:], 1.0), writes=["ONES16"])
        P.op("pool", lambda e: e.memset(EPSC[:], EPS), writes=["EPSC"])
        P.op("pool", lambda e: e.memset(small, 0.0), writes=["small"])
        for wi, nm in enumerate(["norm_ffn1", "norm_mix", "norm_ffn2", "norm_out"]):
            r0 = OFF_N + wi * L * 8
            dma(small[r0:r0 + L * 8, :], W[nm].rearrange("l (kc p) -> (l kc) p", p=128), reads=["small0"], writes=["small"])
        for wi, nm in enumerate(["q_norm", "k_norm"]):
            r0 = OFF_QK + wi * L
            dma(small[r0:r0 + L, 0:64], W[nm], reads=["small0"], writes=["small"])
            dma(small[r0:r0 + L, 64:128], W[nm], reads=["small0"], writes=["small"])
        dma(small[OFF_GLA:OFF_GLA + L, :], W["gla_norm"], reads=["small0"], writes=["small"])
        dma(small[OFF_BG:OFF_BG + 4 * L, 0:64], W["b_gate_f"].rearrange("l (h n) -> (l h) n", n=64), reads=["small0"], writes=["small"])
        dma(small[OFF_BG:OFF_BG + 4 * L, 64:128], W["b_gate_b"].rearrange("l (h n) -> (l h) n", n=64), reads=["small0"], writes=["small"])
        pt, pk = psn()
        P.op("pe", lambda e: e.transpose(out=pt[:, 0:128], in_=small, identity=ID32[:]), reads=["small", "ID32"], writes=[pk])
        P.op("dve", lambda e: e.tensor_copy(out=SV[:], in_=pt[:, 0:128]), reads=[pk], writes=["SV"])
        P.op("dve", lambda e: e.tensor_scalar(out=SV[:, OFF_BG:OFF_BG + 4 * L], in0=SV[:, OFF_BG:OFF_BG + 4 * L],
                                              scalar1=-1.0, scalar2=None, op0=ALU.mult), reads=["SV"], writes=["SV"])
        P.op("pool", lambda e: e.memset(wg32, 0.0), writes=["wg32"])
        wgv = wg32.rearrange("p (l h n) -> p l h n", l=L, h=4)
        for l in range(L):
            dma(wgv[0:16, l, :, 0:64], W["w_gate_f"][l].rearrange("r (h n) -> r h n", n=64), reads=["wg320"], writes=["wg32"])
            dma(wgv[16:32, l, :, 64:128], W["w_gate_b"][l].rearrange("r (h n) -> r h n", n=64), reads=["wg320"], writes=["wg32"])
        P.op("dve", lambda e: e.tensor_copy(out=WG16[:], in_=wg32), reads=["wg32"], writes=["WG16"])

        st = {"i": 0}
        engs = ["dve", "act"]
        pend = []

        def cast(out, in_, r, w):
            en = engs[st["i"] % 2]
            st["i"] += 1
            if en == "act":
                P.op("act", lambda e: e.activation(out=out, in_=in_, func=AF.Copy), reads=r, writes=w)
            else:
                P.op(en, lambda e: e.tensor_copy(out=out, in_=in_), reads=r, writes=w)

        pi = {"i": 0}

        def plain(src, nkc, c0, ncols, dst_fn):
            i = pi["i"] % 2
            pi["i"] += 1
            s32 = stg32[i][:, 0:nkc * ncols].rearrange("p (kc n) -> p kc n", kc=nkc)
            nm = ncols // 128
            half = (nkc + 1) // 2
            srcv = src[:, c0:c0 + ncols].rearrange("(kc p) n -> p kc n", p=128)
            dma(s32[:, 0:half, :], srcv[:, 0:half, :], writes=[("s32", i)])
            dma(s32[:, half:nkc, :], srcv[:, half:nkc, :], writes=[("s32", i)])
            while pend:
                pend.pop(0)()
            s16 = stg16[i][:, 0:nkc * ncols].rearrange("p (m kc c) -> p m kc c", m=nm, kc=nkc)
            s32p = stg32[i][:, 0:nkc * ncols].rearrange("p (kc m c) -> p m kc c", kc=nkc, m=nm)
            step = max(1, nm // 3)
            for m0 in range(0, nm, step):
                m1 = min(nm, m0 + step)
                cast(s16[:, m0:m1], s32p[:, m0:m1], [("s32", i)], [("s16", i)])
            s16v_ = stg16[i][:, 0:nkc * ncols].rearrange("p (m n) -> p m n", m=nm)
            pend.append(lambda: dst_fn(s16v_, ("s16", i)))

        for l in range(L):
            for f in range(2):
                src = W["w_ffn%d_gu" % (f + 1)][l]
                for t in range(2):
                    for jg in range(2):
                        def dst_fn(s16v, key, l=l, f=f, t=t, jg=jg):
                            dv = WGU[l][f][11 * jg:11 * jg + 11].rearrange("j p (t n) -> p j t n", t=2)[:, :, t, :]
                            dma(dv, s16v, reads=[key], writes=["WGU"])
                        plain(src, 8, t * DFF + jg * 1408, 1408, dst_fn)
                src = W["w_ffn%d_down" % (f + 1)][l]
                for mh in range(2):
                    def dst_fn(s16v, key, l=l, f=f, mh=mh):
                        dv = WDN[l][f][4 * mh:4 * mh + 4].rearrange("m p n -> p m n")
                        dma(dv, s16v, reads=[key], writes=["WDN"])
                    plain(src, 22, mh * 512, 512, dst_fn)
            def dst_fn(s16v, key, l=l):
                dma(WOUT[l].rearrange("m p n -> p m n"), s16v, reads=[key], writes=["WOUT"])
            plain(W["w_out"][l], 8, 0, 1024, dst_fn)
            src = W["w_in"][l]
            i = pi["i"] % 2
            pi["i"] += 1
            sA = stg32[i][:, 0:8 * 1280].rearrange("p (kc n) -> p kc n", kc=8)
            srcv = src[:, 0:1280].rearrange("(kc p) n -> p kc n", p=128)
            dma(sA[:, 0:4, :], srcv[:, 0:4, :], writes=[("s32", i)])
            dma(sA[:, 4:8, :], srcv[:, 4:8, :], writes=[("s32", i)])
            while pend:
                pend.pop(0)()
            T16 = A.view(152 * KiB, BF16, [128, 18 * 1024]).rearrange("p (t kc c) -> p t kc c", t=18, kc=8)
            k32, k16 = [("s32", i)], ["T16"]
            cast(T16[:, 1:5, :, 0:64], sA[:, :, 0:256].rearrange("p kc (c n) -> p c kc n", n=64), k32, k16)
            cast(T16[:, 1:5, :, 64:128], sA[:, :, 256:512].rearrange("p kc (c n) -> p c kc n", n=64), k32, k16)
            cast(T16[:, 5, :, :], sA[:, :, 512:640], k32, k16)
            for h in range(4):
                for half in range(2):
                    cast(T16[:, 6 + 2 * h, :, 64 * half:64 * half + 64], sA[:, :, 768 + 64 * h:768 + 64 * h + 64], k32, k16)
                    cast(T16[:, 7 + 2 * h, :, 64 * half:64 * half + 64], sA[:, :, 1024 + 64 * h:1024 + 64 * h + 64], k32, k16)
            R16 = A.view(141 * KiB, BF16, [128, 8, 640])
            cast(R16[:, :, 0:128], sA[:, :, 640:768], k32, ["R16"])
            j = pi["i"] % 2
            pi["i"] += 1
            sB = stg32[j][:, 0:8 * 1056].rearrange("p (kc n) -> p kc n", kc=8)
            srcv = src[:, 1280:2336].rearrange("(kc p) n -> p kc n", p=128)
            dma(sB[:, 0:4, :], srcv[:, 0:4, :], writes=[("s32", j)])
            dma(sB[:, 4:8, :], srcv[:, 4:8, :], writes=[("s32", j)])
            kb = [("s32", j)]
            cast(R16[:, :, 128:640], sB[:, :, 0:512], kb, ["R16"])
            cast(T16[:, 14:18, :, :], sB[:, :, 512:1024].rearrange("p kc (h n) -> p h kc n", n=128), kb, k16)
            P.op("pool", lambda e, T16=T16: e.memset(T16[:, 0, :, 32:128], 0.0), writes=k16)
            cast(T16[:, 0, :, 0:32], sB[:, :, 1024:1056], kb, k16)
            pend.append(lambda l=l, k16=k16: dma(WINL[l].rearrange("t p n -> p t n"), A.view(152 * KiB, BF16, [128, 18, 1024]),
                                                 reads=k16, writes=["WINL"]))
            pend.append(lambda l=l, R16=R16: dma(WINR[l], R16.rearrange("p kc n -> p (kc n)"), reads=["R16"], writes=["WINR"]))
        while pend:
            pend.pop(0)()
        P.flush()

    prep()
    if stop_after == "prep":
        return nc

    o = 0
    def take(nbytes):
        nonlocal o
        r = o
        o += nbytes
        return r
    O_KT = take(2 * SMAX)
    O_VO = take(NCHMAX * 2 * 128 * 2)
    O_DEC = take(4 * NCHMAX * 4)
    O_MAIN = o
    KT = A.view(O_KT, BF16, [128, SMAX])
    VO = A.view(O_VO, BF16, [128, NCHMAX, 2, 128])
    DEC = A.view(O_DEC, F32, [128, 4, NCHMAX, 1])
    XT = A.view(take(16 * KiB), F32, [128, 8, 512])
    XN = A.view(take(8 * KiB), BF16, [128, 8, 512])
    O_BIG = take(22 * KiB)
    ACTT = A.view(O_BIG, BF16, [128, 22, 512])
    XTOK = A.view(O_BIG, F32, [128, 4, 1024])
    SQ = A.view(O_BIG + 16 * KiB, BF16, [128, 4, 512])
    RS = A.view(take(2 * KiB), F32, [128, 512])
    TMPN = A.view(take(2 * KiB), F32, [128, 512])
    SG = A.view(take(4 * KiB), F32, [128, 2, 512])
    NW = 8
    WR = [A.view(take(4 * KiB), BF16, [128, 2048]) for _ in range(NW)]
    OC = A.view(take(8 * KiB), BF16, [128, 8, 512])
    CS = A.view(take(4 * KiB), F32, [128, 2, 512])
    U32 = A.view(take(2 * KiB), F32, [128, 512])
    QN32 = A.view(take(2 * KiB), F32, [128, 512])
    T132 = A.view(take(2 * KiB), F32, [128, 512])
    T232 = A.view(take(2 * KiB), F32, [128, 512])
    QF = A.view(take(2 * KiB), BF16, [128, 2, 512])
    SQ2 = A.view(take(1 * KiB), BF16, [128, 512])
    GLOW = A.view(take(1 * KiB), BF16, [128, 512])
    L32 = A.view(take(2 * KiB), F32, [128, 512])
    PP32 = A.view(take(2 * KiB), F32, [128, 512])
    Z32 = A.view(take(2 * KiB), F32, [128, 512])
    TZ32 = A.view(take(2 * KiB), F32, [128, 512])
    EQ32 = A.view(take(2 * KiB), F32, [128, 512])
    EK32 = A.view(take(2 * KiB), F32, [128, 512])
    QD16 = A.view(take(2 * KiB), BF16, [128, 2, 512])
    KD16 = A.view(take(2 * KiB), BF16, [128, 2, 512])
    KDS16 = A.view(take(2 * KiB), BF16, [128, 2, 512])
    KDT16 = A.view(take(2 * KiB), BF16, [128, 2, 512])
    SR32 = A.view(take(4 * KiB), F32, [128, 2, 512])
    VLT = A.view(take(4 * KiB), BF16, [128, 4, 512])
    assert o <= ARENA_BYTES, o
    O_END_MAIN = o

    carry = []

    def run_carry():
        while carry:
            carry.pop(0)()

    def bigkeys(j0, j1):
        return [("big", j) for j in range(j0, j1)]

    class WS:
        def __init__(self):
            self.plan = []
            self.loaded = 0
            self.next = 0

        def add(self, ap2d, n):
            self.plan.append((ap2d, n))

        def _load(self, i):
            ap2d, n = self.plan[i]
            s = i % NW
            dma(WR[s][:, 0:n], ap2d, writes=[("wr", s)])

        def get(self):
            i = self.next
            self.next += 1
            while self.loaded < min(len(self.plan), i + 4):
                self._load(self.loaded)
                self.loaded += 1
            assert i < self.loaded
            s = i % NW
            return WR[s], ("wr", s)

    def plan_ffn(ws, l, f):
        for j in range(JC):
            ws.add(WGU[l][f][j], 2048)
        for m in range(8):
            ws.add(WDN[l][f][m][:, 0:1408], 1408)
            ws.add(WDN[l][f][m][:, 1408:2816], 1408)

    def plan_front(ws, l):
        plan_ffn(ws, l, 0)
        def pl_(pc):
            ws.add(WINL[l][2 * pc:2 * pc + 2].rearrange("t p n -> p t n"), 2048)
        wr_ = WINR[l].rearrange("p (kc n) -> p kc n", kc=8)
        pl_(0); pl_(7); pl_(1); pl_(8); pl_(2)
        ws.add(wr_[:, :, 0:128], 1024)
        pl_(3)
        ws.add(wr_[:, 0:4, 128:640], 2048)
        ws.add(wr_[:, 4:8, 128:640], 2048)
        pl_(4); pl_(5); pl_(6)

    def plan_back(ws, l):
        for m in range(8):
            ws.add(WOUT[l][m], 1024)
        plan_ffn(ws, l, 1)

    def _load(self, i):
        ap, n = self.plan[i]
        s = i % NW
        dst = WR[s][:, 0:n]
        if len(ap.shape) == 3:
            dst = dst.rearrange("p (a b) -> p a b", a=ap.shape[1])
        dma(dst, ap, writes=[("wr", s)])
    WS._load = _load

    def rmsnorm(which, l, dst16=None):
        pn, pk = psn()
        for kc in range(8):
            if kc % 2 == 0:
                P.op("act", lambda e, kc=kc: e.activation(out=XN[:, kc, :], in_=XT[:, kc, :], func=AF.Square),
                     reads=[("XT", kc)], writes=[("XN", kc)])
            else:
                P.op("dve", lambda e, kc=kc: e.tensor_tensor(out=XN[:, kc, :], in0=XT[:, kc, :], in1=XT[:, kc, :], op=ALU.mult),
                     reads=[("XT", kc)], writes=[("XN", kc)])
        for kc in range(8):
            P.op("pe", lambda e, kc=kc: e.matmul(pn[:], lhsT=ONES16[:], rhs=XN[:, kc, :], start=(kc == 0), stop=(kc == 7)),
                 reads=[("XN", kc), "ONES16"], writes=[pk])
        P.op("act", lambda e: e.activation(out=TMPN, in_=pn[:], func=AF.Ln, scale=1.0 / D, bias=EPSC[:]), reads=[pk], writes=["TMPN"])
        P.op("act", lambda e: e.activation(out=RS, in_=TMPN, func=AF.Exp, scale=-0.5), reads=["TMPN"], writes=["RS"])
        for kc in range(8):
            if dst16 is not None:
                P.op("dve", lambda e, kc=kc: e.scalar_tensor_tensor(out=XN[:, kc, :], in0=XT[:, kc, :], scalar=gcol(which, l, kc),
                                                                    in1=RS, op0=ALU.mult, op1=ALU.mult),
                     reads=[("XT", kc), "RS", "SV"], writes=[("XN", kc)])
            else:
                P.op("dve", lambda e, kc=kc: e.scalar_tensor_tensor(out=XT[:, kc, :], in0=XT[:, kc, :], scalar=gcol(which, l, kc),
                                                                    in1=RS, op0=ALU.mult, op1=ALU.mult),
                     reads=[("XT", kc), "RS", "SV"], writes=[("XT", kc)])

    def ffn(ws):
        for j in range(JC):
            w, wk = ws.get()
            wv = w[:, 0:2048].rearrange("p (t kc c) -> p t kc c", t=2, kc=8)
            pg, pgk = psn()
            pu, puk = psn()
            for kc in range(8):
                P.op("pe", lambda e, kc=kc, wv=wv, pg=pg: e.matmul(pg[:], lhsT=wv[:, 0, kc, :], rhs=XN[:, kc, :], start=(kc == 0), stop=(kc == 7)),
                     reads=[wk, ("XN", kc)], writes=[pgk])
            for kc in range(8):
                P.op("pe", lambda e, kc=kc, wv=wv, pu=pu: e.matmul(pu[:], lhsT=wv[:, 1, kc, :], rhs=XN[:, kc, :], start=(kc == 0), stop=(kc == 7)),
                     reads=[wk, ("XN", kc)], writes=[puk])
            r = j % 2
            P.op("act", lambda e, r=r, pg=pg: e.activation(out=SG[:, r, :], in_=pg[:], func=AF.Silu), reads=[pgk], writes=[("SG", r)])
            P.op("dve", lambda e, r=r, pu=pu, j=j: e.tensor_tensor(out=ACTT[:, j, :], in0=SG[:, r, :], in1=pu[:], op=ALU.mult),
                 reads=[("SG", r), puk], writes=[("big", j)])
        for m in range(8):
            wa, wak = ws.get()
            wb, wbk = ws.get()
            pd, pdk = psn()
            for kc in range(22):
                wsrc = wa if kc < 11 else wb
                wkk = wak if kc < 11 else wbk
                kk = kc if kc < 11 else kc - 11
                P.op("pe", lambda e, kc=kc, kk=kk, wsrc=wsrc, pd=pd: e.matmul(pd[:], lhsT=wsrc[:, kk * 128:(kk + 1) * 128], rhs=ACTT[:, kc, :],
                                                                             start=(kc == 0), stop=(kc == 21)),
                     reads=[wkk, ("big", kc)], writes=[pdk])
            P.op("dve", lambda e, m=m, pd=pd: e.scalar_tensor_tensor(out=XT[:, m, :], in0=pd[:], scalar=0.5, in1=XT[:, m, :],
                                                                     op0=ALU.mult, op1=ALU.add),
                 reads=[pdk, ("XT", m)], writes=[("XT", m)])

    def load_x_dma(x_ap, t0):
        xv = x_ap[t0:t0 + 512, :].rearrange("(g p) n -> p g n", p=128)
        dma(XTOK[:, 0:2, :], xv[:, 0:2, :], writes=bigkeys(0, 8))
        dma(XTOK[:, 2:4, :], xv[:, 2:4, :], writes=bigkeys(8, 16))

    def load_x(x_ap, t0):
        for kc in range(8):
            pt, pk = psn()
            for g in range(4):
                P.op("pe", lambda e, g=g, kc=kc, pt=pt: e.transpose(out=pt[:, g * 128:(g + 1) * 128], in_=XTOK[:, g, kc * 128:(kc + 1) * 128], identity=ID32[:]),
                     reads=bigkeys(4 * g, 4 * g + 4) + ["ID32"], writes=[pk])
            if kc % 2 == 0:
                P.op("act", lambda e, kc=kc, pt=pt: e.activation(out=XT[:, kc, :], in_=pt[:], func=AF.Copy), reads=[pk], writes=[("XT", kc)])
            else:
                P.op("dve", lambda e, kc=kc, pt=pt: e.tensor_copy(out=XT[:, kc, :], in_=pt[:]), reads=[pk], writes=[("XT", kc)])

    def store_y(y_ap, t0):
        yv = y_ap[t0:t0 + 512, :].rearrange("(g p) n -> p g n", p=128)
        for g in range(4):
            for hh in range(2):
                pt, pk = psn()
                for k4 in range(4):
                    kc = hh * 4 + k4
                    P.op("pe", lambda e, g=g, kc=kc, k4=k4, pt=pt: e.transpose(out=pt[:, k4 * 128:(k4 + 1) * 128], in_=XT[:, kc, g * 128:(g + 1) * 128], identity=ID32[:]),
                         reads=[("XT", kc), "ID32"], writes=[pk])
                if hh == 0:
                    P.op("act", lambda e, g=g, pt=pt: e.activation(out=XTOK[:, g, 0:512], in_=pt[:], func=AF.Copy), reads=[pk], writes=bigkeys(4 * g, 4 * g + 2))
                else:
                    P.op("dve", lambda e, g=g, pt=pt: e.tensor_copy(out=XTOK[:, g, 512:1024], in_=pt[:]), reads=[pk], writes=bigkeys(4 * g + 2, 4 * g + 4))
        dma(yv[:, 0:2, :], XTOK[:, 0:2, :], reads=bigkeys(0, 8), writes=["Y"])
        dma(yv[:, 2:4, :], XTOK[:, 2:4, :], reads=bigkeys(8, 16), writes=["Y"])

    def front(ws, l, b, hook=None):
        t0 = b * 512
        rmsnorm(0, l, XN)
        ffn(ws)
        dma(HS.rearrange("kc p t -> p kc t")[:, :, t0:t0 + 512], XT, reads=[("XT", kc) for kc in range(8)], writes=[("HS", b)])
        rmsnorm(1, l, XN)
        if hook is not None:
            hook()
        dma(CS[:, 0, :], c_cos[:, t0:t0 + 512], writes=["CS"])
        dma(CS[:, 1, :], c_sin[:, t0:t0 + 512], writes=["CS"])
        xnk = [("XN", kc) for kc in range(8)]

        def proj(w, wk, ti, M=128):
            pp, ppk = psn()
            wv = w[:, 0:2048].rearrange("p (t kc c) -> p t kc c", t=2, kc=8)
            for kc in range(8):
                P.op("pe", lambda e, kc=kc: e.matmul(pp[0:M, :], lhsT=wv[:, ti, kc, 0:M], rhs=XN[:, kc, :], start=(kc == 0), stop=(kc == 7)),
                     reads=[wk, ("XN", kc)], writes=[ppk])
            return pp, ppk

        def qk_post(pp, ppk, gc, dest, dkeys):
            P.op("act", lambda e: e.activation(out=U32, in_=pp[:], func=AF.Copy), reads=[ppk], writes=["U32"])
            P.op("act", lambda e: e.activation(out=SQ2, in_=pp[:], func=AF.Square), reads=[ppk], writes=["SQ2"])
            p2, p2k = psn()
            P.op("pe", lambda e: e.matmul(p2[:], lhsT=BLK16[:], rhs=SQ2, start=True, stop=True), reads=["SQ2", "BLK16"], writes=[p2k])
            P.op("act", lambda e: e.activation(out=TMPN, in_=p2[:], func=AF.Ln, scale=1.0 / 64, bias=EPSC[:]), reads=[p2k], writes=["TMPN"])
            P.op("act", lambda e: e.activation(out=RS, in_=TMPN, func=AF.Exp, scale=-0.5), reads=["TMPN"], writes=["RS"])
            P.op("dve", lambda e: e.scalar_tensor_tensor(out=QN32, in0=U32, scalar=SV[:, gc:gc + 1], in1=RS, op0=ALU.mult, op1=ALU.mult),
                 reads=["U32", "RS", "SV"], writes=["QN32"])
            p3, p3k = psn()
            P.op("pe", lambda e: e.matmul(p3[:], lhsT=ROT32[:], rhs=QN32, start=True, stop=True), reads=["QN32", "ROT32"], writes=[p3k])
            P.op("pool", lambda e: e.tensor_tensor(out=T132, in0=QN32, in1=CS[:, 0, :], op=ALU.mult), reads=["QN32", "CS"], writes=["T132"])
            P.op("dve", lambda e: e.tensor_tensor(out=T232, in0=p3[:], in1=CS[:, 1, :], op=ALU.mult), reads=[p3k, "CS"], writes=["T232"])
            P.op("dve", lambda e: e.tensor_tensor(out=dest, in0=T132, in1=T232, op=ALU.add), reads=["T132", "T232"], writes=dkeys)

        def qk_s1(pp, ppk):
            P.op("act", lambda e: e.activation(out=U32, in_=pp[:], func=AF.Copy), reads=[ppk], writes=["U32"])
            P.op("act", lambda e: e.activation(out=SQ2, in_=pp[:], func=AF.Square), reads=[ppk], writes=["SQ2"])

        def qk_p2():
            p2, p2k = psn()
            P.op("pe", lambda e: e.matmul(p2[:], lhsT=BLK16[:], rhs=SQ2, start=True, stop=True), reads=["SQ2", "BLK16"], writes=[p2k])
            return p2, p2k

        def qk_s2(p2, p2k, gc):
            P.op("act", lambda e: e.activation(out=TMPN, in_=p2[:], func=AF.Ln, scale=1.0 / 64, bias=EPSC[:]), reads=[p2k], writes=["TMPN"])
            P.op("act", lambda e: e.activation(out=RS, in_=TMPN, func=AF.Exp, scale=-0.5), reads=["TMPN"], writes=["RS"])
            P.op("dve", lambda e: e.scalar_tensor_tensor(out=QN32, in0=U32, scalar=SV[:, gc:gc + 1], in1=RS, op0=ALU.mult, op1=ALU.mult),
                 reads=["U32", "RS", "SV"], writes=["QN32"])

        def qk_p3():
            p3, p3k = psn()
            P.op("pe", lambda e: e.matmul(p3[:], lhsT=ROT32[:], rhs=QN32, start=True, stop=True), reads=["QN32", "ROT32"], writes=[p3k])
            return p3, p3k

        def qk_s3(p3, p3k, dest, dkeys):
            P.op("dve", lambda e: e.tensor_tensor(out=T132, in0=QN32, in1=CS[:, 0, :], op=ALU.mult), reads=["QN32", "CS"], writes=["T132"])
            P.op("dve", lambda e: e.tensor_tensor(out=T232, in0=p3[:], in1=CS[:, 1, :], op=ALU.mult), reads=[p3k, "CS"], writes=["T232"])
            P.op("dve", lambda e: e.tensor_tensor(out=dest, in0=T132, in1=T232, op=ALU.add), reads=["T132", "T232"], writes=dkeys)

        def qa_dest(c):
            r = c % 2
            return QF[:, r, :], [("QF", r)]

        def qa_store(c):
            r = c % 2
            dma(QS[c][:, t0:t0 + 512], QF[:, r, :], reads=[("QF", r)], writes=[("QS", b)])

        def rl(w_, wk_, ti, h):
            pp, ppk = proj(w_, wk_, ti)
            r = h % 2
            P.op("act", lambda e, r=r, pp=pp: e.activation(out=SR32[:, r, :], in_=pp[:], func=AF.Silu), reads=[ppk], writes=[("SR", r)])
            dma(GR[h][:, t0:t0 + 512], SR32[:, r, :], reads=[("SR", r)], writes=[("GR", b)])

        def va(w_, wk_):
            wva = w_[:, 0:1024].rearrange("p (kc n) -> p kc n", kc=8)
            pv, pvk = psn()
            for g in range(4):
                for kc in range(8):
                    P.op("pe", lambda e, g=g, kc=kc: e.matmul(pv[:, g * 128:(g + 1) * 128], lhsT=XN[:, kc, g * 128:(g + 1) * 128], rhs=wva[:, kc, :],
                                                              start=(kc == 0), stop=(kc == 7)),
                         reads=[wk_, ("XN", kc)], writes=[pvk])
            P.op("act", lambda e: e.activation(out=VO[:, 4 * b:4 * b + 4, :, 0:64], in_=pv[:].rearrange("p (g k n) -> p g k n", g=4, k=2), func=AF.Copy),
                 reads=[pvk], writes=[("VO", b)])

        def vl(g, wl0, wl0k, wl1, wl1k):
            pl, plk = psn()
            for kc in range(8):
                wsrc = wl0 if kc < 4 else wl1
                wkk = wl0k if kc < 4 else wl1k
                wvv = wsrc[:, 0:2048].rearrange("p (kc n) -> p kc n", kc=4)
                P.op("pe", lambda e, kc=kc, wvv=wvv: e.matmul(pl[:], lhsT=XN[:, kc, g * 128:(g + 1) * 128], rhs=wvv[:, kc % 4, :],
                                                               start=(kc == 0), stop=(kc == 7)),
                     reads=[wkk, ("XN", kc)], writes=[plk])
            if g % 2 == 0:
                P.op("act", lambda e: e.activation(out=VLT[:, g, :], in_=pl[:], func=AF.Copy), reads=[plk], writes=[("VLT", g)])
            else:
                P.op("dve", lambda e: e.tensor_copy(out=VLT[:, g, :], in_=pl[:]), reads=[plk], writes=[("VLT", g)])
            if g == 3:
                dma(GV[4 * b:4 * b + 4].rearrange("c p n -> p c n"), VLT, reads=[("VLT", g_) for g_ in range(4)], writes=[("GV", b)])

        v4 = lambda t, lo, hi: t[lo:hi, :].rearrange("p (c n) -> p c n", c=4)
        wg = WG16[:].rearrange("p (l h n) -> p l h n", l=L, h=4)

        def gla_a(w_, wk_, h):
            pq, pqk = proj(w_, wk_, 0)
            pkk_, pkkk = proj(w_, wk_, 1)
            px, pxk = psn()
            P.op("pe", lambda e: e.matmul(px[:], lhsT=wg[0:32, l, h, :], rhs=GLOW[0:32, :], start=True, stop=True),
                 reads=["GLOW", "WG16"], writes=[pxk])
            bcol = OFF_BG + 4 * l + h
            P.op("act", lambda e: e.activation(out=TZ32, in_=px[:], func=AF.Exp, scale=-1.0, bias=SV[:, bcol:bcol + 1]),
                 reads=[pxk, "SV"], writes=["TZ32"])
            P.op("act", lambda e: e.activation(out=L32, in_=TZ32, func=AF.Ln, bias=1.0), reads=["TZ32"], writes=["L32"])
            P.op("dve", lambda e: e.tensor_tensor_scan(out=PP32, data0=RESET[:], data1=L32, initial=0.0, op0=ALU.mult, op1=ALU.add),
                 reads=["L32", "RESET"], writes=["PP32"])
            P.op("act", lambda e: e.activation(out=Z32[0:64, :], in_=PP32[0:64, :], func=AF.Copy), reads=["PP32"], writes=["Z32"])
            P.op("dve", lambda e: e.tensor_tensor(out=TZ32[64:128, :], in0=L32[64:128, :], in1=PP32[64:128, :], op=ALU.subtract),
                 reads=["L32", "PP32"], writes=["TZ32"])
            P.op("dve", lambda e: e.tensor_tensor(out=v4(Z32, 64, 128), in0=v4(TZ32, 64, 128),
                                                  in1=v4(PP32, 64, 128)[:, :, 127:128].to_broadcast([64, 4, 128]), op=ALU.add),
                 reads=["TZ32", "PP32"], writes=["Z32"])
            P.op("act", lambda e: e.activation(out=EQ32, in_=Z32, func=AF.Exp, scale=-1.0 / 16), reads=["Z32"], writes=["EQ32"])
            P.op("act", lambda e: e.activation(out=EK32, in_=Z32, func=AF.Exp, scale=1.0 / 16), reads=["Z32"], writes=["EK32"])
            P.op("dve", lambda e: e.tensor_copy(out=DEC[0:64, h, 4 * b:4 * b + 4, :], in_=v4(EQ32, 0, 64)[:, :, 127:128]),
                 reads=["EQ32"], writes=[("DEC", h, b)])
            P.op("dve", lambda e: e.tensor_copy(out=DEC[64:128, h, 4 * b:4 * b + 4, :], in_=v4(EQ32, 64, 128)[:, :, 0:1]),
                 reads=["EQ32"], writes=[("DEC", h, b)])
            r = h % 2
            P.op("dve", lambda e: e.scalar_tensor_tensor(out=QD16[:, r, :], in0=pq[:], scalar=0.125, in1=EQ32, op0=ALU.mult, op1=ALU.mult),
                 reads=[pqk, "EQ32"], writes=[("QD", r)])
            P.op("dve", lambda e: e.tensor_tensor(out=KD16[:, r, :], in0=pkk_[:], in1=EK32, op=ALU.mult),
                 reads=[pkkk, "EK32"], writes=[("KD", r)])
            P.op("dve", lambda e: e.tensor_tensor(out=KDS16[:, r, :].rearrange("p (c n) -> p c n", c=4),
                                                  in0=KD16[:, r, :].rearrange("p (c n) -> p c n", c=4),
                                                  in1=DEC[:, h, 4 * b:4 * b + 4, :].to_broadcast([128, 4, 128]), op=ALU.mult),
                 reads=[("KD", r), ("DEC", h, b)], writes=[("KDS", r)])
            dma(GQ[h][:, t0:t0 + 512], QD16[:, r, :], reads=[("QD", r)], writes=[("GQ", b)])
            dma(GK[h][:, t0:t0 + 512], KD16[:, r, :], reads=[("KD", r)], writes=[("GK", b)])

        def gla_b(h):
            r = h % 2
            ptt, ptk = psn()
            ptb = ptt[:].bitcast(BF16)
            for ch in range(4):
                P.op("pe", lambda e, ch=ch: e.transpose(out=ptb[:, ch * 128:(ch + 1) * 128], in_=KDS16[:, r, ch * 128:(ch + 1) * 128], identity=ID16[:]),
                     reads=[("KDS", r), "ID16"], writes=[ptk])
            P.op("act", lambda e: e.activation(out=KDT16[:, r, :], in_=ptb[:, 0:512], func=AF.Copy), reads=[ptk], writes=[("KDT", r)])
            dma(GKT[h][4 * b:4 * b + 4].rearrange("c p n -> p c n"), KDT16[:, r, :].rearrange("p (c n) -> p c n", c=4),
                reads=[("KDT", r)], writes=[("GKT", b)])

        gq = OFF_QK + 0 * L + l
        gk = OFF_QK + 1 * L + l
        w0, w0k = ws.get()
        pp, ppk = proj(w0, w0k, 0, M=32)
        P.op("act", lambda e, pp=pp: e.activation(out=GLOW[0:32, :], in_=pp[0:32, :], func=AF.Copy), reads=[ppk], writes=["GLOW"])
        ppa, ppak = proj(w0, w0k, 1)
        qk_s1(ppa, ppak)
        w7, w7k = ws.get()
        rl(w7, w7k, 0, 0)
        p2, p2k = qk_p2()
        qk_s2(p2, p2k, gq)
        w1, w1k = ws.get()
        ppb, ppbk = proj(w1, w1k, 0)
        rl(w7, w7k, 1, 1)
        p3, p3k = qk_p3()
        qk_s3(p3, p3k, *qa_dest(0))
        qa_store(0)
        qk_s1(ppb, ppbk)
        ppc, ppck = proj(w1, w1k, 1)
        p2, p2k = qk_p2()
        qk_s2(p2, p2k, gq)
        w8, w8k = ws.get()
        rl(w8, w8k, 0, 2)
        p3, p3k = qk_p3()
        qk_s3(p3, p3k, *qa_dest(1))
        qa_store(1)
        qk_s1(ppc, ppck)
        w2, w2k = ws.get()
        ppd, ppdk = proj(w2, w2k, 0)
        p2, p2k = qk_p2()
        qk_s2(p2, p2k, gq)
        rl(w8, w8k, 1, 3)
        p3, p3k = qk_p3()
        qk_s3(p3, p3k, *qa_dest(2))
        qa_store(2)
        qk_s1(ppd, ppdk)
        ppe, ppek = proj(w2, w2k, 1)
        p2, p2k = qk_p2()
        qk_s2(p2, p2k, gq)
        wv_, wvk_ = ws.get()
        va(wv_, wvk_)
        p3, p3k = qk_p3()
        qk_s3(p3, p3k, *qa_dest(3))
        qa_store(3)
        qk_s1(ppe, ppek)
        w3, w3k = ws.get()
        gla_a(w3, w3k, 0)
        p2, p2k = qk_p2()
        qk_s2(p2, p2k, gk)
        wl0, wl0k = ws.get()
        wl1, wl1k = ws.get()
        vl(0, wl0, wl0k, wl1, wl1k)
        p3, p3k = qk_p3()
        qk_s3(p3, p3k, KT[:, t0:t0 + 512], [("KT", b)])
        vl(1, wl0, wl0k, wl1, wl1k)
        w4, w4k = ws.get()
        gla_a(w4, w4k, 1)
        gla_b(0)
        vl(2, wl0, wl0k, wl1, wl1k)
        vl(3, wl0, wl0k, wl1, wl1k)
        w5, w5k = ws.get()
        gla_a(w5, w5k, 2)
        gla_b(1)
        w6, w6k = ws.get()
        gla_a(w6, w6k, 3)
        gla_b(2)
        carry.append(lambda: gla_b(3))

    def back_loads(b, oc=True, xt=True):
        t0 = b * 512
        if oc:
            dma(OC[:, 0:4, :], OAS.rearrange("a p t -> p a t")[:, :, t0:t0 + 512], writes=[("OC", 0)])
            dma(OC[:, 4:8, :], OG.rearrange("a p t -> p a t")[:, :, t0:t0 + 512], writes=[("OC", 1)])
        if xt:
            dma(XT, HS.rearrange("kc p t -> p kc t")[:, :, t0:t0 + 512], reads=[("HS", b)], writes=[("XT", kc) for kc in range(8)])

    def back(ws, l, b, hook=None):
        t0 = b * 512
        for m in range(8):
            w, wk = ws.get()
            wv = w[:, 0:1024].rearrange("p (kc c) -> p kc c", kc=8)
            pd, pdk = psn()
            for kc in range(8):
                P.op("pe", lambda e, kc=kc, wv=wv, pd=pd: e.matmul(pd[:], lhsT=wv[:, kc, :], rhs=OC[:, kc, :], start=(kc == 0), stop=(kc == 7)),
                     reads=[wk, ("OC", kc // 4)], writes=[pdk])
            P.op("dve", lambda e, m=m, pd=pd: e.tensor_tensor(out=XT[:, m, :], in0=pd[:], in1=XT[:, m, :], op=ALU.add),
                 reads=[pdk, ("XT", m)], writes=[("XT", m)])
        run_carry()
        if hook is not None:
            hook()
        rmsnorm(2, l, XN)
        ffn(ws)
        rmsnorm(3, l, None)

    def attention(S):
        NB, NCH = S // 512, S // 128
        o2 = O_MAIN
        QTt = A.view(o2, BF16, [128, 2, 512]); o2 += 2 * KiB
        PT = A.view(o2, BF16, [128, 8, 512]); o2 += 8 * KiB
        ACCS = A.view(o2, F32, [128, 2, 512]); o2 += 4 * KiB
        RCP = A.view(o2, F32, [128, 2, 512]); o2 += 4 * KiB
        OAb = A.view(o2, BF16, [128, 2, 512]); o2 += 2 * KiB
        iters = [(qb, c) for qb in range(NB) for c in range(4)]
        steps = [(k, sc) for k in range(len(iters)) for sc in range(NCH)]
        pt_slot = {}
        pstate = {"pti": 0}

        def acc_of(k):
            r = k % 2
            return [(ps[2 * r], ("ps", 2 * r)), (ps[2 * r + 1], ("ps", 2 * r + 1))]

        def st_of(i):
            sp_ = 4 + 2 * (i % 2)
            return [(ps[sp_], ("ps", sp_)), (ps[sp_ + 1], ("ps", sp_ + 1))]

        def emit_mm1(i):
            k, sc = steps[i]
            qb, c = iters[k]
            r = k % 2
            if sc == 0:
                dma(QTt[:, r, :], QS[c][:, qb * 512:qb * 512 + 512], writes=[("QTt", r)])
            st = st_of(i)
            for hf in range(2):
                lo, hi = 64 * hf, 64 * hf + 64
                P.op("pe", lambda e, hf=hf, lo=lo, hi=hi, sc=sc, r=r, st=st: e.matmul(st[hf][0][:], lhsT=KT[lo:hi, sc * 128:(sc + 1) * 128],
                                                                                    rhs=QTt[lo:hi, r, :], start=True, stop=True),
                     reads=["KTall", ("QTt", r)], writes=[st[hf][1]])

        def emit_exp(i):
            st = st_of(i)
            sl = []
            for hf in range(2):
                s_ = pstate["pti"] % 8
                pstate["pti"] += 1
                sl.append(s_)
                P.op("act", lambda e, hf=hf, s_=s_, st=st: e.activation(out=PT[:, s_, :], in_=st[hf][0][:], func=AF.Exp, scale=0.125),
                     reads=[st[hf][1]], writes=[("PT", s_)])
            pt_slot[i] = sl

        def emit_mm2(i):
            k, sc = steps[i]
            acc = acc_of(k)
            for hf in range(2):
                s_ = pt_slot[i][hf]
                P.op("pe", lambda e, hf=hf, s_=s_, sc=sc, acc=acc: e.matmul(acc[hf][0][:], lhsT=VO[:, sc, hf, :], rhs=PT[:, s_, :],
                                                                           start=(sc == 0), stop=(sc == NCH - 1)),
                     reads=["VOall", ("PT", s_)], writes=[acc[hf][1]])

        def fin_a(k):
            acc = acc_of(k)
            for hf in range(2):
                ap_, ak = acc[hf]
                P.op("dve", lambda e, hf=hf, ap_=ap_: e.tensor_copy(out=ACCS[:, hf, :], in_=ap_[:]), reads=[ak], writes=[("ACCS", hf)])
                P.op("dve", lambda e, hf=hf: e.reciprocal(out=RCP[64:128, hf, :], in_=ACCS[64:128, hf, :]), reads=[("ACCS", hf)], writes=[("RCP", hf)])

        def fin_b(k):
            qb, c = iters[k]
            acc = acc_of(k)
            for hf in range(2):
                head = c + 4 * hf
                ap_, ak = acc[hf]
                P.op("pe", lambda e, hf=hf, ap_=ap_: e.matmul(ap_[0:64, :], lhsT=ID32[64:128, 64:128], rhs=RCP[64:128, hf, :], start=True, stop=True),
                     reads=[("RCP", hf), "ID32"], writes=[ak])
                P.op("dve", lambda e, hf=hf, ap_=ap_: e.tensor_tensor(out=OAb[0:64, hf, :], in0=ACCS[0:64, hf, :], in1=ap_[0:64, :], op=ALU.mult),
                     reads=[("ACCS", hf), ak], writes=[("OAb", hf)])
                po_ = (head % 2) * 64
                dma(OAS[head // 2][po_:po_ + 64, qb * 512:qb * 512 + 512], OAb[0:64, hf, :], reads=[("OAb", hf)], writes=["OAS"])

        nst = len(steps)
        emit_mm1(0)
        pending_fin = None
        for i in range(nst):
            k, sc = steps[i]
            if i + 1 < nst:
                emit_mm1(i + 1)
            emit_exp(i)
            emit_mm2(i)
            if pending_fin is not None and sc == min(8, NCH - 1):
                fin_b(pending_fin)
                pending_fin = None
            if sc == NCH - 1:
                fin_a(k)
                pending_fin = k
        if pending_fin is not None:
            fin_b(pending_fin)

    def gla(S, l):
        NB, NCH = S // 512, S // 128
        o2 = O_MAIN
        if 6 * NCH * 128 <= O_DEC:
            DSB = A.view(0, F32, [128, NCH, 128])
            STATE16 = A.view(4 * NCH * 128, BF16, [128, NCH, 128])
        else:
            DSB = A.view(o2, F32, [128, NCH, 128]); o2 += 4 * NCH * 128
            STATE16 = A.view(o2, BF16, [128, NCH, 128]); o2 += 2 * NCH * 128
        GQh = A.view(o2, BF16, [128, S]); o2 += 2 * S
        GKh = A.view(o2, BF16, [128, S]); o2 += 2 * S
        GKTh = A.view(o2, BF16, [128, NCH, 128]); o2 += 2 * S
        GVh = A.view(o2, BF16, [128, NCH, 128]); o2 += 2 * S
        SALL = A.view(o2, F32, [128, NCH, 128]); o2 += 4 * NCH * 128
        AT2 = A.view(o2, BF16, [128, 3, 2, 512]); o2 += 6 * KiB
        OSQ = A.view(o2, BF16, [128, 2, 512]); o2 += 2 * KiB
        SRb = A.view(o2, F32, [128, 3, 512]); o2 += 6 * KiB
        T1 = A.view(o2, F32, [128, 512]); o2 += 2 * KiB
        OGb = A.view(o2, BF16, [128, 2, 512]); o2 += 2 * KiB
        TN2 = A.view(o2, F32, [128, 512]); o2 += 2 * KiB
        RS2 = A.view(o2, F32, [128, 512]); o2 += 2 * KiB
        assert o2 <= ARENA_BYTES, o2
        for h in range(4):
            cst = min(16, NCH)
            for c0 in range(0, NCH, cst):
                dma(GKTh[:, c0:c0 + cst, :], GKT[h][c0:c0 + cst].rearrange("c p n -> p c n"), writes=[("GKTh", c0 // cst)])
                dma(GVh[:, c0:c0 + cst, :], GV[c0:c0 + cst, :, h * 128:(h + 1) * 128].rearrange("c p n -> p c n"), writes=[("GVh", c0 // cst)])
            step = min(2048, S)
            for c0 in range(0, S, step):
                dma(GQh[:, c0:c0 + step], GQ[h][:, c0:c0 + step], writes=[("GQh", c0 // step)])
                dma(GKh[:, c0:c0 + step], GK[h][:, c0:c0 + step], writes=[("GKh", c0 // step)])
            P.op("pool", lambda e: e.memset(SALL[0:64, 0, :], 0.0), writes=[("SA", 0)])
            P.op("pool", lambda e: e.memset(SALL[64:128, NCH - 1, :], 0.0), writes=[("SB", NCH - 1)])
            for cb in range(NB):
                pd, pdk = psn()
                for ci in range(4):
                    c = 4 * cb + ci
                    P.op("pe", lambda e, c=c, ci=ci, pd=pd: e.matmul(pd[:, ci * 128:(ci + 1) * 128], lhsT=GKTh[:, c, :], rhs=GVh[:, c, :], start=True, stop=True),
                         reads=[("GKTh", c // cst), ("GVh", c // cst)], writes=[pdk])
                P.op("act", lambda e, cb=cb, pd=pd: e.activation(out=DSB[:, 4 * cb:4 * cb + 4, :], in_=pd[:].rearrange("p (c n) -> p c n", c=4), func=AF.Copy),
                     reads=[pdk], writes=[("DSB", cb)])

            def fwd_step(c):
                P.op("dve", lambda e, c=c, h=h: e.scalar_tensor_tensor(out=SALL[0:64, c + 1, :], in0=SALL[0:64, c, :], scalar=DEC[0:64, h, c, :],
                                                                     in1=DSB[0:64, c, :], op0=ALU.mult, op1=ALU.add),
                     reads=[("SA", c), ("DSB", c // 4), "DECall"], writes=[("SA", c + 1)])

            def bwd_step(c):
                P.op("dve", lambda e, c=c, h=h: e.scalar_tensor_tensor(out=SALL[64:128, c - 1, :], in0=SALL[64:128, c, :], scalar=DEC[64:128, h, c, :],
                                                                     in1=DSB[64:128, c, :], op0=ALU.mult, op1=ALU.add),
                     reads=[("SB", c), ("DSB", c // 4), "DECall"], writes=[("SB", c - 1)])

            LEAD = min(12, max(1, (NCH - 1) // 2))
            fl = list(range(0, NCH - 1))
            bl = list(range(NCH - 1, 0, -1))
            for k in range(len(fl) + LEAD):
                if k < len(fl):
                    fwd_step(fl[k])
                if 0 <= k - LEAD < len(bl):
                    bwd_step(bl[k - LEAD])
            cst2 = min(16, NCH)
            for c0 in range(0, NCH, cst2):
                en_ = "act"
                rk = [("SA", c) for c in range(c0, c0 + cst2)] + [("SB", c) for c in range(c0, c0 + cst2)]
                wk_ = [("ST16", c) for c in range(c0, c0 + cst2)]
                if en_ == "act":
                    P.op("act", lambda e, c0=c0: e.activation(out=STATE16[:, c0:c0 + cst2, :], in_=SALL[:, c0:c0 + cst2, :], func=AF.Copy), reads=rk, writes=wk_)
                else:
                    P.op("pool", lambda e, c0=c0: e.tensor_copy(out=STATE16[:, c0:c0 + cst2, :], in_=SALL[:, c0:c0 + cst2, :]), reads=rk, writes=wk_)
            m2f = M2[:, 0:128].rearrange("p (o n) -> p o n", o=1).to_broadcast([128, 4, 128])
            m2b = M2[:, 128:256].rearrange("p (o n) -> p o n", o=1).to_broadcast([128, 4, 128])
            v4_ = lambda t: t.rearrange("p (c n) -> p c n", c=4)
            gc = OFF_GLA + l
            pos = {}

            def st3_A(cb):
                paF, pafk = psn()
                paB, pabk = psn()
                r3 = cb % 3
                r2 = cb % 2
                dma(SRb[:, r3, :], GR[h][:, cb * 512:cb * 512 + 512], writes=[("SRb", r3)])
                for ci in range(4):
                    c = 4 * cb + ci
                    cs = slice(c * 128, (c + 1) * 128)
                    osl = slice(ci * 128, (ci + 1) * 128)
                    qk_ = [("GKh", (c * 128) // step), ("GQh", (c * 128) // step)]
                    P.op("pe", lambda e, cs=cs, osl=osl: e.matmul(paF[:, osl], lhsT=GKh[0:64, cs], rhs=GQh[0:64, cs], start=True, stop=True),
                         reads=qk_, writes=[pafk])
                    P.op("pe", lambda e, cs=cs, osl=osl: e.matmul(paB[:, osl], lhsT=GKh[64:128, cs], rhs=GQh[64:128, cs], start=True, stop=True),
                         reads=qk_, writes=[pabk])
                P.op("dve", lambda e: e.tensor_tensor(out=v4_(AT2[:, r3, 0, :]), in0=v4_(paF[:]), in1=m2f, op=ALU.mult),
                     reads=[pafk, "M2"], writes=[("AT2", r3, 0)])
                P.op("dve", lambda e: e.tensor_tensor(out=v4_(AT2[:, r3, 1, :]), in0=v4_(paB[:]), in1=m2b, op=ALU.mult),
                     reads=[pabk, "M2"], writes=[("AT2", r3, 1)])

            def st3_O(cb):
                po, pok = psn()
                pos[cb] = (po, pok)
                r3 = cb % 3
                r2 = cb % 2
                for ci in range(4):
                    c = 4 * cb + ci
                    cs = slice(c * 128, (c + 1) * 128)
                    osl = slice(ci * 128, (ci + 1) * 128)
                    P.op("pe", lambda e, c=c, osl=osl: e.matmul(po[:, osl], lhsT=GVh[:, c, :], rhs=AT2[:, r3, 0, osl], start=True, stop=False),
                         reads=[("GVh", c // cst), ("AT2", r3, 0)], writes=[pok])
                    P.op("pe", lambda e, c=c, osl=osl: e.matmul(po[:, osl], lhsT=GVh[:, c, :], rhs=AT2[:, r3, 1, osl], start=False, stop=False),
                         reads=[("GVh", c // cst), ("AT2", r3, 1)], writes=[pok])
                    P.op("pe", lambda e, c=c, cs=cs, osl=osl: e.matmul(po[:, osl], lhsT=STATE16[:, c, :], rhs=GQh[:, cs], start=False, stop=True),
                         reads=[("ST16", c), ("GQh", (c * 128) // step)], writes=[pok])
                P.op("act", lambda e: e.activation(out=OSQ[:, r2, :], in_=po[:], func=AF.Square), reads=[pok], writes=[("OSQ", r2)])

            def st3_N(cb):
                po, pok = pos.pop(cb)
                r2 = cb % 2
                pn, pnk = psn()
                P.op("pe", lambda e: e.matmul(pn[:], lhsT=ONES16[:], rhs=OSQ[:, r2, :], start=True, stop=True), reads=[("OSQ", r2), "ONES16"], writes=[pnk])
                P.op("act", lambda e: e.activation(out=TN2, in_=pn[:], func=AF.Ln, scale=1.0 / 128, bias=EPSC[:]), reads=[pnk], writes=["TN2"])
                P.op("act", lambda e: e.activation(out=RS2, in_=TN2, func=AF.Exp, scale=-0.5), reads=["TN2"], writes=["RS2"])
                P.op("dve", lambda e: e.scalar_tensor_tensor(out=T1, in0=po[:], scalar=SV[:, gc:gc + 1], in1=RS2, op0=ALU.mult, op1=ALU.mult),
                     reads=[pok, "RS2", "SV"], writes=["T1"])
                r3 = cb % 3
                P.op("dve", lambda e: e.tensor_tensor(out=OGb[:, r2, :], in0=T1, in1=SRb[:, r3, :], op=ALU.mult),
                     reads=["T1", ("SRb", r3)], writes=[("OGb", r2)])
                dma(OG[h][:, cb * 512:cb * 512 + 512], OGb[:, r2, :], reads=[("OGb", r2)], writes=["OG"])

            for stp in range(NB + 2):
                if stp < NB:
                    st3_A(stp)
                if 0 <= stp - 1 < NB:
                    st3_O(stp - 1)
                if 0 <= stp - 2 < NB:
                    st3_N(stp - 2)

    for name, S in seqs:
        NB = S // 512
        P.op("pool", lambda e: e.memset(VO[:, :, :, 64:128], 1.0), writes=["VOones"])
        ws = WS()
        for b in range(NB):
            plan_front(ws, 0)
        load_x_dma(xin[name], 0)
        for b in range(NB):
            load_x(xin[name], b * 512)
            run_carry()
            hk = (lambda b=b: load_x_dma(xin[name], (b + 1) * 512)) if b + 1 < NB else None
            front(ws, 0, b, hook=hk)
        run_carry()
        P.flush()
        if stop_after == "front":
            return nc
        for l in range(L):
            attention(S)
            P.flush()
            if stop_after == "att":
                return nc
            gla(S, l)
            P.flush()
            if stop_after == "gla":
                return nc
            ws = WS()
            for b in range(NB):
                plan_back(ws, l)
                if l + 1 < L:
                    plan_front(ws, l + 1)
            if l + 1 == L:
                pass
            else:
                P.op("pool", lambda e: e.memset(VO[:, :, :, 64:128], 1.0), writes=["VOones"])
            back_loads(0)
            for b in range(NB):
                if l + 1 < L:
                    back(ws, l, b)
                    hk = (lambda b=b: back_loads(b + 1)) if b + 1 < NB else None
                    front(ws, l + 1, b, hook=hk)
                else:
                    hk = (lambda b=b: back_loads(b + 1, oc=True, xt=False)) if b + 1 < NB else None
                    back(ws, l, b, hook=hk)
                    store_y(yout[name], b * 512)
                    if b + 1 < NB:
                        back_loads(b + 1, oc=False, xt=True)
            run_carry()
            P.flush()
    return nc


def make_consts(SMAX):
    ident = np.eye(128, dtype=np.float32)
    rot = np.zeros((128, 128), np.float32)
    for hb in (0, 64):
        for i in range(16):
            rot[hb + 16 + i, hb + i] = -1.0
            rot[hb + i, hb + 16 + i] = 1.0
            rot[hb + 48 + i, hb + 32 + i] = -1.0
            rot[hb + 32 + i, hb + 48 + i] = 1.0
    blk = np.zeros((128, 128), np.float32)
    blk[0:64, 0:64] = 1.0
    blk[64:128, 64:128] = 1.0
    j = np.arange(128)[:, None]
    i = np.arange(128)[None, :]
    m2 = np.concatenate([(j <= i), (j > i)], axis=1).astype(np.float32)
    reset = np.ones((128, 512), np.float32)
    reset[:, ::128] = 0.0
    t = np.arange(SMAX)
    row = (t // 64).astype(np.float32)
    col = (t % 64).astype(np.float32)
    inv_freq = (1.0 / (np.float32(10000.0) ** (np.arange(0, 32, 2, dtype=np.float32) / np.float32(32)))).astype(np.float32)
    ang_r = row[:, None] * inv_freq[None, :]
    ang_c = col[:, None] * inv_freq[None, :]
    ang = np.concatenate([ang_r, ang_r, ang_c, ang_c], axis=-1).astype(np.float32)
    cos = np.cos(ang).astype(np.float32).T
    sin = np.sin(ang).astype(np.float32).T
    cosT = np.ascontiguousarray(np.concatenate([cos, cos], axis=0))
    sinT = np.ascontiguousarray(np.concatenate([sin, sin], axis=0))
    return {"c_ident": ident, "c_rot": rot, "c_blk": blk, "c_m2": m2, "c_reset": reset, "c_cos": cosT, "c_sin": sinT}


WNAMES = ["norm_ffn1", "w_ffn1_gu", "w_ffn1_down", "norm_mix", "w_in", "q_norm", "k_norm", "w_gate_f", "b_gate_f",
          "w_gate_b", "b_gate_b", "gla_norm", "w_out", "norm_ffn2", "w_ffn2_gu", "w_ffn2_down", "norm_out"]


def kernel(**inputs):
    xp = np.asarray(inputs["x_prompt"], dtype=np.float32)
    xs = np.asarray(inputs["x_sample"], dtype=np.float32)
    nb = xp.shape[0]
    SP, SS = xp.shape[1], xs.shape[1]
    depth = np.asarray(inputs["w_in"]).shape[0]
    nc = build([("p", SP), ("s", SS)], depth=depth)
    consts = make_consts(max(SP, SS))
    wts = {k: np.ascontiguousarray(np.asarray(inputs[k], dtype=np.float32)) for k in WNAMES}
    in_maps = []
    for i in range(nb):
        m = {"x_p": np.ascontiguousarray(xp[i]), "x_s": np.ascontiguousarray(xs[i])}
        m.update(wts)
        m.update(consts)
        in_maps.append(m)
    res = run_bass_kernel_spmd(nc, in_maps, core_ids=list(range(nb)))
    yp = np.stack([np.asarray(r["y_p"], dtype=np.float32) for r in res.results], axis=0)
    ys = np.stack([np.asarray(r["y_s"], dtype=np.float32) for r in res.results], axis=0)
    return (yp, ys)
```

```python
import numpy as np
import concourse.bass as bass
import concourse.mybir as mybir
from concourse.bass_utils import run_bass_kernel_spmd

F32 = mybir.dt.float32
BF16 = mybir.dt.bfloat16
AF = mybir.ActivationFunctionType
ALU = mybir.AluOpType

D = 1024
KC = 8
DFF = 2816
JC = 22
DIN = 2336
EPS = 1e-6
TB = 512
NQ = 12


class Op:
    __slots__ = ("eng", "fn", "dma", "deps", "signal", "sem", "val", "prev_val")

    def __init__(self, eng, fn, dma):
        self.eng = eng
        self.fn = fn
        self.dma = dma
        self.deps = []
        self.signal = False
        self.sem = None
        self.val = 0
        self.prev_val = 0


class Prog:
    ENGS = ["pe", "act", "dve", "pool", "sp"]

    def __init__(self, nc):
        self.nc = nc
        self.sem = {e: nc.alloc_semaphore("s_" + e) for e in ["pe", "act", "dve", "pool"]}
        self.cnt = {e: 0 for e in self.sem}
        self.dsem = {q: [nc.alloc_semaphore("d_%s_%d" % (q, i)) for i in range(NQ)] for q in ["sp", "act"]}
        self.dcnt = {q: 0 for q in self.dsem}
        self.dlast = {q: [0] * NQ for q in self.dsem}
        self.waited = {}
        self.reset()
        allsems = list(self.sem.values()) + [s for q in self.dsem for s in self.dsem[q]]
        with nc.Block() as block:
            def clr(e):
                for s in allsems:
                    e.sem_clear(s)
            block.sync(clr)

    def reset(self):
        self.ops = []
        self.writers = {}
        self.readers = {}

    def op(self, eng, fn, reads=(), writes=(), dma=False):
        o = Op(eng, fn, dma)
        idx = len(self.ops)
        deps = {}
        for k in reads:
            for w in self.writers.get(k, ()):
                deps[w] = "RAW"
        for k in writes:
            rs = self.readers.get(k, ())
            for w in self.writers.get(k, ()):
                deps.setdefault(w, "WAW")
            for r in rs:
                deps.setdefault(r, "WAR")
        deps.pop(idx, None)
        o.deps = list(deps.items())
        def keep(lst):
            if dma:
                return lst
            return [i for i in lst if self.ops[i].dma or self.ops[i].eng != eng]
        for k in reads:
            self.readers[k] = keep(self.readers.get(k, [])) + [idx]
        for k in writes:
            if self.readers.get(k):
                self.writers[k] = [idx]
                self.readers[k] = []
            else:
                self.writers[k] = keep(self.writers.get(k, [])) + [idx]
        self.ops.append(o)
        return o

    def _needs_sem(self, p, o, kind):
        if p.dma:
            return True
        if p.eng != o.eng or o.dma:
            return True
        if p.eng == "pe":
            return False
        return True

    def flush(self, name=None):
        nc = self.nc
        ops = self.ops
        if not ops:
            return
        for o in ops:
            for d, kind in o.deps:
                p = ops[d]
                if not p.dma and self._needs_sem(p, o, kind):
                    p.signal = True
        for o in ops:
            if o.dma:
                q = o.eng
                k = self.dcnt[q]
                self.dcnt[q] += 1
                o.sem = (q, k % NQ)
                o.val = 16 * (k // NQ + 1)
                o.prev_val = 16 * (k // NQ)
                self.dlast[q][k % NQ] = o.val
            elif o.signal:
                self.cnt[o.eng] += 1
                o.sem = o.eng
                o.val = self.cnt[o.eng]

        def semobj(key):
            if isinstance(key, tuple):
                return self.dsem[key[0]][key[1]]
            return self.sem[key]

        bname = {"pe": "tensor", "act": "scalar", "dve": "vector", "pool": "gpsimd", "sp": "sync"}
        with nc.Block() as block:
            for ename in self.ENGS:
                elist = [o for o in ops if o.eng == ename]

                def body(e, elist=elist, ename=ename):
                    for o in elist:
                        waits = {}
                        for d, kind in o.deps:
                            p = ops[d]
                            if p.sem is None or not self._needs_sem(p, o, kind):
                                continue
                            if waits.get(p.sem, 0) < p.val:
                                waits[p.sem] = p.val
                        if o.dma and o.prev_val > 0:
                            if waits.get(o.sem, 0) < o.prev_val:
                                waits[o.sem] = o.prev_val
                        for sk, val in waits.items():
                            if self.waited.get((ename, sk), 0) < val:
                                e.wait_ge(semobj(sk), val)
                                self.waited[(ename, sk)] = val
                        ins = o.fn(e)
                        if o.sem is not None:
                            ins.then_inc(semobj(o.sem), 16 if o.dma else 1)
                    if ename in self.dsem:
                        for i in range(NQ):
                            v = self.dlast[ename][i]
                            if v > 0 and self.waited.get((ename, (ename, i)), 0) < v:
                                e.wait_ge(self.dsem[ename][i], v)
                                self.waited[(ename, (ename, i))] = v

                getattr(block, bname[ename])(body)
        self.reset()


def _prod(s):
    r = 1
    for v in s:
        r *= v
    return r


_RE = {2: None, 3: "p (a b) -> p a b", 4: "p (a b c) -> p a b c"}


class Arena:
    def __init__(self, nc, nbytes):
        self.n = nbytes
        self.t = nc.alloc_sbuf_tensor("arena", [128, nbytes // 2], BF16)

    def view(self, off, dtype, shape):
        n = _prod(shape[1:])
        esz = 4 if dtype == F32 else 2
        assert off % 4 == 0 and off + n * esz <= self.n, (off, n * esz, self.n)
        ap = self.t[0:shape[0], off // 2: off // 2 + n * esz // 2]
        if dtype != BF16:
            ap = ap.bitcast(dtype)
        if len(shape) == 3:
            ap = ap.rearrange("p (a b) -> p a b", a=shape[1])
        elif len(shape) == 4:
            ap = ap.rearrange("p (a b c) -> p a b c", a=shape[1], b=shape[2])
        return ap


def build(seqs, depth=2, dbg=False, stop_after=None):
    nc = bass.Bass("TRN2", target_bir_lowering=False)
    P = Prog(nc)
    L = depth
    SMAX = max(S for _, S in seqs)
    NCHMAX = SMAX // 128
    K = 1024

    def din(name, shape, dt=F32):
        return nc.dram_tensor(name, list(shape), dt, kind="ExternalInput").ap()

    def dscr(name, shape, dt):
        kind = "ExternalOutput" if dbg else "Internal"
        return nc.dram_tensor(name, list(shape), dt, kind=kind).ap()

    xin = {n: din("x_" + n, [S, D]) for n, S in seqs}
    yout = {n: nc.dram_tensor("y_" + n, [S, D], F32, kind="ExternalOutput").ap() for n, S in seqs}
    W = {}
    for nm, shp in [("norm_ffn1", [L, D]), ("w_ffn1_gu", [L, D, 2 * DFF]), ("w_ffn1_down", [L, DFF, D]),
                    ("norm_mix", [L, D]), ("w_in", [L, D, DIN]), ("q_norm", [L, 64]), ("k_norm", [L, 64]),
                    ("w_gate_f", [L, 16, 256]), ("b_gate_f", [L, 256]), ("w_gate_b", [L, 16, 256]),
                    ("b_gate_b", [L, 256]), ("gla_norm", [L, 128]), ("w_out", [L, D, D]),
                    ("norm_ffn2", [L, D]), ("w_ffn2_gu", [L, D, 2 * DFF]), ("w_ffn2_down", [L, DFF, D]),
                    ("norm_out", [L, D])]:
        W[nm] = din(nm, shp)
    c_ident = din("c_ident", [128, 128])
    c_rot = din("c_rot", [128, 128])
    c_blk = din("c_blk", [128, 128])
    c_m2 = din("c_m2", [128, 256])
    c_reset = din("c_reset", [128, 512])
    c_cos = din("c_cos", [128, SMAX])
    c_sin = din("c_sin", [128, SMAX])

    WGU = [[dscr("WGU%d%d" % (l, f), [JC, 128, 2 * 8 * 128], BF16) for f in range(2)] for l in range(L)]
    WDN = [[dscr("WDN%d%d" % (l, f), [8, 128, 22 * 128], BF16) for f in range(2)] for l in range(L)]
    WINL = [dscr("WINL%d" % l, [18, 128, 8 * 128], BF16) for l in range(L)]
    WINR = [dscr("WINR%d" % l, [128, 8 * 640], BF16) for l in range(L)]
    WOUT = [dscr("WOUT%d" % l, [8, 128, 8 * 128], BF16) for l in range(L)]
    HS = dscr("HS", [8, 128, SMAX], F32)
    QS = dscr("QS", [4, 128, SMAX], BF16)
    GQ = dscr("GQ", [4, 128, SMAX], BF16)
    GK = dscr("GK", [4, 128, SMAX], BF16)
    GKT = dscr("GKT", [4, NCHMAX, 128, 128], BF16)
    GV = dscr("GV", [NCHMAX, 128, 512], BF16)
    GR = dscr("GR", [4, 128, SMAX], F32)
    OAS = dscr("OAS", [4, 128, SMAX], BF16)
    OG = dscr("OG", [4, 128, SMAX], BF16)

    sb = nc.alloc_sbuf_tensor
    ID32 = sb("ID32", [128, 128], F32)
    ID16 = sb("ID16", [128, 128], BF16)
    ROT32 = sb("ROT32", [128, 128], F32)
    ONES16 = sb("ONES16", [128, 128], BF16)
    BLK16 = sb("BLK16", [128, 128], BF16)
    M2 = sb("M2", [128, 256], F32)
    RESET = sb("RESET", [128, 512], F32)
    SV = sb("SV", [128, 128], F32)
    WG16 = sb("WG16", [32, L * 4 * 128], BF16)
    EPSC = sb("EPSC", [128, 1], F32)
    ps = [nc.alloc_psum_tensor("ps%d" % i, [128, 512], F32) for i in range(8)]
    ARENA_BYTES = 192 * 1024
    A = Arena(nc, ARENA_BYTES)

    OFF_N, OFF_QK, OFF_GLA, OFF_BG = 0, 32 * L, 34 * L, 35 * L

    def gcol(which, l, kc):
        c = OFF_N + which * L * 8 + l * 8 + kc
        return SV[:, c:c + 1]

    state = {"ps": 0}

    def psn():
        i = state["ps"] % 8
        state["ps"] += 1
        return ps[i], ("ps", i)

    def dma(out, in_, reads=(), writes=(), q="sp"):
        return P.op(q, lambda e: e.dma_start(out=out, in_=in_), reads=reads, writes=writes, dma=True)

    KiB = 1024

    def prep():
        stg32 = [A.view(i * 66 * KiB, F32, [128, 11264]) for i in range(2)]
        stg16 = [A.view(i * 66 * KiB + 44 * KiB, BF16, [128, 11264]) for i in range(2)]
        small = A.view(132 * KiB, F32, [128, 128])
        wg32 = A.view(133 * KiB, F32, [32, L * 4 * 128])
        tmpc = A.view(140 * KiB, F32, [128, 128])
        dma(ID32[:], c_ident, writes=["ID32"])
        dma(ROT32[:], c_rot, writes=["ROT32"])
        dma(M2[:], c_m2, writes=["M2"])
        dma(RESET[:], c_reset, writes=["RESET"])
        dma(tmpc, c_blk, writes=["tmpc"])
        P.op("dve", lambda e: e.tensor_copy(out=BLK16[:], in_=tmpc), reads=["tmpc"], writes=["BLK16"])
        P.op("dve", lambda e: e.tensor_copy(out=ID16[:], in_=ID32[:]), reads=["ID32"], writes=["ID16"])
        P.op("pool", lambda e: e.memset(ONES16[:], 1.0), writes=["ONES16"])
        P.op("pool", lambda e: e.memset(EPSC[:], EPS), writes=["EPSC"])
        P.op("pool", lambda e: e.memset(small, 0.0), writes=["small"])
        for wi, nm in enumerate(["norm_ffn1", "norm_mix", "norm_ffn2", "norm_out"]):
            r0 = OFF_N + wi * L * 8
            dma(small[r0:r0 + L * 8, :], W[nm].rearrange("l (kc p) -> (l kc) p", p=128), reads=["small0"], writes=["small"])
        for wi, nm in enumerate(["q_norm", "k_norm"]):
            r0 = OFF_QK + wi * L
            dma(small[r0:r0 + L, 0:64], W[nm], reads=["small0"], writes=["small"])
            dma(small[r0:r0 + L, 64:128], W[nm], reads=["small0"], writes=["small"])
        dma(small[OFF_GLA:OFF_GLA + L, :], W["gla_norm"], reads=["small0"], writes=["small"])
        dma(small[OFF_BG:OFF_BG + 4 * L, 0:64], W["b_gate_f"].rearrange("l (h n) -> (l h) n", n=64), reads=["small0"], writes=["small"])
        dma(small[OFF_BG:OFF_BG + 4 * L, 64:128], W["b_gate_b"].rearrange("l (h n) -> (l h) n", n=64), reads=["small0"], writes=["small"])
        pt, pk = psn()
        P.op("pe", lambda e: e.transpose(out=pt[:, 0:128], in_=small, identity=ID32[:]), reads=["small", "ID32"], writes=[pk])
        P.op("dve", lambda e: e.tensor_copy(out=SV[:], in_=pt[:, 0:128]), reads=[pk], writes=["SV"])
        P.op("dve", lambda e: e.tensor_scalar(out=SV[:, OFF_BG:OFF_BG + 4 * L], in0=SV[:, OFF_BG:OFF_BG + 4 * L],
                                              scalar1=-1.0, scalar2=None, op0=ALU.mult), reads=["SV"], writes=["SV"])
        P.op("pool", lambda e: e.memset(wg32, 0.0), writes=["wg32"])
        wgv = wg32.rearrange("p (l h n) -> p l h n", l=L, h=4)
        for l in range(L):
            dma(wgv[0:16, l, :, 0:64], W["w_gate_f"][l].rearrange("r (h n) -> r h n", n=64), reads=["wg320"], writes=["wg32"])
            dma(wgv[16:32, l, :, 64:128], W["w_gate_b"][l].rearrange("r (h n) -> r h n", n=64), reads=["wg320"], writes=["wg32"])
        P.op("dve", lambda e: e.tensor_copy(out=WG16[:], in_=wg32), reads=["wg32"], writes=["WG16"])

        st = {"i": 0}
        engs = ["dve", "act"]
        pend = []

        def cast(out, in_, r, w):
            en = engs[st["i"] % 2]
            st["i"] += 1
            if en == "act":
                P.op("act", lambda e: e.activation(out=out, in_=in_, func=AF.Copy), reads=r, writes=w)
            else:
                P.op(en, lambda e: e.tensor_copy(out=out, in_=in_), reads=r, writes=w)

        pi = {"i": 0}

        def plain(src, nkc, c0, ncols, dst_fn):
            i = pi["i"] % 2
            pi["i"] += 1
            s32 = stg32[i][:, 0:nkc * ncols].rearrange("p (kc n) -> p kc n", kc=nkc)
            nm = ncols // 128
            half = (nkc + 1) // 2
            srcv = src[:, c0:c0 + ncols].rearrange("(kc p) n -> p kc n", p=128)
            dma(s32[:, 0:half, :], srcv[:, 0:half, :], writes=[("s32", i)])
            dma(s32[:, half:nkc, :], srcv[:, half:nkc, :], writes=[("s32", i)])
            while pend:
                pend.pop(0)()
            s16 = stg16[i][:, 0:nkc * ncols].rearrange("p (m kc c) -> p m kc c", m=nm, kc=nkc)
            s32p = stg32[i][:, 0:nkc * ncols].rearrange("p (kc m c) -> p m kc c", kc=nkc, m=nm)
            step = max(1, nm // 3)
            for m0 in range(0, nm, step):
                m1 = min(nm, m0 + step)
                cast(s16[:, m0:m1], s32p[:, m0:m1], [("s32", i)], [("s16", i)])
            s16v_ = stg16[i][:, 0:nkc * ncols].rearrange("p (m n) -> p m n", m=nm)
            pend.append(lambda: dst_fn(s16v_, ("s16", i)))

        for l in range(L):
            for f in range(2):
                src = W["w_ffn%d_gu" % (f + 1)][l]
                for t in range(2):
                    for jg in range(2):
                        def dst_fn(s16v, key, l=l, f=f, t=t, jg=jg):
                            dv = WGU[l][f][11 * jg:11 * jg + 11].rearrange("j p (t n) -> p j t n", t=2)[:, :, t, :]
                            dma(dv, s16v, reads=[key], writes=["WGU"])
                        plain(src, 8, t * DFF + jg * 1408, 1408, dst_fn)
                src = W["w_ffn%d_down" % (f + 1)][l]
                for mh in range(2):
                    def dst_fn(s16v, key, l=l, f=f, mh=mh):
                        dv = WDN[l][f][4 * mh:4 * mh + 4].rearrange("m p n -> p m n")
                        dma(dv, s16v, reads=[key], writes=["WDN"])
                    plain(src, 22, mh * 512, 512, dst_fn)
            def dst_fn(s16v, key, l=l):
                dma(WOUT[l].rearrange("m p n -> p m n"), s16v, reads=[key], writes=["WOUT"])
            plain(W["w_out"][l], 8, 0, 1024, dst_fn)
            src = W["w_in"][l]
            i = pi["i"] % 2
            pi["i"] += 1
            sA = stg32[i][:, 0:8 * 1280].rearrange("p (kc n) -> p kc n", kc=8)
            srcv = src[:, 0:1280].rearrange("(kc p) n -> p kc n", p=128)
            dma(sA[:, 0:4, :], srcv[:, 0:4, :], writes=[("s32", i)])
            dma(sA[:, 4:8, :], srcv[:, 4:8, :], writes=[("s32", i)])
            while pend:
                pend.pop(0)()
            T16 = A.view(152 * KiB, BF16, [128, 18 * 1024]).rearrange("p (t kc c) -> p t kc c", t=18, kc=8)
            k32, k16 = [("s32", i)], ["T16"]
            cast(T16[:, 1:5, :, 0:64], sA[:, :, 0:256].rearrange("p kc (c n) -> p c kc n", n=64), k32, k16)
            cast(T16[:, 1:5, :, 64:128], sA[:, :, 256:512].rearrange("p kc (c n) -> p c kc n", n=64), k32, k16)
            cast(T16[:, 5, :, :], sA[:, :, 512:640], k32, k16)
            for h in range(4):
                for half in range(2):
                    cast(T16[:, 6 + 2 * h, :, 64 * half:64 * half + 64], sA[:, :, 768 + 64 * h:768 + 64 * h + 64], k32, k16)
                    cast(T16[:, 7 + 2 * h, :, 64 * half:64 * half + 64], sA[:, :, 1024 + 64 * h:1024 + 64 * h + 64], k32, k16)
            R16 = A.view(141 * KiB, BF16, [128, 8, 640])
            cast(R16[:, :, 0:128], sA[:, :, 640:768], k32, ["R16"])
            j = pi["i"] % 2
            pi["i"] += 1
            sB = stg32[j][:, 0:8 * 1056].rearrange("p (kc n) -> p kc n", kc=8)
            srcv = src[:, 1280:2336].rearrange("(kc p) n -> p kc n", p=128)
            dma(sB[:, 0:4, :], srcv[:, 0:4, :], writes=[("s32", j)])
            dma(sB[:, 4:8, :], srcv[:, 4:8, :], writes=[("s32", j)])
            kb = [("s32", j)]
            cast(R16[:, :, 128:640], sB[:, :, 0:512], kb, ["R16"])
            cast(T16[:, 14:18, :, :], sB[:, :, 512:1024].rearrange("p kc (h n) -> p h kc n", n=128), kb, k16)
            P.op("pool", lambda e, T16=T16: e.memset(T16[:, 0, :, 32:128], 0.0), writes=k16)
            cast(T16[:, 0, :, 0:32], sB[:, :, 1024:1056], kb, k16)
            pend.append(lambda l=l, k16=k16: dma(WINL[l].rearrange("t p n -> p t n"), A.view(152 * KiB, BF16, [128, 18, 1024]),
                                                 reads=k16, writes=["WINL"]))
            pend.append(lambda l=l, R16=R16: dma(WINR[l], R16.rearrange("p kc n -> p (kc n)"), reads=["R16"], writes=["WINR"]))
        while pend:
            pend.pop(0)()
        P.flush()

    prep()
    if stop_after == "prep":
        return nc

    o = 0
    def take(nbytes):
        nonlocal o
        r = o
        o += nbytes
        return r
    O_KT = take(2 * SMAX)
    O_VO = take(NCHMAX * 2 * 128 * 2)
    O_DEC = take(4 * NCHMAX * 4)
    O_MAIN = o
    KT = A.view(O_KT, BF16, [128, SMAX])
    VO = A.view(O_VO, BF16, [128, NCHMAX, 2, 128])
    DEC = A.view(O_DEC, F32, [128, 4, NCHMAX, 1])
    XT = A.view(take(16 * KiB), F32, [128, 8, 512])
    XN = A.view(take(8 * KiB), BF16, [128, 8, 512])
    O_BIG = take(22 * KiB)
    ACTT = A.view(O_BIG, BF16, [128, 22, 512])
    XTOK = A.view(O_BIG, F32, [128, 4, 1024])
    SQ = A.view(O_BIG + 16 * KiB, BF16, [128, 4, 512])
    RS = A.view(take(2 * KiB), F32, [128, 512])
    TMPN = A.view(take(2 * KiB), F32, [128, 512])
    SG = A.view(take(4 * KiB), F32, [128, 2, 512])
    NW = 8
    WR = [A.view(take(4 * KiB), BF16, [128, 2048]) for _ in range(NW)]
    OC = A.view(take(8 * KiB), BF16, [128, 8, 512])
    CS = A.view(take(4 * KiB), F32, [128, 2, 512])
    U32 = A.view(take(2 * KiB), F32, [128, 512])
    QN32 = A.view(take(2 * KiB), F32, [128, 512])
    T132 = A.view(take(2 * KiB), F32, [128, 512])
    T232 = A.view(take(2 * KiB), F32, [128, 512])
    QF = A.view(take(2 * KiB), BF16, [128, 2, 512])
    SQ2 = A.view(take(1 * KiB), BF16, [128, 512])
    GLOW = A.view(take(1 * KiB), BF16, [128, 512])
    L32 = A.view(take(2 * KiB), F32, [128, 512])
    PP32 = A.view(take(2 * KiB), F32, [128, 512])
    Z32 = A.view(take(2 * KiB), F32, [128, 512])
    TZ32 = A.view(take(2 * KiB), F32, [128, 512])
    EQ32 = A.view(take(2 * KiB), F32, [128, 512])
    EK32 = A.view(take(2 * KiB), F32, [128, 512])
    QD16 = A.view(take(2 * KiB), BF16, [128, 2, 512])
    KD16 = A.view(take(2 * KiB), BF16, [128, 2, 512])
    KDS16 = A.view(take(2 * KiB), BF16, [128, 2, 512])
    KDT16 = A.view(take(2 * KiB), BF16, [128, 2, 512])
    SR32 = A.view(take(4 * KiB), F32, [128, 2, 512])
    VLT = A.view(take(4 * KiB), BF16, [128, 4, 512])
    assert o <= ARENA_BYTES, o
    O_END_MAIN = o

    carry = []

    def run_carry():
        while carry:
            carry.pop(0)()

    def bigkeys(j0, j1):
        return [("big", j) for j in range(j0, j1)]

    class WS:
        def __init__(self):
            self.plan = []
            self.loaded = 0
            self.next = 0

        def add(self, ap2d, n):
            self.plan.append((ap2d, n))

        def _load(self, i):
            ap2d, n = self.plan[i]
            s = i % NW
            dma(WR[s][:, 0:n], ap2d, writes=[("wr", s)])

        def get(self):
            i = self.next
            self.next += 1
            while self.loaded < min(len(self.plan), i + 4):
                self._load(self.loaded)
                self.loaded += 1
            assert i < self.loaded
            s = i % NW
            return WR[s], ("wr", s)

    def plan_ffn(ws, l, f):
        for j in range(JC):
            ws.add(WGU[l][f][j], 2048)
        for m in range(8):
            ws.add(WDN[l][f][m][:, 0:1408], 1408)
            ws.add(WDN[l][f][m][:, 1408:2816], 1408)

    def plan_front(ws, l):
        plan_ffn(ws, l, 0)
        def pl_(pc):
            ws.add(WINL[l][2 * pc:2 * pc + 2].rearrange("t p n -> p t n"), 2048)
        wr_ = WINR[l].rearrange("p (kc n) -> p kc n", kc=8)
        pl_(0); pl_(7); pl_(1); pl_(8); pl_(2)
        ws.add(wr_[:, :, 0:128], 1024)
        pl_(3)
        ws.add(wr_[:, 0:4, 128:640], 2048)
        ws.add(wr_[:, 4:8, 128:640], 2048)
        pl_(4); pl_(5); pl_(6)

    def plan_back(ws, l):
        for m in range(8):
            ws.add(WOUT[l][m], 1024)
        plan_ffn(ws, l, 1)

    def _load(self, i):
        ap, n = self.plan[i]
        s = i % NW
        dst = WR[s][:, 0:n]
        if len(ap.shape) == 3:
            dst = dst.rearrange("p (a b) -> p a b", a=ap.shape[1])
        dma(dst, ap, writes=[("wr", s)])
    WS._load = _load

    def rmsnorm(which, l, dst16=None):
        pn, pk = psn()
        for kc in range(8):
            if kc % 2 == 0:
                P.op("act", lambda e, kc=kc: e.activation(out=XN[:, kc, :], in_=XT[:, kc, :], func=AF.Square),
                     reads=[("XT", kc)], writes=[("XN", kc)])
            else:
                P.op("dve", lambda e, kc=kc: e.tensor_tensor(out=XN[:, kc, :], in0=XT[:, kc, :], in1=XT[:, kc, :], op=ALU.mult),
                     reads=[("XT", kc)], writes=[("XN", kc)])
        for kc in range(8):
            P.op("pe", lambda e, kc=kc: e.matmul(pn[:], lhsT=ONES16[:], rhs=XN[:, kc, :], start=(kc == 0), stop=(kc == 7)),
                 reads=[("XN", kc), "ONES16"], writes=[pk])
        P.op("act", lambda e: e.activation(out=TMPN, in_=pn[:], func=AF.Ln, scale=1.0 / D, bias=EPSC[:]), reads=[pk], writes=["TMPN"])
        P.op("act", lambda e: e.activation(out=RS, in_=TMPN, func=AF.Exp, scale=-0.5), reads=["TMPN"], writes=["RS"])
        for kc in range(8):
            if dst16 is not None:
                P.op("dve", lambda e, kc=kc: e.scalar_tensor_tensor(out=XN[:, kc, :], in0=XT[:, kc, :], scalar=gcol(which, l, kc),
                                                                    in1=RS, op0=ALU.mult, op1=ALU.mult),
                     reads=[("XT", kc), "RS", "SV"], writes=[("XN", kc)])
            else:
                P.op("dve", lambda e, kc=kc: e.scalar_tensor_tensor(out=XT[:, kc, :], in0=XT[:, kc, :], scalar=gcol(which, l, kc),
                                                                    in1=RS, op0=ALU.mult, op1=ALU.mult),
                     reads=[("XT", kc), "RS", "SV"], writes=[("XT", kc)])

    def ffn(ws):
        for j in range(JC):
            w, wk = ws.get()
            wv = w[:, 0:2048].rearrange("p (t kc c) -> p t kc c", t=2, kc=8)
            pg, pgk = psn()
            pu, puk = psn()
            for kc in range(8):
                P.op("pe", lambda e, kc=kc, wv=wv, pg=pg: e.matmul(pg[:], lhsT=wv[:, 0, kc, :], rhs=XN[:, kc, :], start=(kc == 0), stop=(kc == 7)),
                     reads=[wk, ("XN", kc)], writes=[pgk])
            for kc in range(8):
                P.op("pe", lambda e, kc=kc, wv=wv, pu=pu: e.matmul(pu[:], lhsT=wv[:, 1, kc, :], rhs=XN[:, kc, :], start=(kc == 0), stop=(kc == 7)),
                     reads=[wk, ("XN", kc)], writes=[puk])
            r = j % 2
            P.op("act", lambda e, r=r, pg=pg: e.activation(out=SG[:, r, :], in_=pg[:], func=AF.Silu), reads=[pgk], writes=[("SG", r)])
            P.op("dve", lambda e, r=r, pu=pu, j=j: e.tensor_tensor(out=ACTT[:, j, :], in0=SG[:, r, :], in1=pu[:], op=ALU.mult),
                 reads=[("SG", r), puk], writes=[("big", j)])
        for m in range(8):
            wa, wak = ws.get()
            wb, wbk = ws.get()
            pd, pdk = psn()
            for kc in range(22):
                wsrc = wa if kc < 11 else wb
                wkk = wak if kc < 11 else wbk
                kk = kc if kc < 11 else kc - 11
                P.op("pe", lambda e, kc=kc, kk=kk, wsrc=wsrc, pd=pd: e.matmul(pd[:], lhsT=wsrc[:, kk * 128:(kk + 1) * 128], rhs=ACTT[:, kc, :],
                                                                             start=(kc == 0), stop=(kc == 21)),
                     reads=[wkk, ("big", kc)], writes=[pdk])
            P.op("dve", lambda e, m=m, pd=pd: e.scalar_tensor_tensor(out=XT[:, m, :], in0=pd[:], scalar=0.5, in1=XT[:, m, :],
                                                                     op0=ALU.mult, op1=ALU.add),
                 reads=[pdk, ("XT", m)], writes=[("XT", m)])

    def load_x_dma(x_ap, t0):
        xv = x_ap[t0:t0 + 512, :].rearrange("(g p) n -> p g n", p=128)
        dma(XTOK[:, 0:2, :], xv[:, 0:2, :], writes=bigkeys(0, 8))
        dma(XTOK[:, 2:4, :], xv[:, 2:4, :], writes=bigkeys(8, 16))

    def load_x(x_ap, t0):
        for kc in range(8):
            pt, pk = psn()
            for g in range(4):
                P.op("pe", lambda e, g=g, kc=kc, pt=pt: e.transpose(out=pt[:, g * 128:(g + 1) * 128], in_=XTOK[:, g, kc * 128:(kc + 1) * 128], identity=ID32[:]),
                     reads=bigkeys(4 * g, 4 * g + 4) + ["ID32"], writes=[pk])
            if kc % 2 == 0:
                P.op("act", lambda e, kc=kc, pt=pt: e.activation(out=XT[:, kc, :], in_=pt[:], func=AF.Copy), reads=[pk], writes=[("XT", kc)])
            else:
                P.op("dve", lambda e, kc=kc, pt=pt: e.tensor_copy(out=XT[:, kc, :], in_=pt[:]), reads=[pk], writes=[("XT", kc)])

    def store_y(y_ap, t0):
        yv = y_ap[t0:t0 + 512, :].rearrange("(g p) n -> p g n", p=128)
        for g in range(4):
            for hh in range(2):
                pt, pk = psn()
                for k4 in range(4):
                    kc = hh * 4 + k4
                    P.op("pe", lambda e, g=g, kc=kc, k4=k4, pt=pt: e.transpose(out=pt[:, k4 * 128:(k4 + 1) * 128], in_=XT[:, kc, g * 128:(g + 1) * 128], identity=ID32[:]),
                         reads=[("XT", kc), "ID32"], writes=[pk])
                if hh == 0:
                    P.op("act", lambda e, g=g, pt=pt: e.activation(out=XTOK[:, g, 0:512], in_=pt[:], func=AF.Copy), reads=[pk], writes=bigkeys(4 * g, 4 * g + 2))
                else:
                    P.op("dve", lambda e, g=g, pt=pt: e.tensor_copy(out=XTOK[:, g, 512:1024], in_=pt[:]), reads=[pk], writes=bigkeys(4 * g + 2, 4 * g + 4))
        dma(yv[:, 0:2, :], XTOK[:, 0:2, :], reads=bigkeys(0, 8), writes=["Y"])
        dma(yv[:, 2:4, :], XTOK[:, 2:4, :], reads=bigkeys(8, 16), writes=["Y"])

    def front(ws, l, b, hook=None):
        t0 = b * 512
        rmsnorm(0, l, XN)
        ffn(ws)
        dma(HS.rearrange("kc p t -> p kc t")[:, :, t0:t0 + 512], XT, reads=[("XT", kc) for kc in range(8)], writes=[("HS", b)])
        rmsnorm(1, l, XN)
        if hook is not None:
            hook()
        dma(CS[:, 0, :], c_cos[:, t0:t0 + 512], writes=["CS"])
        dma(CS[:, 1, :], c_sin[:, t0:t0 + 512], writes=["CS"])
        xnk = [("XN", kc) for kc in range(8)]

        def proj(w, wk, ti, M=128):
            pp, ppk = psn()
            wv = w[:, 0:2048].rearrange("p (t kc c) -> p t kc c", t=2, kc=8)
            for kc in range(8):
                P.op("pe", lambda e, kc=kc: e.matmul(pp[0:M, :], lhsT=wv[:, ti, kc, 0:M], rhs=XN[:, kc, :], start=(kc == 0), stop=(kc == 7)),
                     reads=[wk, ("XN", kc)], writes=[ppk])
            return pp, ppk

        def qk_post(pp, ppk, gc, dest, dkeys):
            P.op("act", lambda e: e.activation(out=U32, in_=pp[:], func=AF.Copy), reads=[ppk], writes=["U32"])
            P.op("act", lambda e: e.activation(out=SQ2, in_=pp[:], func=AF.Square), reads=[ppk], writes=["SQ2"])
            p2, p2k = psn()
            P.op("pe", lambda e: e.matmul(p2[:], lhsT=BLK16[:], rhs=SQ2, start=True, stop=True), reads=["SQ2", "BLK16"], writes=[p2k])
            P.op("act", lambda e: e.activation(out=TMPN, in_=p2[:], func=AF.Ln, scale=1.0 / 64, bias=EPSC[:]), reads=[p2k], writes=["TMPN"])
            P.op("act", lambda e: e.activation(out=RS, in_=TMPN, func=AF.Exp, scale=-0.5), reads=["TMPN"], writes=["RS"])
            P.op("dve", lambda e: e.scalar_tensor_tensor(out=QN32, in0=U32, scalar=SV[:, gc:gc + 1], in1=RS, op0=ALU.mult, op1=ALU.mult),
                 reads=["U32", "RS", "SV"], writes=["QN32"])
            p3, p3k = psn()
            P.op("pe", lambda e: e.matmul(p3[:], lhsT=ROT32[:], rhs=QN32, start=True, stop=True), reads=["QN32", "ROT32"], writes=[p3k])
            P.op("pool", lambda e: e.tensor_tensor(out=T132, in0=QN32, in1=CS[:, 0, :], op=ALU.mult), reads=["QN32", "CS"], writes=["T132"])
            P.op("dve", lambda e: e.tensor_tensor(out=T232, in0=p3[:], in1=CS[:, 1, :], op=ALU.mult), reads=[p3k, "CS"], writes=["T232"])
            P.op("dve", lambda e: e.tensor_tensor(out=dest, in0=T132, in1=T232, op=ALU.add), reads=["T132", "T232"], writes=dkeys)

        def qk_s1(pp, ppk):
            P.op("act", lambda e: e.activation(out=U32, in_=pp[:], func=AF.Copy), reads=[ppk], writes=["U32"])
            P.op("act", lambda e: e.activation(out=SQ2, in_=pp[:], func=AF.Square), reads=[ppk], writes=["SQ2"])

        def qk_p2():
            p2, p2k = psn()
            P.op("pe", lambda e: e.matmul(p2[:], lhsT=BLK16[:], rhs=SQ2, start=True, stop=True), reads=["SQ2", "BLK16"], writes=[p2k])
            return p2, p2k

        def qk_s2(p2, p2k, gc):
            P.op("act", lambda e: e.activation(out=TMPN, in_=p2[:], func=AF.Ln, scale=1.0 / 64, bias=EPSC[:]), reads=[p2k], writes=["TMPN"])
            P.op("act", lambda e: e.activation(out=RS, in_=TMPN, func=AF.Exp, scale=-0.5), reads=["TMPN"], writes=["RS"])
            P.op("dve", lambda e: e.scalar_tensor_tensor(out=QN32, in0=U32, scalar=SV[:, gc:gc + 1], in1=RS, op0=ALU.mult, op1=ALU.mult),
                 reads=["U32", "RS", "SV"], writes=["QN32"])

        def qk_p3():
            p3, p3k = psn()
            P.op("pe", lambda e: e.matmul(p3[:], lhsT=ROT32[:], rhs=QN32, start=True, stop=True), reads=["QN32", "ROT32"], writes=[p3k])
            return p3, p3k

        def qk_s3(p3, p3k, dest, dkeys):
            P.op("dve", lambda e: e.tensor_tensor(out=T132, in0=QN32, in1=CS[:, 0, :], op=ALU.mult), reads=["QN32", "CS"], writes=["T132"])
            P.op("dve", lambda e: e.tensor_tensor(out=T232, in0=p3[:], in1=CS[:, 1, :], op=ALU.mult), reads=[p3k, "CS"], writes=["T232"])
            P.op("dve", lambda e: e.tensor_tensor(out=dest, in0=T132, in1=T232, op=ALU.add), reads=["T132", "T232"], writes=dkeys)

        def qa_dest(c):
            r = c % 2
            return QF[:, r, :], [("QF", r)]

        def qa_store(c):
            r = c % 2
            dma(QS[c][:, t0:t0 + 512], QF[:, r, :], reads=[("QF", r)], writes=[("QS", b)])

        def rl(w_, wk_, ti, h):
            pp, ppk = proj(w_, wk_, ti)
            r = h % 2
            P.op("act", lambda e, r=r, pp=pp: e.activation(out=SR32[:, r, :], in_=pp[:], func=AF.Silu), reads=[ppk], writes=[("SR", r)])
            dma(GR[h][:, t0:t0 + 512], SR32[:, r, :], reads=[("SR", r)], writes=[("GR", b)])

        def va(w_, wk_):
            wva = w_[:, 0:1024].rearrange("p (kc n) -> p kc n", kc=8)
            pv, pvk = psn()
            for g in range(4):
                for kc in range(8):
                    P.op("pe", lambda e, g=g, kc=kc: e.matmul(pv[:, g * 128:(g + 1) * 128], lhsT=XN[:, kc, g * 128:(g + 1) * 128], rhs=wva[:, kc, :],
                                                              start=(kc == 0), stop=(kc == 7)),
                         reads=[wk_, ("XN", kc)], writes=[pvk])
            P.op("act", lambda e: e.activation(out=VO[:, 4 * b:4 * b + 4, :, 0:64], in_=pv[:].rearrange("p (g k n) -> p g k n", g=4, k=2), func=AF.Copy),
                 reads=[pvk], writes=[("VO", b)])

        def vl(g, wl0, wl0k, wl1, wl1k):
            pl, plk = psn()
            for kc in range(8):
                wsrc = wl0 if kc < 4 else wl1
                wkk = wl0k if kc < 4 else wl1k
                wvv = wsrc[:, 0:2048].rearrange("p (kc n) -> p kc n", kc=4)
                P.op("pe", lambda e, kc=kc, wvv=wvv: e.matmul(pl[:], lhsT=XN[:, kc, g * 128:(g + 1) * 128], rhs=wvv[:, kc % 4, :],
                                                               start=(kc == 0), stop=(kc == 7)),
                     reads=[wkk, ("XN", kc)], writes=[plk])
            if g % 2 == 0:
                P.op("act", lambda e: e.activation(out=VLT[:, g, :], in_=pl[:], func=AF.Copy), reads=[plk], writes=[("VLT", g)])
            else:
                P.op("dve", lambda e: e.tensor_copy(out=VLT[:, g, :], in_=pl[:]), reads=[plk], writes=[("VLT", g)])
            if g == 3:
                dma(GV[4 * b:4 * b + 4].rearrange("c p n -> p c n"), VLT, reads=[("VLT", g_) for g_ in range(4)], writes=[("GV", b)])

        v4 = lambda t, lo, hi: t[lo:hi, :].rearrange("p (c n) -> p c n", c=4)
        wg = WG16[:].rearrange("p (l h n) -> p l h n", l=L, h=4)

        def gla_a(w_, wk_, h):
            pq, pqk = proj(w_, wk_, 0)
            pkk_, pkkk = proj(w_, wk_, 1)
            px, pxk = psn()
            P.op("pe", lambda e: e.matmul(px[:], lhsT=wg[0:32, l, h, :], rhs=GLOW[0:32, :], start=True, stop=True),
                 reads=["GLOW", "WG16"], writes=[pxk])
            bcol = OFF_BG + 4 * l + h
            P.op("act", lambda e: e.activation(out=TZ32, in_=px[:], func=AF.Exp, scale=-1.0, bias=SV[:, bcol:bcol + 1]),
                 reads=[pxk, "SV"], writes=["TZ32"])
            P.op("act", lambda e: e.activation(out=L32, in_=TZ32, func=AF.Ln, bias=1.0), reads=["TZ32"], writes=["L32"])
            P.op("dve", lambda e: e.tensor_tensor_scan(out=PP32, data0=RESET[:], data1=L32, initial=0.0, op0=ALU.mult, op1=ALU.add),
                 reads=["L32", "RESET"], writes=["PP32"])
            P.op("act", lambda e: e.activation(out=Z32[0:64, :], in_=PP32[0:64, :], func=AF.Copy), reads=["PP32"], writes=["Z32"])
            P.op("dve", lambda e: e.tensor_tensor(out=TZ32[64:128, :], in0=L32[64:128, :], in1=PP32[64:128, :], op=ALU.subtract),
                 reads=["L32", "PP32"], writes=["TZ32"])
            P.op("dve", lambda e: e.tensor_tensor(out=v4(Z32, 64, 128), in0=v4(TZ32, 64, 128),
                                                  in1=v4(PP32, 64, 128)[:, :, 127:128].to_broadcast([64, 4, 128]), op=ALU.add),
                 reads=["TZ32", "PP32"], writes=["Z32"])
            P.op("act", lambda e: e.activation(out=EQ32, in_=Z32, func=AF.Exp, scale=-1.0 / 16), reads=["Z32"], writes=["EQ32"])
            P.op("act", lambda e: e.activation(out=EK32, in_=Z32, func=AF.Exp, scale=1.0 / 16), reads=["Z32"], writes=["EK32"])
            P.op("dve", lambda e: e.tensor_copy(out=DEC[0:64, h, 4 * b:4 * b + 4, :], in_=v4(EQ32, 0, 64)[:, :, 127:128]),
                 reads=["EQ32"], writes=[("DEC", h, b)])
            P.op("dve", lambda e: e.tensor_copy(out=DEC[64:128, h, 4 * b:4 * b + 4, :], in_=v4(EQ32, 64, 128)[:, :, 0:1]),
                 reads=["EQ32"], writes=[("DEC", h, b)])
            r = h % 2
            P.op("dve", lambda e: e.scalar_tensor_tensor(out=QD16[:, r, :], in0=pq[:], scalar=0.125, in1=EQ32, op0=ALU.mult, op1=ALU.mult),
                 reads=[pqk, "EQ32"], writes=[("QD", r)])
            P.op("dve", lambda e: e.tensor_tensor(out=KD16[:, r, :], in0=pkk_[:], in1=EK32, op=ALU.mult),
                 reads=[pkkk, "EK32"], writes=[("KD", r)])
            P.op("dve", lambda e: e.tensor_tensor(out=KDS16[:, r, :].rearrange("p (c n) -> p c n", c=4),
                                                  in0=KD16[:, r, :].rearrange("p (c n) -> p c n", c=4),
                                                  in1=DEC[:, h, 4 * b:4 * b + 4, :].to_broadcast([128, 4, 128]), op=ALU.mult),
                 reads=[("KD", r), ("DEC", h, b)], writes=[("KDS", r)])
            dma(GQ[h][:, t0:t0 + 512], QD16[:, r, :], reads=[("QD", r)], writes=[("GQ", b)])
            dma(GK[h][:, t0:t0 + 512], KD16[:, r, :], reads=[("KD", r)], writes=[("GK", b)])

        def gla_b(h):
            r = h % 2
            ptt, ptk = psn()
            ptb = ptt[:].bitcast(BF16)
            for ch in range(4):
                P.op("pe", lambda e, ch=ch: e.transpose(out=ptb[:, ch * 128:(ch + 1) * 128], in_=KDS16[:, r, ch * 128:(ch + 1) * 128], identity=ID16[:]),
                     reads=[("KDS", r), "ID16"], writes=[ptk])
            P.op("act", lambda e: e.activation(out=KDT16[:, r, :], in_=ptb[:, 0:512], func=AF.Copy), reads=[ptk], writes=[("KDT", r)])
            dma(GKT[h][4 * b:4 * b + 4].rearrange("c p n -> p c n"), KDT16[:, r, :].rearrange("p (c n) -> p c n", c=4),
                reads=[("KDT", r)], writes=[("GKT", b)])

        gq = OFF_QK + 0 * L + l
        gk = OFF_QK + 1 * L + l
        w0, w0k = ws.get()
        pp, ppk = proj(w0, w0k, 0, M=32)
        P.op("act", lambda e, pp=pp: e.activation(out=GLOW[0:32, :], in_=pp[0:32, :], func=AF.Copy), reads=[ppk], writes=["GLOW"])
        ppa, ppak = proj(w0, w0k, 1)
        qk_s1(ppa, ppak)
        w7, w7k = ws.get()
        rl(w7, w7k, 0, 0)
        p2, p2k = qk_p2()
        qk_s2(p2, p2k, gq)
        w1, w1k = ws.get()
        ppb, ppbk = proj(w1, w1k, 0)
        rl(w7, w7k, 1, 1)
        p3, p3k = qk_p3()
        qk_s3(p3, p3k, *qa_dest(0))
        qa_store(0)
        qk_s1(ppb, ppbk)
        ppc, ppck = proj(w1, w1k, 1)
        p2, p2k = qk_p2()
        qk_s2(p2, p2k, gq)
        w8, w8k = ws.get()
        rl(w8, w8k, 0, 2)
        p3, p3k = qk_p3()
        qk_s3(p3, p3k, *qa_dest(1))
        qa_store(1)
        qk_s1(ppc, ppck)
        w2, w2k = ws.get()
        ppd, ppdk = proj(w2, w2k, 0)
        p2, p2k = qk_p2()
        qk_s2(p2, p2k, gq)
        rl(w8, w8k, 1, 3)
        p3, p3k = qk_p3()
        qk_s3(p3, p3k, *qa_dest(2))
        qa_store(2)
        qk_s1(ppd, ppdk)
        ppe, ppek = proj(w2, w2k, 1)
        p2, p2k = qk_p2()
        qk_s2(p2, p2k, gq)
        wv_, wvk_ = ws.get()
        va(wv_, wvk_)
        p3, p3k = qk_p3()
        qk_s3(p3, p3k, *qa_dest(3))
        qa_store(3)
        qk_s1(ppe, ppek)
        w3, w3k = ws.get()
        gla_a(w3, w3k, 0)
        p2, p2k = qk_p2()
        qk_s2(p2, p2k, gk)
        wl0, wl0k = ws.get()
        wl1, wl1k = ws.get()
        vl(0, wl0, wl0k, wl1, wl1k)
        p3, p3k = qk_p3()
        qk_s3(p3, p3k, KT[:, t0:t0 + 512], [("KT", b)])
        vl(1, wl0, wl0k, wl1, wl1k)
        w4, w4k = ws.get()
        gla_a(w4, w4k, 1)
        gla_b(0)
        vl(2, wl0, wl0k, wl1, wl1k)
        vl(3, wl0, wl0k, wl1, wl1k)
        w5, w5k = ws.get()
        gla_a(w5, w5k, 2)
        gla_b(1)
        w6, w6k = ws.get()
        gla_a(w6, w6k, 3)
        gla_b(2)
        carry.append(lambda: gla_b(3))

    def back_loads(b, oc=True, xt=True):
        t0 = b * 512
        if oc:
            dma(OC[:, 0:4, :], OAS.rearrange("a p t -> p a t")[:, :, t0:t0 + 512], writes=[("OC", 0)])
            dma(OC[:, 4:8, :], OG.rearrange("a p t -> p a t")[:, :, t0:t0 + 512], writes=[("OC", 1)])
        if xt:
            dma(XT, HS.rearrange("kc p t -> p kc t")[:, :, t0:t0 + 512], reads=[("HS", b)], writes=[("XT", kc) for kc in range(8)])

    def back(ws, l, b, hook=None):
        t0 = b * 512
        for m in range(8):
            w, wk = ws.get()
            wv = w[:, 0:1024].rearrange("p (kc c) -> p kc c", kc=8)
            pd, pdk = psn()
            for kc in range(8):
                P.op("pe", lambda e, kc=kc, wv=wv, pd=pd: e.matmul(pd[:], lhsT=wv[:, kc, :], rhs=OC[:, kc, :], start=(kc == 0), stop=(kc == 7)),
                     reads=[wk, ("OC", kc // 4)], writes=[pdk])
            P.op("dve", lambda e, m=m, pd=pd: e.tensor_tensor(out=XT[:, m, :], in0=pd[:], in1=XT[:, m, :], op=ALU.add),
                 reads=[pdk, ("XT", m)], writes=[("XT", m)])
        run_carry()
        if hook is not None:
            hook()
        rmsnorm(2, l, XN)
        ffn(ws)
        rmsnorm(3, l, None)

    def attention(S):
        NB, NCH = S // 512, S // 128
        o2 = O_MAIN
        QTt = A.view(o2, BF16, [128, 2, 512]); o2 += 2 * KiB
        PT = A.view(o2, BF16, [128, 8, 512]); o2 += 8 * KiB
        ACCS = A.view(o2, F32, [128, 2, 512]); o2 += 4 * KiB
        RCP = A.view(o2, F32, [128, 2, 512]); o2 += 4 * KiB
        OAb = A.view(o2, BF16, [128, 2, 512]); o2 += 2 * KiB
        iters = [(qb, c) for qb in range(NB) for c in range(4)]
        steps = [(k, sc) for k in range(len(iters)) for sc in range(NCH)]
        pt_slot = {}
        pstate = {"pti": 0}

        def acc_of(k):
            r = k % 2
            return [(ps[2 * r], ("ps", 2 * r)), (ps[2 * r + 1], ("ps", 2 * r + 1))]

        def st_of(i):
            sp_ = 4 + 2 * (i % 2)
            return [(ps[sp_], ("ps", sp_)), (ps[sp_ + 1], ("ps", sp_ + 1))]

        def emit_mm1(i):
            k, sc = steps[i]
            qb, c = iters[k]
            r = k % 2
            if sc == 0:
                def ld_(kk):
                    qb_, c_ = iters[kk]
                    dma(QTt[:, kk % 2, :], QS[c_][:, qb_ * 512:qb_ * 512 + 512], writes=[("QTt", kk % 2)])
                if k == 0:
                    ld_(0)
                if k + 1 < len(iters):
                    ld_(k + 1)
            st = st_of(i)
            for hf in range(2):
                lo, hi = 64 * hf, 64 * hf + 64
                P.op("pe", lambda e, hf=hf, lo=lo, hi=hi, sc=sc, r=r, st=st: e.matmul(st[hf][0][:], lhsT=KT[lo:hi, sc * 128:(sc + 1) * 128],
                                                                                    rhs=QTt[lo:hi, r, :], start=True, stop=True),
                     reads=["KTall", ("QTt", r)], writes=[st[hf][1]])

        def emit_exp(i):
            st = st_of(i)
            sl = []
            for hf in range(2):
                s_ = pstate["pti"] % 8
                pstate["pti"] += 1
                sl.append(s_)
                P.op("act", lambda e, hf=hf, s_=s_, st=st: e.activation(out=PT[:, s_, :], in_=st[hf][0][:], func=AF.Exp, scale=0.125),
                     reads=[st[hf][1]], writes=[("PT", s_)])
            pt_slot[i] = sl

        def emit_mm2(i):
            k, sc = steps[i]
            acc = acc_of(k)
            for hf in range(2):
                s_ = pt_slot[i][hf]
                P.op("pe", lambda e, hf=hf, s_=s_, sc=sc, acc=acc: e.matmul(acc[hf][0][:], lhsT=VO[:, sc, hf, :], rhs=PT[:, s_, :],
                                                                           start=(sc == 0), stop=(sc == NCH - 1)),
                     reads=["VOall", ("PT", s_)], writes=[acc[hf][1]])

        def fin_a(k):
            acc = acc_of(k)
            for hf in range(2):
                ap_, ak = acc[hf]
                P.op("dve", lambda e, hf=hf, ap_=ap_: e.tensor_copy(out=ACCS[:, hf, :], in_=ap_[:]), reads=[ak], writes=[("ACCS", hf)])
                P.op("dve", lambda e, hf=hf: e.reciprocal(out=RCP[64:128, hf, :], in_=ACCS[64:128, hf, :]), reads=[("ACCS", hf)], writes=[("RCP", hf)])

        def fin_b(k):
            qb, c = iters[k]
            acc = acc_of(k)
            for hf in range(2):
                head = c + 4 * hf
                ap_, ak = acc[hf]
                P.op("pe", lambda e, hf=hf, ap_=ap_: e.matmul(ap_[0:64, :], lhsT=ID32[64:128, 64:128], rhs=RCP[64:128, hf, :], start=True, stop=True),
                     reads=[("RCP", hf), "ID32"], writes=[ak])
                P.op("dve", lambda e, hf=hf, ap_=ap_: e.tensor_tensor(out=OAb[0:64, hf, :], in0=ACCS[0:64, hf, :], in1=ap_[0:64, :], op=ALU.mult),
                     reads=[("ACCS", hf), ak], writes=[("OAb", hf)])
                po_ = (head % 2) * 64
                dma(OAS[head // 2][po_:po_ + 64, qb * 512:qb * 512 + 512], OAb[0:64, hf, :], reads=[("OAb", hf)], writes=["OAS"])

        nst = len(steps)
        emit_mm1(0)
        pending_fin = None
        for i in range(nst):
            k, sc = steps[i]
            if i + 1 < nst:
                emit_mm1(i + 1)
            emit_exp(i)
            emit_mm2(i)
            if pending_fin is not None and sc == min(8, NCH - 1):
                fin_b(pending_fin)
                pending_fin = None
            if sc == NCH - 1:
                fin_a(k)
                pending_fin = k
        if pending_fin is not None:
            fin_b(pending_fin)

    def gla(S, l):
        NB, NCH = S // 512, S // 128
        o2 = O_MAIN
        if 6 * NCH * 128 <= O_DEC:
            DSB = A.view(0, F32, [128, NCH, 128])
            STATE16 = A.view(4 * NCH * 128, BF16, [128, NCH, 128])
        else:
            DSB = A.view(o2, F32, [128, NCH, 128]); o2 += 4 * NCH * 128
            STATE16 = A.view(o2, BF16, [128, NCH, 128]); o2 += 2 * NCH * 128
        GQh = A.view(o2, BF16, [128, S]); o2 += 2 * S
        GKh = A.view(o2, BF16, [128, S]); o2 += 2 * S
        GKTh = A.view(o2, BF16, [128, NCH, 128]); o2 += 2 * S
        GVh = A.view(o2, BF16, [128, NCH, 128]); o2 += 2 * S
        SALL = A.view(o2, F32, [128, NCH, 128]); o2 += 4 * NCH * 128
        AT2 = A.view(o2, BF16, [128, 3, 2, 512]); o2 += 6 * KiB
        OSQ = A.view(o2, BF16, [128, 2, 512]); o2 += 2 * KiB
        SRb = A.view(o2, F32, [128, 3, 512]); o2 += 6 * KiB
        T1 = A.view(o2, F32, [128, 512]); o2 += 2 * KiB
        OGb = A.view(o2, BF16, [128, 2, 512]); o2 += 2 * KiB
        TN2 = A.view(o2, F32, [128, 512]); o2 += 2 * KiB
        RS2 = A.view(o2, F32, [128, 512]); o2 += 2 * KiB
        assert o2 <= ARENA_BYTES, o2
        for h in range(4):
            cst = min(16, NCH)
            for c0 in range(0, NCH, cst):
                dma(GKTh[:, c0:c0 + cst, :], GKT[h][c0:c0 + cst].rearrange("c p n -> p c n"), writes=[("GKTh", c0 // cst)])
                dma(GVh[:, c0:c0 + cst, :], GV[c0:c0 + cst, :, h * 128:(h + 1) * 128].rearrange("c p n -> p c n"), writes=[("GVh", c0 // cst)])
            step = min(2048, S)
            for c0 in range(0, S, step):
                dma(GQh[:, c0:c0 + step], GQ[h][:, c0:c0 + step], writes=[("GQh", c0 // step)])
                dma(GKh[:, c0:c0 + step], GK[h][:, c0:c0 + step], writes=[("GKh", c0 // step)])
            P.op("pool", lambda e: e.memset(SALL[0:64, 0, :], 0.0), writes=[("SA", 0)])
            P.op("pool", lambda e: e.memset(SALL[64:128, NCH - 1, :], 0.0), writes=[("SB", NCH - 1)])
            for cb in range(NB):
                pd, pdk = psn()
                for ci in range(4):
                    c = 4 * cb + ci
                    P.op("pe", lambda e, c=c, ci=ci, pd=pd: e.matmul(pd[:, ci * 128:(ci + 1) * 128], lhsT=GKTh[:, c, :], rhs=GVh[:, c, :], start=True, stop=True),
                         reads=[("GKTh", c // cst), ("GVh", c // cst)], writes=[pdk])
                P.op("act", lambda e, cb=cb, pd=pd: e.activation(out=DSB[:, 4 * cb:4 * cb + 4, :], in_=pd[:].rearrange("p (c n) -> p c n", c=4), func=AF.Copy),
                     reads=[pdk], writes=[("DSB", cb)])

            def fwd_step(c):
                P.op("dve", lambda e, c=c, h=h: e.scalar_tensor_tensor(out=SALL[0:64, c + 1, :], in0=SALL[0:64, c, :], scalar=DEC[0:64, h, c, :],
                                                                     in1=DSB[0:64, c, :], op0=ALU.mult, op1=ALU.add),
                     reads=[("SA", c), ("DSB", c // 4), "DECall"], writes=[("SA", c + 1)])

            def bwd_step(c):
                P.op("dve", lambda e, c=c, h=h: e.scalar_tensor_tensor(out=SALL[64:128, c - 1, :], in0=SALL[64:128, c, :], scalar=DEC[64:128, h, c, :],
                                                                     in1=DSB[64:128, c, :], op0=ALU.mult, op1=ALU.add),
                     reads=[("SB", c), ("DSB", c // 4), "DECall"], writes=[("SB", c - 1)])

            LEAD = min(12, max(1, (NCH - 1) // 2))
            fl = list(range(0, NCH - 1))
            bl = list(range(NCH - 1, 0, -1))
            for k in range(len(fl) + LEAD):
                if k < len(fl):
                    fwd_step(fl[k])
                if 0 <= k - LEAD < len(bl):
                    bwd_step(bl[k - LEAD])
            cst2 = min(16, NCH)
            for c0 in range(0, NCH, cst2):
                en_ = "act"
                rk = [("SA", c) for c in range(c0, c0 + cst2)] + [("SB", c) for c in range(c0, c0 + cst2)]
                wk_ = [("ST16", c) for c in range(c0, c0 + cst2)]
                if en_ == "act":
                    P.op("act", lambda e, c0=c0: e.activation(out=STATE16[:, c0:c0 + cst2, :], in_=SALL[:, c0:c0 + cst2, :], func=AF.Copy), reads=rk, writes=wk_)
                else:
                    P.op("pool", lambda e, c0=c0: e.tensor_copy(out=STATE16[:, c0:c0 + cst2, :], in_=SALL[:, c0:c0 + cst2, :]), reads=rk, writes=wk_)
            m2f = M2[:, 0:128].rearrange("p (o n) -> p o n", o=1).to_broadcast([128, 4, 128])
            m2b = M2[:, 128:256].rearrange("p (o n) -> p o n", o=1).to_broadcast([128, 4, 128])
            v4_ = lambda t: t.rearrange("p (c n) -> p c n", c=4)
            gc = OFF_GLA + l
            pos = {}

            def st3_A(cb):
                paF, pafk = psn()
                paB, pabk = psn()
                r3 = cb % 3
                r2 = cb % 2
                dma(SRb[:, r3, :], GR[h][:, cb * 512:cb * 512 + 512], writes=[("SRb", r3)])
                for ci in range(4):
                    c = 4 * cb + ci
                    cs = slice(c * 128, (c + 1) * 128)
                    osl = slice(ci * 128, (ci + 1) * 128)
                    qk_ = [("GKh", (c * 128) // step), ("GQh", (c * 128) // step)]
                    P.op("pe", lambda e, cs=cs, osl=osl: e.matmul(paF[:, osl], lhsT=GKh[0:64, cs], rhs=GQh[0:64, cs], start=True, stop=True),
                         reads=qk_, writes=[pafk])
                    P.op("pe", lambda e, cs=cs, osl=osl: e.matmul(paB[:, osl], lhsT=GKh[64:128, cs], rhs=GQh[64:128, cs], start=True, stop=True),
                         reads=qk_, writes=[pabk])
                P.op("dve", lambda e: e.tensor_tensor(out=v4_(AT2[:, r3, 0, :]), in0=v4_(paF[:]), in1=m2f, op=ALU.mult),
                     reads=[pafk, "M2"], writes=[("AT2", r3, 0)])
                P.op("dve", lambda e: e.tensor_tensor(out=v4_(AT2[:, r3, 1, :]), in0=v4_(paB[:]), in1=m2b, op=ALU.mult),
                     reads=[pabk, "M2"], writes=[("AT2", r3, 1)])

            def st3_O(cb):
                po, pok = psn()
                pos[cb] = (po, pok)
                r3 = cb % 3
                r2 = cb % 2
                for ci in range(4):
                    c = 4 * cb + ci
                    cs = slice(c * 128, (c + 1) * 128)
                    osl = slice(ci * 128, (ci + 1) * 128)
                    P.op("pe", lambda e, c=c, osl=osl: e.matmul(po[:, osl], lhsT=GVh[:, c, :], rhs=AT2[:, r3, 0, osl], start=True, stop=False),
                         reads=[("GVh", c // cst), ("AT2", r3, 0)], writes=[pok])
                    P.op("pe", lambda e, c=c, osl=osl: e.matmul(po[:, osl], lhsT=GVh[:, c, :], rhs=AT2[:, r3, 1, osl], start=False, stop=False),
                         reads=[("GVh", c // cst), ("AT2", r3, 1)], writes=[pok])
                    P.op("pe", lambda e, c=c, cs=cs, osl=osl: e.matmul(po[:, osl], lhsT=STATE16[:, c, :], rhs=GQh[:, cs], start=False, stop=True),
                         reads=[("ST16", c), ("GQh", (c * 128) // step)], writes=[pok])
                P.op("act", lambda e: e.activation(out=OSQ[:, r2, :], in_=po[:], func=AF.Square), reads=[pok], writes=[("OSQ", r2)])

            def st3_N(cb):
                po, pok = pos.pop(cb)
                r2 = cb % 2
                pn, pnk = psn()
                P.op("pe", lambda e: e.matmul(pn[:], lhsT=ONES16[:], rhs=OSQ[:, r2, :], start=True, stop=True), reads=[("OSQ", r2), "ONES16"], writes=[pnk])
                P.op("act", lambda e: e.activation(out=TN2, in_=pn[:], func=AF.Ln, scale=1.0 / 128, bias=EPSC[:]), reads=[pnk], writes=["TN2"])
                P.op("act", lambda e: e.activation(out=RS2, in_=TN2, func=AF.Exp, scale=-0.5), reads=["TN2"], writes=["RS2"])
                P.op("dve", lambda e: e.scalar_tensor_tensor(out=T1, in0=po[:], scalar=SV[:, gc:gc + 1], in1=RS2, op0=ALU.mult, op1=ALU.mult),
                     reads=[pok, "RS2", "SV"], writes=["T1"])
                r3 = cb % 3
                P.op("dve", lambda e: e.tensor_tensor(out=OGb[:, r2, :], in0=T1, in1=SRb[:, r3, :], op=ALU.mult),
                     reads=["T1", ("SRb", r3)], writes=[("OGb", r2)])
                dma(OG[h][:, cb * 512:cb * 512 + 512], OGb[:, r2, :], reads=[("OGb", r2)], writes=["OG"])

            for stp in range(NB + 2):
                if stp < NB:
                    st3_A(stp)
                if 0 <= stp - 1 < NB:
                    st3_O(stp - 1)
                if 0 <= stp - 2 < NB:
                    st3_N(stp - 2)

    for name, S in seqs:
        NB = S // 512
        P.op("pool", lambda e: e.memset(VO[:, :, :, 64:128], 1.0), writes=["VOones"])
        ws = WS()
        for b in range(NB):
            plan_front(ws, 0)
        load_x_dma(xin[name], 0)
        for b in range(NB):
            load_x(xin[name], b * 512)
            run_carry()
            hk = (lambda b=b: load_x_dma(xin[name], (b + 1) * 512)) if b + 1 < NB else None
            front(ws, 0, b, hook=hk)
        run_carry()
        P.flush()
        if stop_after == "front":
            return nc
        for l in range(L):
            attention(S)
            P.flush()
            if stop_after == "att":
                return nc
            gla(S, l)
            P.flush()
            if stop_after == "gla":
                return nc
            ws = WS()
            for b in range(NB):
                plan_back(ws, l)
                if l + 1 < L:
                    plan_front(ws, l + 1)
            if l + 1 == L:
                pass
            else:
                P.op("pool", lambda e: e.memset(VO[:, :, :, 64:128], 1.0), writes=["VOones"])
            back_loads(0)
            for b in range(NB):
                if l + 1 < L:
                    back(ws, l, b)
                    hk = (lambda b=b: back_loads(b + 1)) if b + 1 < NB else None
                    front(ws, l + 1, b, hook=hk)
                else:
                    hk = (lambda b=b: back_loads(b + 1, oc=True, xt=False)) if b + 1 < NB else None
                    back(ws, l, b, hook=hk)
                    store_y(yout[name], b * 512)
                    if b + 1 < NB:
                        back_loads(b + 1, oc=False, xt=True)
            run_carry()
            P.flush()
    return nc


def make_consts(SMAX):
    ident = np.eye(128, dtype=np.float32)
    rot = np.zeros((128, 128), np.float32)
    for hb in (0, 64):
        for i in range(16):
            rot[hb + 16 + i, hb + i] = -1.0
            rot[hb + i, hb + 16 + i] = 1.0
            rot[hb + 48 + i, hb + 32 + i] = -1.0
            rot[hb + 32 + i, hb + 48 + i] = 1.0
    blk = np.zeros((128, 128), np.float32)
    blk[0:64, 0:64] = 1.0
    blk[64:128, 64:128] = 1.0
    j = np.arange(128)[:, None]
    i = np.arange(128)[None, :]
    m2 = np.concatenate([(j <= i), (j > i)], axis=1).astype(np.float32)
    reset = np.ones((128, 512), np.float32)
    reset[:, ::128] = 0.0
    t = np.arange(SMAX)
    row = (t // 64).astype(np.float32)
    col = (t % 64).astype(np.float32)
    inv_freq = (1.0 / (np.float32(10000.0) ** (np.arange(0, 32, 2, dtype=np.float32) / np.float32(32)))).astype(np.float32)
    ang_r = row[:, None] * inv_freq[None, :]
    ang_c = col[:, None] * inv_freq[None, :]
    ang = np.concatenate([ang_r, ang_r, ang_c, ang_c], axis=-1).astype(np.float32)
    cos = np.cos(ang).astype(np.float32).T
    sin = np.sin(ang).astype(np.float32).T
    cosT = np.ascontiguousarray(np.concatenate([cos, cos], axis=0))
    sinT = np.ascontiguousarray(np.concatenate([sin, sin], axis=0))
    return {"c_ident": ident, "c_rot": rot, "c_blk": blk, "c_m2": m2, "c_reset": reset, "c_cos": cosT, "c_sin": sinT}


WNAMES = ["norm_ffn1", "w_ffn1_gu", "w_ffn1_down", "norm_mix", "w_in", "q_norm", "k_norm", "w_gate_f", "b_gate_f",
          "w_gate_b", "b_gate_b", "gla_norm", "w_out", "norm_ffn2", "w_ffn2_gu", "w_ffn2_down", "norm_out"]


def kernel(**inputs):
    xp = np.asarray(inputs["x_prompt"], dtype=np.float32)
    xs = np.asarray(inputs["x_sample"], dtype=np.float32)
    nb = xp.shape[0]
    SP, SS = xp.shape[1], xs.shape[1]
    depth = np.asarray(inputs["w_in"]).shape[0]
    nc = build([("p", SP), ("s", SS)], depth=depth)
    consts = make_consts(max(SP, SS))
    wts = {k: np.ascontiguousarray(np.asarray(inputs[k], dtype=np.float32)) for k in WNAMES}
    in_maps = []
    for i in range(nb):
        m = {"x_p": np.ascontiguousarray(xp[i]), "x_s": np.ascontiguousarray(xs[i])}
        m.update(wts)
        m.update(consts)
        in_maps.append(m)
    res = run_bass_kernel_spmd(nc, in_maps, core_ids=list(range(nb)))
    yp = np.stack([np.asarray(r["y_p"], dtype=np.float32) for r in res.results], axis=0)
    ys = np.stack([np.asarray(r["y_s"], dtype=np.float32) for r in res.results], axis=0)
    return (yp, ys)
```

```python
import numpy as np
import concourse.bass as bass
import concourse.mybir as mybir
from concourse.bass_utils import run_bass_kernel_spmd

F32 = mybir.dt.float32
BF16 = mybir.dt.bfloat16
AF = mybir.ActivationFunctionType
ALU = mybir.AluOpType

D = 1024
KC = 8
DFF = 2816
JC = 22
DIN = 2336
EPS = 1e-6
TB = 512
NQ = 12


class Op:
    __slots__ = ("eng", "fn", "dma", "deps", "signal", "sem", "val", "prev_val")

    def __init__(self, eng, fn, dma):
        self.eng = eng
        self.fn = fn
        self.dma = dma
        self.deps = []
        self.signal = False
        self.sem = None
        self.val = 0
        self.prev_val = 0


class Prog:
    ENGS = ["pe", "act", "dve", "pool", "sp"]

    def __init__(self, nc):
        self.nc = nc
        self.sem = {e: nc.alloc_semaphore("s_" + e) for e in ["pe", "act", "dve", "pool"]}
        self.cnt = {e: 0 for e in self.sem}
        self.dsem = {q: [nc.alloc_semaphore("d_%s_%d" % (q, i)) for i in range(NQ)] for q in ["sp", "act"]}
        self.dcnt = {q: 0 for q in self.dsem}
        self.dlast = {q: [0] * NQ for q in self.dsem}
        self.waited = {}
        self.reset()
        allsems = list(self.sem.values()) + [s for q in self.dsem for s in self.dsem[q]]
        with nc.Block() as block:
            def clr(e):
                for s in allsems:
                    e.sem_clear(s)
            block.sync(clr)

    def reset(self):
        self.ops = []
        self.writers = {}
        self.readers = {}

    def op(self, eng, fn, reads=(), writes=(), dma=False):
        o = Op(eng, fn, dma)
        idx = len(self.ops)
        deps = {}
        for k in reads:
            for w in self.writers.get(k, ()):
                deps[w] = "RAW"
        for k in writes:
            rs = self.readers.get(k, ())
            for w in self.writers.get(k, ()):
                deps.setdefault(w, "WAW")
            for r in rs:
                deps.setdefault(r, "WAR")
        deps.pop(idx, None)
        o.deps = list(deps.items())
        def keep(lst):
            if dma:
                return lst
            return [i for i in lst if self.ops[i].dma or self.ops[i].eng != eng]
        for k in reads:
            self.readers[k] = keep(self.readers.get(k, [])) + [idx]
        for k in writes:
            if self.readers.get(k):
                self.writers[k] = [idx]
                self.readers[k] = []
            else:
                self.writers[k] = keep(self.writers.get(k, [])) + [idx]
        self.ops.append(o)
        return o

    def _needs_sem(self, p, o, kind):
        if p.dma:
            return True
        if p.eng != o.eng or o.dma:
            return True
        if p.eng == "pe":
            return False
        return True

    def flush(self, name=None):
        nc = self.nc
        ops = self.ops
        if not ops:
            return
        for o in ops:
            for d, kind in o.deps:
                p = ops[d]
                if not p.dma and self._needs_sem(p, o, kind):
                    p.signal = True
        for o in ops:
            if o.dma:
                q = o.eng
                k = self.dcnt[q]
                self.dcnt[q] += 1
                o.sem = (q, k % NQ)
                o.val = 16 * (k // NQ + 1)
                o.prev_val = 16 * (k // NQ)
                self.dlast[q][k % NQ] = o.val
            elif o.signal:
                self.cnt[o.eng] += 1
                o.sem = o.eng
                o.val = self.cnt[o.eng]

        def semobj(key):
            if isinstance(key, tuple):
                return self.dsem[key[0]][key[1]]
            return self.sem[key]

        bname = {"pe": "tensor", "act": "scalar", "dve": "vector", "pool": "gpsimd", "sp": "sync"}
        with nc.Block() as block:
            for ename in self.ENGS:
                elist = [o for o in ops if o.eng == ename]

                def body(e, elist=elist, ename=ename):
                    for o in elist:
                        waits = {}
                        for d, kind in o.deps:
                            p = ops[d]
                            if p.sem is None or not self._needs_sem(p, o, kind):
                                continue
                            if waits.get(p.sem, 0) < p.val:
                                waits[p.sem] = p.val
                        if o.dma and o.prev_val > 0:
                            if waits.get(o.sem, 0) < o.prev_val:
                                waits[o.sem] = o.prev_val
                        for sk, val in waits.items():
                            if self.waited.get((ename, sk), 0) < val:
                                e.wait_ge(semobj(sk), val)
                                self.waited[(ename, sk)] = val
                        ins = o.fn(e)
                        if o.sem is not None:
                            ins.then_inc(semobj(o.sem), 16 if o.dma else 1)
                    if ename in self.dsem:
                        for i in range(NQ):
                            v = self.dlast[ename][i]
                            if v > 0 and self.waited.get((ename, (ename, i)), 0) < v:
                                e.wait_ge(self.dsem[ename][i], v)
                                self.waited[(ename, (ename, i))] = v

                getattr(block, bname[ename])(body)
        self.reset()


def _prod(s):
    r = 1
    for v in s:
        r *= v
    return r


_RE = {2: None, 3: "p (a b) -> p a b", 4: "p (a b c) -> p a b c"}


class Arena:
    def __init__(self, nc, nbytes):
        self.n = nbytes
        self.t = nc.alloc_sbuf_tensor("arena", [128, nbytes // 2], BF16)

    def view(self, off, dtype, shape):
        n = _prod(shape[1:])
        esz = 4 if dtype == F32 else 2
        assert off % 4 == 0 and off + n * esz <= self.n, (off, n * esz, self.n)
        ap = self.t[0:shape[0], off // 2: off // 2 + n * esz // 2]
        if dtype != BF16:
            ap = ap.bitcast(dtype)
        if len(shape) == 3:
            ap = ap.rearrange("p (a b) -> p a b", a=shape[1])
        elif len(shape) == 4:
            ap = ap.rearrange("p (a b c) -> p a b c", a=shape[1], b=shape[2])
        return ap


def build(seqs, depth=2, dbg=False, stop_after=None):
    nc = bass.Bass("TRN2", target_bir_lowering=False)
    P = Prog(nc)
    L = depth
    SMAX = max(S for _, S in seqs)
    NCHMAX = SMAX // 128
    K = 1024

    def din(name, shape, dt=F32):
        return nc.dram_tensor(name, list(shape), dt, kind="ExternalInput").ap()

    def dscr(name, shape, dt):
        kind = "ExternalOutput" if dbg else "Internal"
        return nc.dram_tensor(name, list(shape), dt, kind=kind).ap()

    xin = {n: din("x_" + n, [S, D]) for n, S in seqs}
    yout = {n: nc.dram_tensor("y_" + n, [S, D], F32, kind="ExternalOutput").ap() for n, S in seqs}
    W = {}
    for nm, shp in [("norm_ffn1", [L, D]), ("w_ffn1_gu", [L, D, 2 * DFF]), ("w_ffn1_down", [L, DFF, D]),
                    ("norm_mix", [L, D]), ("w_in", [L, D, DIN]), ("q_norm", [L, 64]), ("k_norm", [L, 64]),
                    ("w_gate_f", [L, 16, 256]), ("b_gate_f", [L, 256]), ("w_gate_b", [L, 16, 256]),
                    ("b_gate_b", [L, 256]), ("gla_norm", [L, 128]), ("w_out", [L, D, D]),
                    ("norm_ffn2", [L, D]), ("w_ffn2_gu", [L, D, 2 * DFF]), ("w_ffn2_down", [L, DFF, D]),
                    ("norm_out", [L, D])]:
        W[nm] = din(nm, shp)
    c_ident = din("c_ident", [128, 128])
    c_rot = din("c_rot", [128, 128])
    c_blk = din("c_blk", [128, 128])
    c_m2 = din("c_m2", [128, 256])
    c_reset = din("c_reset", [128, 512])
    c_cos = din("c_cos", [128, SMAX])
    c_sin = din("c_sin", [128, SMAX])

    WGU = [[dscr("WGU%d%d" % (l, f), [JC, 128, 2 * 8 * 128], BF16) for f in range(2)] for l in range(L)]
    WDN = [[dscr("WDN%d%d" % (l, f), [8, 128, 22 * 128], BF16) for f in range(2)] for l in range(L)]
    WINL = [dscr("WINL%d" % l, [18, 128, 8 * 128], BF16) for l in range(L)]
    WINR = [dscr("WINR%d" % l, [128, 8 * 640], BF16) for l in range(L)]
    WOUT = [dscr("WOUT%d" % l, [8, 128, 8 * 128], BF16) for l in range(L)]
    HS = dscr("HS", [8, 128, SMAX], F32)
    QS = dscr("QS", [4, 128, SMAX], BF16)
    GQ = dscr("GQ", [4, 128, SMAX], BF16)
    GK = dscr("GK", [4, 128, SMAX], BF16)
    GKT = dscr("GKT", [4, NCHMAX, 128, 128], BF16)
    GV = dscr("GV", [NCHMAX, 128, 512], BF16)
    GR = dscr("GR", [4, 128, SMAX], F32)
    OAS = dscr("OAS", [4, 128, SMAX], BF16)
    OG = dscr("OG", [4, 128, SMAX], BF16)

    sb = nc.alloc_sbuf_tensor
    ID32 = sb("ID32", [128, 128], F32)
    ID16 = sb("ID16", [128, 128], BF16)
    ROT32 = sb("ROT32", [128, 128], F32)
    ONES16 = sb("ONES16", [128, 128], BF16)
    BLK16 = sb("BLK16", [128, 128], BF16)
    M2 = sb("M2", [128, 256], F32)
    RESET = sb("RESET", [128, 512], F32)
    SV = sb("SV", [128, 128], F32)
    WG16 = sb("WG16", [32, L * 4 * 128], BF16)
    EPSC = sb("EPSC", [128, 1], F32)
    ps = [nc.alloc_psum_tensor("ps%d" % i, [128, 512], F32) for i in range(8)]
    ARENA_BYTES = 192 * 1024
    A = Arena(nc, ARENA_BYTES)

    OFF_N, OFF_QK, OFF_GLA, OFF_BG = 0, 32 * L, 34 * L, 35 * L

    def gcol(which, l, kc):
        c = OFF_N + which * L * 8 + l * 8 + kc
        return SV[:, c:c + 1]

    state = {"ps": 0}

    def psn():
        i = state["ps"] % 8
        state["ps"] += 1
        return ps[i], ("ps", i)

    def dma(out, in_, reads=(), writes=(), q="sp"):
        return P.op(q, lambda e: e.dma_start(out=out, in_=in_), reads=reads, writes=writes, dma=True)

    KiB = 1024

    def prep():
        stg32 = [A.view(i * 66 * KiB, F32, [128, 11264]) for i in range(2)]
        stg16 = [A.view(i * 66 * KiB + 44 * KiB, BF16, [128, 11264]) for i in range(2)]
        small = A.view(132 * KiB, F32, [128, 128])
        wg32 = A.view(133 * KiB, F32, [32, L * 4 * 128])
        tmpc = A.view(140 * KiB, F32, [128, 128])
        dma(ID32[:], c_ident, writes=["ID32"])
        dma(ROT32[:], c_rot, writes=["ROT32"])
        dma(M2[:], c_m2, writes=["M2"])
        dma(RESET[:], c_reset, writes=["RESET"])
        dma(tmpc, c_blk, writes=["tmpc"])
        P.op("dve", lambda e: e.tensor_copy(out=BLK16[:], in_=tmpc), reads=["tmpc"], writes=["BLK16"])
        P.op("dve", lambda e: e.tensor_copy(out=ID16[:], in_=ID32[:]), reads=["ID32"], writes=["ID16"])
        P.op("pool", lambda e: e.memset(ONES16[:], 1.0), writes=["ONES16"])
        P.op("pool", lambda e: e.memset(EPSC[:], EPS), writes=["EPSC"])
        P.op("pool", lambda e: e.memset(small, 0.0), writes=["small"])
        for wi, nm in enumerate(["norm_ffn1", "norm_mix", "norm_ffn2", "norm_out"]):
            r0 = OFF_N + wi * L * 8
            dma(small[r0:r0 + L * 8, :], W[nm].rearrange("l (kc p) -> (l kc) p", p=128), reads=["small0"], writes=["small"])
        for wi, nm in enumerate(["q_norm", "k_norm"]):
            r0 = OFF_QK + wi * L
            dma(small[r0:r0 + L, 0:64], W[nm], reads=["small0"], writes=["small"])
            dma(small[r0:r0 + L, 64:128], W[nm], reads=["small0"], writes=["small"])
        dma(small[OFF_GLA:OFF_GLA + L, :], W["gla_norm"], reads=["small0"], writes=["small"])
        dma(small[OFF_BG:OFF_BG + 4 * L, 0:64], W["b_gate_f"].rearrange("l (h n) -> (l h) n", n=64), reads=["small0"], writes=["small"])
        dma(small[OFF_BG:OFF_BG + 4 * L, 64:128], W["b_gate_b"].rearrange("l (h n) -> (l h) n", n=64), reads=["small0"], writes=["small"])
        pt, pk = psn()
        P.op("pe", lambda e: e.transpose(out=pt[:, 0:128], in_=small, identity=ID32[:]), reads=["small", "ID32"], writes=[pk])
        P.op("dve", lambda e: e.tensor_copy(out=SV[:], in_=pt[:, 0:128]), reads=[pk], writes=["SV"])
        P.op("dve", lambda e: e.tensor_scalar(out=SV[:, OFF_BG:OFF_BG + 4 * L], in0=SV[:, OFF_BG:OFF_BG + 4 * L],
                                              scalar1=-1.0, scalar2=None, op0=ALU.mult), reads=["SV"], writes=["SV"])
        P.op("pool", lambda e: e.memset(wg32, 0.0), writes=["wg32"])
        wgv = wg32.rearrange("p (l h n) -> p l h n", l=L, h=4)
        for l in range(L):
            dma(wgv[0:16, l, :, 0:64], W["w_gate_f"][l].rearrange("r (h n) -> r h n", n=64), reads=["wg320"], writes=["wg32"])
            dma(wgv[16:32, l, :, 64:128], W["w_gate_b"][l].rearrange("r (h n) -> r h n", n=64), reads=["wg320"], writes=["wg32"])
        P.op("dve", lambda e: e.tensor_copy(out=WG16[:], in_=wg32), reads=["wg32"], writes=["WG16"])

        st = {"i": 0}
        engs = ["dve", "act"]
        pend = []

        def cast(out, in_, r, w):
            en = engs[st["i"] % 2]
            st["i"] += 1
            if en == "act":
                P.op("act", lambda e: e.activation(out=out, in_=in_, func=AF.Copy), reads=r, writes=w)
            else:
                P.op(en, lambda e: e.tensor_copy(out=out, in_=in_), reads=r, writes=w)

        pi = {"i": 0}

        def plain(src, nkc, c0, ncols, dst_fn):
            i = pi["i"] % 2
            pi["i"] += 1
            s32 = stg32[i][:, 0:nkc * ncols].rearrange("p (kc n) -> p kc n", kc=nkc)
            nm = ncols // 128
            half = (nkc + 1) // 2
            srcv = src[:, c0:c0 + ncols].rearrange("(kc p) n -> p kc n", p=128)
            dma(s32[:, 0:half, :], srcv[:, 0:half, :], writes=[("s32", i)])
            dma(s32[:, half:nkc, :], srcv[:, half:nkc, :], writes=[("s32", i)])
            while pend:
                pend.pop(0)()
            s16 = stg16[i][:, 0:nkc * ncols].rearrange("p (m kc c) -> p m kc c", m=nm, kc=nkc)
            s32p = stg32[i][:, 0:nkc * ncols].rearrange("p (kc m c) -> p m kc c", kc=nkc, m=nm)
            step = max(1, nm // 3)
            for m0 in range(0, nm, step):
                m1 = min(nm, m0 + step)
                cast(s16[:, m0:m1], s32p[:, m0:m1], [("s32", i)], [("s16", i)])
            s16v_ = stg16[i][:, 0:nkc * ncols].rearrange("p (m n) -> p m n", m=nm)
            pend.append(lambda: dst_fn(s16v_, ("s16", i)))

        for l in range(L):
            for f in range(2):
                src = W["w_ffn%d_gu" % (f + 1)][l]
                for t in range(2):
                    for jg in range(2):
                        def dst_fn(s16v, key, l=l, f=f, t=t, jg=jg):
                            dv = WGU[l][f][11 * jg:11 * jg + 11].rearrange("j p (t n) -> p j t n", t=2)[:, :, t, :]
                            dma(dv, s16v, reads=[key], writes=["WGU"])
                        plain(src, 8, t * DFF + jg * 1408, 1408, dst_fn)
                src = W["w_ffn%d_down" % (f + 1)][l]
                for mh in range(2):
                    def dst_fn(s16v, key, l=l, f=f, mh=mh):
                        dv = WDN[l][f][4 * mh:4 * mh + 4].rearrange("m p n -> p m n")
                        dma(dv, s16v, reads=[key], writes=["WDN"])
                    plain(src, 22, mh * 512, 512, dst_fn)
            def dst_fn(s16v, key, l=l):
                dma(WOUT[l].rearrange("m p n -> p m n"), s16v, reads=[key], writes=["WOUT"])
            plain(W["w_out"][l], 8, 0, 1024, dst_fn)
            src = W["w_in"][l]
            i = pi["i"] % 2
            pi["i"] += 1
            sA = stg32[i][:, 0:8 * 1280].rearrange("p (kc n) -> p kc n", kc=8)
            srcv = src[:, 0:1280].rearrange("(kc p) n -> p kc n", p=128)
            dma(sA[:, 0:4, :], srcv[:, 0:4, :], writes=[("s32", i)])
            dma(sA[:, 4:8, :], srcv[:, 4:8, :], writes=[("s32", i)])
            while pend:
                pend.pop(0)()
            T16 = A.view(152 * KiB, BF16, [128, 18 * 1024]).rearrange("p (t kc c) -> p t kc c", t=18, kc=8)
            k32, k16 = [("s32", i)], ["T16"]
            cast(T16[:, 1:5, :, 0:64], sA[:, :, 0:256].rearrange("p kc (c n) -> p c kc n", n=64), k32, k16)
            cast(T16[:, 1:5, :, 64:128], sA[:, :, 256:512].rearrange("p kc (c n) -> p c kc n", n=64), k32, k16)
            cast(T16[:, 5, :, :], sA[:, :, 512:640], k32, k16)
            for h in range(4):
                for half in range(2):
                    cast(T16[:, 6 + 2 * h, :, 64 * half:64 * half + 64], sA[:, :, 768 + 64 * h:768 + 64 * h + 64], k32, k16)
                    cast(T16[:, 7 + 2 * h, :, 64 * half:64 * half + 64], sA[:, :, 1024 + 64 * h:1024 + 64 * h + 64], k32, k16)
            R16 = A.view(141 * KiB, BF16, [128, 8, 640])
            cast(R16[:, :, 0:128], sA[:, :, 640:768], k32, ["R16"])
            j = pi["i"] % 2
            pi["i"] += 1
            sB = stg32[j][:, 0:8 * 1056].rearrange("p (kc n) -> p kc n", kc=8)
            srcv = src[:, 1280:2336].rearrange("(kc p) n -> p kc n", p=128)
            dma(sB[:, 0:4, :], srcv[:, 0:4, :], writes=[("s32", j)])
            dma(sB[:, 4:8, :], srcv[:, 4:8, :], writes=[("s32", j)])
            kb = [("s32", j)]
            cast(R16[:, :, 128:640], sB[:, :, 0:512], kb, ["R16"])
            cast(T16[:, 14:18, :, :], sB[:, :, 512:1024].rearrange("p kc (h n) -> p h kc n", n=128), kb, k16)
            P.op("pool", lambda e, T16=T16: e.memset(T16[:, 0, :, 32:128], 0.0), writes=k16)
            cast(T16[:, 0, :, 0:32], sB[:, :, 1024:1056], kb, k16)
            pend.append(lambda l=l, k16=k16: dma(WINL[l].rearrange("t p n -> p t n"), A.view(152 * KiB, BF16, [128, 18, 1024]),
                                                 reads=k16, writes=["WINL"]))
            pend.append(lambda l=l, R16=R16: dma(WINR[l], R16.rearrange("p kc n -> p (kc n)"), reads=["R16"], writes=["WINR"]))
        while pend:
            pend.pop(0)()
        P.flush()

    prep()
    if stop_after == "prep":
        return nc

    o = 0
    def take(nbytes):
        nonlocal o
        r = o
        o += nbytes
        return r
    O_KT = take(2 * SMAX)
    O_VO = take(NCHMAX * 2 * 128 * 2)
    O_DEC = take(4 * NCHMAX * 4)
    O_MAIN = o
    KT = A.view(O_KT, BF16, [128, SMAX])
    VO = A.view(O_VO, BF16, [128, NCHMAX, 2, 128])
    DEC = A.view(O_DEC, F32, [128, 4, NCHMAX, 1])
    XT = A.view(take(16 * KiB), F32, [128, 8, 512])
    XN = A.view(take(8 * KiB), BF16, [128, 8, 512])
    O_BIG = take(22 * KiB)
    ACTT = A.view(O_BIG, BF16, [128, 22, 512])
    XTOK = A.view(O_BIG, F32, [128, 4, 1024])
    SQ = A.view(O_BIG + 16 * KiB, BF16, [128, 4, 512])
    RS = A.view(take(2 * KiB), F32, [128, 512])
    TMPN = A.view(take(2 * KiB), F32, [128, 512])
    SG = A.view(take(4 * KiB), F32, [128, 2, 512])
    NW = 8
    WR = [A.view(take(4 * KiB), BF16, [128, 2048]) for _ in range(NW)]
    OC = A.view(take(8 * KiB), BF16, [128, 8, 512])
    CS = A.view(take(4 * KiB), F32, [128, 2, 512])
    U32 = A.view(take(2 * KiB), F32, [128, 512])
    QN32 = A.view(take(2 * KiB), F32, [128, 512])
    T132 = A.view(take(2 * KiB), F32, [128, 512])
    T232 = A.view(take(2 * KiB), F32, [128, 512])
    QF = A.view(take(2 * KiB), BF16, [128, 2, 512])
    SQ2 = A.view(take(1 * KiB), BF16, [128, 512])
    GLOW = A.view(take(1 * KiB), BF16, [128, 512])
    L32 = A.view(take(2 * KiB), F32, [128, 512])
    PP32 = A.view(take(2 * KiB), F32, [128, 512])
    Z32 = A.view(take(2 * KiB), F32, [128, 512])
    TZ32 = A.view(take(2 * KiB), F32, [128, 512])
    EQ32 = A.view(take(2 * KiB), F32, [128, 512])
    EK32 = A.view(take(2 * KiB), F32, [128, 512])
    QD16 = A.view(take(2 * KiB), BF16, [128, 2, 512])
    KD16 = A.view(take(2 * KiB), BF16, [128, 2, 512])
    KDS16 = A.view(take(2 * KiB), BF16, [128, 2, 512])
    KDT16 = A.view(take(2 * KiB), BF16, [128, 2, 512])
    SR32 = A.view(take(4 * KiB), F32, [128, 2, 512])
    VLT = A.view(take(4 * KiB), BF16, [128, 4, 512])
    assert o <= ARENA_BYTES, o
    O_END_MAIN = o

    carry = []

    def run_carry():
        while carry:
            carry.pop(0)()

    def bigkeys(j0, j1):
        return [("big", j) for j in range(j0, j1)]

    class WS:
        def __init__(self):
            self.plan = []
            self.loaded = 0
            self.next = 0

        def add(self, ap2d, n):
            self.plan.append((ap2d, n))

        def _load(self, i):
            ap2d, n = self.plan[i]
            s = i % NW
            dma(WR[s][:, 0:n], ap2d, writes=[("wr", s)])

        def get(self):
            i = self.next
            self.next += 1
            while self.loaded < min(len(self.plan), i + 5):
                self._load(self.loaded)
                self.loaded += 1
            assert i < self.loaded
            s = i % NW
            return WR[s], ("wr", s)

    def plan_ffn(ws, l, f):
        for j in range(JC):
            ws.add(WGU[l][f][j], 2048)
        for m in range(8):
            ws.add(WDN[l][f][m][:, 0:1408], 1408)
            ws.add(WDN[l][f][m][:, 1408:2816], 1408)

    def plan_front(ws, l):
        plan_ffn(ws, l, 0)
        def pl_(pc):
            ws.add(WINL[l][2 * pc:2 * pc + 2].rearrange("t p n -> p t n"), 2048)
        wr_ = WINR[l].rearrange("p (kc n) -> p kc n", kc=8)
        pl_(0); pl_(7); pl_(1); pl_(8); pl_(2)
        ws.add(wr_[:, :, 0:128], 1024)
        pl_(3)
        ws.add(wr_[:, 0:4, 128:640], 2048)
        ws.add(wr_[:, 4:8, 128:640], 2048)
        pl_(4); pl_(5); pl_(6)

    def plan_back(ws, l):
        for m in range(8):
            ws.add(WOUT[l][m], 1024)
        plan_ffn(ws, l, 1)

    def _load(self, i):
        ap, n = self.plan[i]
        s = i % NW
        dst = WR[s][:, 0:n]
        if len(ap.shape) == 3:
            dst = dst.rearrange("p (a b) -> p a b", a=ap.shape[1])
        dma(dst, ap, writes=[("wr", s)])
    WS._load = _load

    def rmsnorm(which, l, dst16=None):
        pn, pk = psn()
        for kc in range(8):
            if kc % 2 == 0:
                P.op("act", lambda e, kc=kc: e.activation(out=XN[:, kc, :], in_=XT[:, kc, :], func=AF.Square),
                     reads=[("XT", kc)], writes=[("XN", kc)])
            else:
                P.op("dve", lambda e, kc=kc: e.tensor_tensor(out=XN[:, kc, :], in0=XT[:, kc, :], in1=XT[:, kc, :], op=ALU.mult),
                     reads=[("XT", kc)], writes=[("XN", kc)])
        for kc in range(8):
            P.op("pe", lambda e, kc=kc: e.matmul(pn[:], lhsT=ONES16[:], rhs=XN[:, kc, :], start=(kc == 0), stop=(kc == 7)),
                 reads=[("XN", kc), "ONES16"], writes=[pk])
        P.op("act", lambda e: e.activation(out=TMPN, in_=pn[:], func=AF.Ln, scale=1.0 / D, bias=EPSC[:]), reads=[pk], writes=["TMPN"])
        P.op("act", lambda e: e.activation(out=RS, in_=TMPN, func=AF.Exp, scale=-0.5), reads=["TMPN"], writes=["RS"])
        for kc in range(8):
            if dst16 is not None:
                P.op("dve", lambda e, kc=kc: e.scalar_tensor_tensor(out=XN[:, kc, :], in0=XT[:, kc, :], scalar=gcol(which, l, kc),
                                                                    in1=RS, op0=ALU.mult, op1=ALU.mult),
                     reads=[("XT", kc), "RS", "SV"], writes=[("XN", kc)])
            else:
                P.op("dve", lambda e, kc=kc: e.scalar_tensor_tensor(out=XT[:, kc, :], in0=XT[:, kc, :], scalar=gcol(which, l, kc),
                                                                    in1=RS, op0=ALU.mult, op1=ALU.mult),
                     reads=[("XT", kc), "RS", "SV"], writes=[("XT", kc)])

    def ffn(ws):
        for j in range(JC):
            w, wk = ws.get()
            wv = w[:, 0:2048].rearrange("p (t kc c) -> p t kc c", t=2, kc=8)
            pg, pgk = psn()
            pu, puk = psn()
            for kc in range(8):
                P.op("pe", lambda e, kc=kc, wv=wv, pg=pg: e.matmul(pg[:], lhsT=wv[:, 0, kc, :], rhs=XN[:, kc, :], start=(kc == 0), stop=(kc == 7)),
                     reads=[wk, ("XN", kc)], writes=[pgk])
            for kc in range(8):
                P.op("pe", lambda e, kc=kc, wv=wv, pu=pu: e.matmul(pu[:], lhsT=wv[:, 1, kc, :], rhs=XN[:, kc, :], start=(kc == 0), stop=(kc == 7)),
                     reads=[wk, ("XN", kc)], writes=[puk])
            r = j % 2
            P.op("act", lambda e, r=r, pg=pg: e.activation(out=SG[:, r, :], in_=pg[:], func=AF.Silu), reads=[pgk], writes=[("SG", r)])
            P.op("dve", lambda e, r=r, pu=pu, j=j: e.tensor_tensor(out=ACTT[:, j, :], in0=SG[:, r, :], in1=pu[:], op=ALU.mult),
                 reads=[("SG", r), puk], writes=[("big", j)])
        for m in range(8):
            wa, wak = ws.get()
            wb, wbk = ws.get()
            pd, pdk = psn()
            for kc in range(22):
                wsrc = wa if kc < 11 else wb
                wkk = wak if kc < 11 else wbk
                kk = kc if kc < 11 else kc - 11
                P.op("pe", lambda e, kc=kc, kk=kk, wsrc=wsrc, pd=pd: e.matmul(pd[:], lhsT=wsrc[:, kk * 128:(kk + 1) * 128], rhs=ACTT[:, kc, :],
                                                                             start=(kc == 0), stop=(kc == 21)),
                     reads=[wkk, ("big", kc)], writes=[pdk])
            P.op("dve", lambda e, m=m, pd=pd: e.scalar_tensor_tensor(out=XT[:, m, :], in0=pd[:], scalar=0.5, in1=XT[:, m, :],
                                                                     op0=ALU.mult, op1=ALU.add),
                 reads=[pdk, ("XT", m)], writes=[("XT", m)])

    def load_x_dma(x_ap, t0):
        xv = x_ap[t0:t0 + 512, :].rearrange("(g p) n -> p g n", p=128)
        dma(XTOK[:, 0:2, :], xv[:, 0:2, :], writes=bigkeys(0, 8))
        dma(XTOK[:, 2:4, :], xv[:, 2:4, :], writes=bigkeys(8, 16))

    def load_x(x_ap, t0):
        for kc in range(8):
            pt, pk = psn()
            for g in range(4):
                P.op("pe", lambda e, g=g, kc=kc, pt=pt: e.transpose(out=pt[:, g * 128:(g + 1) * 128], in_=XTOK[:, g, kc * 128:(kc + 1) * 128], identity=ID32[:]),
                     reads=bigkeys(4 * g, 4 * g + 4) + ["ID32"], writes=[pk])
            if kc % 2 == 0:
                P.op("act", lambda e, kc=kc, pt=pt: e.activation(out=XT[:, kc, :], in_=pt[:], func=AF.Copy), reads=[pk], writes=[("XT", kc)])
            else:
                P.op("dve", lambda e, kc=kc, pt=pt: e.tensor_copy(out=XT[:, kc, :], in_=pt[:]), reads=[pk], writes=[("XT", kc)])

    def store_y(y_ap, t0):
        yv = y_ap[t0:t0 + 512, :].rearrange("(g p) n -> p g n", p=128)
        for g in range(4):
            for hh in range(2):
                pt, pk = psn()
                for k4 in range(4):
                    kc = hh * 4 + k4
                    P.op("pe", lambda e, g=g, kc=kc, k4=k4, pt=pt: e.transpose(out=pt[:, k4 * 128:(k4 + 1) * 128], in_=XT[:, kc, g * 128:(g + 1) * 128], identity=ID32[:]),
                         reads=[("XT", kc), "ID32"], writes=[pk])
                if hh == 0:
                    P.op("act", lambda e, g=g, pt=pt: e.activation(out=XTOK[:, g, 0:512], in_=pt[:], func=AF.Copy), reads=[pk], writes=bigkeys(4 * g, 4 * g + 2))
                else:
                    P.op("dve", lambda e, g=g, pt=pt: e.tensor_copy(out=XTOK[:, g, 512:1024], in_=pt[:]), reads=[pk], writes=bigkeys(4 * g + 2, 4 * g + 4))
        dma(yv[:, 0:2, :], XTOK[:, 0:2, :], reads=bigkeys(0, 8), writes=["Y"])
        dma(yv[:, 2:4, :], XTOK[:, 2:4, :], reads=bigkeys(8, 16), writes=["Y"])

    def front(ws, l, b, hook=None):
        t0 = b * 512
        rmsnorm(0, l, XN)
        ffn(ws)
        dma(HS.rearrange("kc p t -> p kc t")[:, :, t0:t0 + 512], XT, reads=[("XT", kc) for kc in range(8)], writes=[("HS", b)])
        rmsnorm(1, l, XN)
        if hook is not None:
            hook()
        dma(CS[:, 0, :], c_cos[:, t0:t0 + 512], writes=["CS"])
        dma(CS[:, 1, :], c_sin[:, t0:t0 + 512], writes=["CS"])
        xnk = [("XN", kc) for kc in range(8)]

        def proj(w, wk, ti, M=128):
            pp, ppk = psn()
            wv = w[:, 0:2048].rearrange("p (t kc c) -> p t kc c", t=2, kc=8)
            for kc in range(8):
                P.op("pe", lambda e, kc=kc: e.matmul(pp[0:M, :], lhsT=wv[:, ti, kc, 0:M], rhs=XN[:, kc, :], start=(kc == 0), stop=(kc == 7)),
                     reads=[wk, ("XN", kc)], writes=[ppk])
            return pp, ppk

        def qk_post(pp, ppk, gc, dest, dkeys):
            P.op("act", lambda e: e.activation(out=U32, in_=pp[:], func=AF.Copy), reads=[ppk], writes=["U32"])
            P.op("act", lambda e: e.activation(out=SQ2, in_=pp[:], func=AF.Square), reads=[ppk], writes=["SQ2"])
            p2, p2k = psn()
            P.op("pe", lambda e: e.matmul(p2[:], lhsT=BLK16[:], rhs=SQ2, start=True, stop=True), reads=["SQ2", "BLK16"], writes=[p2k])
            P.op("act", lambda e: e.activation(out=TMPN, in_=p2[:], func=AF.Ln, scale=1.0 / 64, bias=EPSC[:]), reads=[p2k], writes=["TMPN"])
            P.op("act", lambda e: e.activation(out=RS, in_=TMPN, func=AF.Exp, scale=-0.5), reads=["TMPN"], writes=["RS"])
            P.op("dve", lambda e: e.scalar_tensor_tensor(out=QN32, in0=U32, scalar=SV[:, gc:gc + 1], in1=RS, op0=ALU.mult, op1=ALU.mult),
                 reads=["U32", "RS", "SV"], writes=["QN32"])
            p3, p3k = psn()
            P.op("pe", lambda e: e.matmul(p3[:], lhsT=ROT32[:], rhs=QN32, start=True, stop=True), reads=["QN32", "ROT32"], writes=[p3k])
            P.op("pool", lambda e: e.tensor_tensor(out=T132, in0=QN32, in1=CS[:, 0, :], op=ALU.mult), reads=["QN32", "CS"], writes=["T132"])
            P.op("dve", lambda e: e.tensor_tensor(out=T232, in0=p3[:], in1=CS[:, 1, :], op=ALU.mult), reads=[p3k, "CS"], writes=["T232"])
            P.op("dve", lambda e: e.tensor_tensor(out=dest, in0=T132, in1=T232, op=ALU.add), reads=["T132", "T232"], writes=dkeys)

        def qk_s1(pp, ppk):
            P.op("act", lambda e: e.activation(out=U32, in_=pp[:], func=AF.Copy), reads=[ppk], writes=["U32"])
            P.op("act", lambda e: e.activation(out=SQ2, in_=pp[:], func=AF.Square), reads=[ppk], writes=["SQ2"])

        def qk_p2():
            p2, p2k = psn()
            P.op("pe", lambda e: e.matmul(p2[:], lhsT=BLK16[:], rhs=SQ2, start=True, stop=True), reads=["SQ2", "BLK16"], writes=[p2k])
            return p2, p2k

        def qk_s2(p2, p2k, gc):
            P.op("act", lambda e: e.activation(out=TMPN, in_=p2[:], func=AF.Ln, scale=1.0 / 64, bias=EPSC[:]), reads=[p2k], writes=["TMPN"])
            P.op("act", lambda e: e.activation(out=RS, in_=TMPN, func=AF.Exp, scale=-0.5), reads=["TMPN"], writes=["RS"])
            P.op("dve", lambda e: e.scalar_tensor_tensor(out=QN32, in0=U32, scalar=SV[:, gc:gc + 1], in1=RS, op0=ALU.mult, op1=ALU.mult),
                 reads=["U32", "RS", "SV"], writes=["QN32"])

        def qk_p3():
            p3, p3k = psn()
            P.op("pe", lambda e: e.matmul(p3[:], lhsT=ROT32[:], rhs=QN32, start=True, stop=True), reads=["QN32", "ROT32"], writes=[p3k])
            return p3, p3k

        def qk_s3(p3, p3k, dest, dkeys):
            P.op("dve", lambda e: e.tensor_tensor(out=T132, in0=QN32, in1=CS[:, 0, :], op=ALU.mult), reads=["QN32", "CS"], writes=["T132"])
            P.op("dve", lambda e: e.tensor_tensor(out=T232, in0=p3[:], in1=CS[:, 1, :], op=ALU.mult), reads=[p3k, "CS"], writes=["T232"])
            P.op("dve", lambda e: e.tensor_tensor(out=dest, in0=T132, in1=T232, op=ALU.add), reads=["T132", "T232"], writes=dkeys)

        def qa_dest(c):
            r = c % 2
            return QF[:, r, :], [("QF", r)]

        def qa_store(c):
            r = c % 2
            dma(QS[c][:, t0:t0 + 512], QF[:, r, :], reads=[("QF", r)], writes=[("QS", b)])

        def rl(w_, wk_, ti, h):
            pp, ppk = proj(w_, wk_, ti)
            r = h % 2
            P.op("act", lambda e, r=r, pp=pp: e.activation(out=SR32[:, r, :], in_=pp[:], func=AF.Silu), reads=[ppk], writes=[("SR", r)])
            dma(GR[h][:, t0:t0 + 512], SR32[:, r, :], reads=[("SR", r)], writes=[("GR", b)])

        def va(w_, wk_):
            wva = w_[:, 0:1024].rearrange("p (kc n) -> p kc n", kc=8)
            pv, pvk = psn()
            for g in range(4):
                for kc in range(8):
                    P.op("pe", lambda e, g=g, kc=kc: e.matmul(pv[:, g * 128:(g + 1) * 128], lhsT=XN[:, kc, g * 128:(g + 1) * 128], rhs=wva[:, kc, :],
                                                              start=(kc == 0), stop=(kc == 7)),
                         reads=[wk_, ("XN", kc)], writes=[pvk])
            P.op("act", lambda e: e.activation(out=VO[:, 4 * b:4 * b + 4, :, 0:64], in_=pv[:].rearrange("p (g k n) -> p g k n", g=4, k=2), func=AF.Copy),
                 reads=[pvk], writes=[("VO", b)])

        def vl(g, wl0, wl0k, wl1, wl1k):
            pl, plk = psn()
            for kc in range(8):
                wsrc = wl0 if kc < 4 else wl1
                wkk = wl0k if kc < 4 else wl1k
                wvv = wsrc[:, 0:2048].rearrange("p (kc n) -> p kc n", kc=4)
                P.op("pe", lambda e, kc=kc, wvv=wvv: e.matmul(pl[:], lhsT=XN[:, kc, g * 128:(g + 1) * 128], rhs=wvv[:, kc % 4, :],
                                                               start=(kc == 0), stop=(kc == 7)),
                     reads=[wkk, ("XN", kc)], writes=[plk])
            if g % 2 == 0:
                P.op("act", lambda e: e.activation(out=VLT[:, g, :], in_=pl[:], func=AF.Copy), reads=[plk], writes=[("VLT", g)])
            else:
                P.op("dve", lambda e: e.tensor_copy(out=VLT[:, g, :], in_=pl[:]), reads=[plk], writes=[("VLT", g)])
            if g == 3:
                dma(GV[4 * b:4 * b + 4].rearrange("c p n -> p c n"), VLT, reads=[("VLT", g_) for g_ in range(4)], writes=[("GV", b)])

        v4 = lambda t, lo, hi: t[lo:hi, :].rearrange("p (c n) -> p c n", c=4)
        wg = WG16[:].rearrange("p (l h n) -> p l h n", l=L, h=4)

        def gla_a(w_, wk_, h):
            pq, pqk = proj(w_, wk_, 0)
            pkk_, pkkk = proj(w_, wk_, 1)
            px, pxk = psn()
            P.op("pe", lambda e: e.matmul(px[:], lhsT=wg[0:32, l, h, :], rhs=GLOW[0:32, :], start=True, stop=True),
                 reads=["GLOW", "WG16"], writes=[pxk])
            bcol = OFF_BG + 4 * l + h
            P.op("act", lambda e: e.activation(out=TZ32, in_=px[:], func=AF.Exp, scale=-1.0, bias=SV[:, bcol:bcol + 1]),
                 reads=[pxk, "SV"], writes=["TZ32"])
            P.op("act", lambda e: e.activation(out=L32, in_=TZ32, func=AF.Ln, bias=1.0), reads=["TZ32"], writes=["L32"])
            P.op("dve", lambda e: e.tensor_tensor_scan(out=PP32, data0=RESET[:], data1=L32, initial=0.0, op0=ALU.mult, op1=ALU.add),
                 reads=["L32", "RESET"], writes=["PP32"])
            P.op("act", lambda e: e.activation(out=Z32[0:64, :], in_=PP32[0:64, :], func=AF.Copy), reads=["PP32"], writes=["Z32"])
            P.op("dve", lambda e: e.tensor_tensor(out=TZ32[64:128, :], in0=L32[64:128, :], in1=PP32[64:128, :], op=ALU.subtract),
                 reads=["L32", "PP32"], writes=["TZ32"])
            P.op("dve", lambda e: e.tensor_tensor(out=v4(Z32, 64, 128), in0=v4(TZ32, 64, 128),
                                                  in1=v4(PP32, 64, 128)[:, :, 127:128].to_broadcast([64, 4, 128]), op=ALU.add),
                 reads=["TZ32", "PP32"], writes=["Z32"])
            P.op("act", lambda e: e.activation(out=EQ32, in_=Z32, func=AF.Exp, scale=-1.0 / 16), reads=["Z32"], writes=["EQ32"])
            P.op("act", lambda e: e.activation(out=EK32, in_=Z32, func=AF.Exp, scale=1.0 / 16), reads=["Z32"], writes=["EK32"])
            P.op("dve", lambda e: e.tensor_copy(out=DEC[0:64, h, 4 * b:4 * b + 4, :], in_=v4(EQ32, 0, 64)[:, :, 127:128]),
                 reads=["EQ32"], writes=[("DEC", h, b)])
            P.op("dve", lambda e: e.tensor_copy(out=DEC[64:128, h, 4 * b:4 * b + 4, :], in_=v4(EQ32, 64, 128)[:, :, 0:1]),
                 reads=["EQ32"], writes=[("DEC", h, b)])
            r = h % 2
            P.op("dve", lambda e: e.scalar_tensor_tensor(out=QD16[:, r, :], in0=pq[:], scalar=0.125, in1=EQ32, op0=ALU.mult, op1=ALU.mult),
                 reads=[pqk, "EQ32"], writes=[("QD", r)])
            P.op("dve", lambda e: e.tensor_tensor(out=KD16[:, r, :], in0=pkk_[:], in1=EK32, op=ALU.mult),
                 reads=[pkkk, "EK32"], writes=[("KD", r)])
            P.op("dve", lambda e: e.tensor_tensor(out=KDS16[:, r, :].rearrange("p (c n) -> p c n", c=4),
                                                  in0=KD16[:, r, :].rearrange("p (c n) -> p c n", c=4),
                                                  in1=DEC[:, h, 4 * b:4 * b + 4, :].to_broadcast([128, 4, 128]), op=ALU.mult),
                 reads=[("KD", r), ("DEC", h, b)], writes=[("KDS", r)])
            dma(GQ[h][:, t0:t0 + 512], QD16[:, r, :], reads=[("QD", r)], writes=[("GQ", b)])
            dma(GK[h][:, t0:t0 + 512], KD16[:, r, :], reads=[("KD", r)], writes=[("GK", b)])

        def gla_b(h):
            r = h % 2
            ptt, ptk = psn()
            ptb = ptt[:].bitcast(BF16)
            for ch in range(4):
                P.op("pe", lambda e, ch=ch: e.transpose(out=ptb[:, ch * 128:(ch + 1) * 128], in_=KDS16[:, r, ch * 128:(ch + 1) * 128], identity=ID16[:]),
                     reads=[("KDS", r), "ID16"], writes=[ptk])
            P.op("act", lambda e: e.activation(out=KDT16[:, r, :], in_=ptb[:, 0:512], func=AF.Copy), reads=[ptk], writes=[("KDT", r)])
            dma(GKT[h][4 * b:4 * b + 4].rearrange("c p n -> p c n"), KDT16[:, r, :].rearrange("p (c n) -> p c n", c=4),
                reads=[("KDT", r)], writes=[("GKT", b)])

        gq = OFF_QK + 0 * L + l
        gk = OFF_QK + 1 * L + l
        w0, w0k = ws.get()
        pp, ppk = proj(w0, w0k, 0, M=32)
        P.op("act", lambda e, pp=pp: e.activation(out=GLOW[0:32, :], in_=pp[0:32, :], func=AF.Copy), reads=[ppk], writes=["GLOW"])
        ppa, ppak = proj(w0, w0k, 1)
        qk_s1(ppa, ppak)
        w7, w7k = ws.get()
        rl(w7, w7k, 0, 0)
        p2, p2k = qk_p2()
        qk_s2(p2, p2k, gq)
        w1, w1k = ws.get()
        ppb, ppbk = proj(w1, w1k, 0)
        rl(w7, w7k, 1, 1)
        p3, p3k = qk_p3()
        qk_s3(p3, p3k, *qa_dest(0))
        qa_store(0)
        qk_s1(ppb, ppbk)
        ppc, ppck = proj(w1, w1k, 1)
        p2, p2k = qk_p2()
        qk_s2(p2, p2k, gq)
        w8, w8k = ws.get()
        rl(w8, w8k, 0, 2)
        p3, p3k = qk_p3()
        qk_s3(p3, p3k, *qa_dest(1))
        qa_store(1)
        qk_s1(ppc, ppck)
        w2, w2k = ws.get()
        ppd, ppdk = proj(w2, w2k, 0)
        p2, p2k = qk_p2()
        qk_s2(p2, p2k, gq)
        rl(w8, w8k, 1, 3)
        p3, p3k = qk_p3()
        qk_s3(p3, p3k, *qa_dest(2))
        qa_store(2)
        qk_s1(ppd, ppdk)
        ppe, ppek = proj(w2, w2k, 1)
        p2, p2k = qk_p2()
        qk_s2(p2, p2k, gq)
        wv_, wvk_ = ws.get()
        va(wv_, wvk_)
        p3, p3k = qk_p3()
        qk_s3(p3, p3k, *qa_dest(3))
        qa_store(3)
        qk_s1(ppe, ppek)
        w3, w3k = ws.get()
        gla_a(w3, w3k, 0)
        p2, p2k = qk_p2()
        qk_s2(p2, p2k, gk)
        wl0, wl0k = ws.get()
        wl1, wl1k = ws.get()
        vl(0, wl0, wl0k, wl1, wl1k)
        p3, p3k = qk_p3()
        qk_s3(p3, p3k, KT[:, t0:t0 + 512], [("KT", b)])
        vl(1, wl0, wl0k, wl1, wl1k)
        w4, w4k = ws.get()
        gla_a(w4, w4k, 1)
        gla_b(0)
        vl(2, wl0, wl0k, wl1, wl1k)
        vl(3, wl0, wl0k, wl1, wl1k)
        w5, w5k = ws.get()
        gla_a(w5, w5k, 2)
        gla_b(1)
        w6, w6k = ws.get()
        gla_a(w6, w6k, 3)
        gla_b(2)
        carry.append(lambda: gla_b(3))

    def back_loads(b, oc=True, xt=True):
        t0 = b * 512
        if oc:
            dma(OC[:, 0:4, :], OAS.rearrange("a p t -> p a t")[:, :, t0:t0 + 512], writes=[("OC", 0)])
            dma(OC[:, 4:8, :], OG.rearrange("a p t -> p a t")[:, :, t0:t0 + 512], writes=[("OC", 1)])
        if xt:
            dma(XT, HS.rearrange("kc p t -> p kc t")[:, :, t0:t0 + 512], reads=[("HS", b)], writes=[("XT", kc) for kc in range(8)])

    def back(ws, l, b, hook=None):
        t0 = b * 512
        for m in range(8):
            w, wk = ws.get()
            wv = w[:, 0:1024].rearrange("p (kc c) -> p kc c", kc=8)
            pd, pdk = psn()
            for kc in range(8):
                P.op("pe", lambda e, kc=kc, wv=wv, pd=pd: e.matmul(pd[:], lhsT=wv[:, kc, :], rhs=OC[:, kc, :], start=(kc == 0), stop=(kc == 7)),
                     reads=[wk, ("OC", kc // 4)], writes=[pdk])
            P.op("dve", lambda e, m=m, pd=pd: e.tensor_tensor(out=XT[:, m, :], in0=pd[:], in1=XT[:, m, :], op=ALU.add),
                 reads=[pdk, ("XT", m)], writes=[("XT", m)])
        run_carry()
        if hook is not None:
            hook()
        rmsnorm(2, l, XN)
        ffn(ws)
        rmsnorm(3, l, None)

    def attention(S):
        NB, NCH = S // 512, S // 128
        o2 = O_MAIN
        QTt = A.view(o2, BF16, [128, 2, 512]); o2 += 2 * KiB
        PT = A.view(o2, BF16, [128, 8, 512]); o2 += 8 * KiB
        ACCS = A.view(o2, F32, [128, 2, 512]); o2 += 4 * KiB
        RCP = A.view(o2, F32, [128, 2, 512]); o2 += 4 * KiB
        OAb = A.view(o2, BF16, [128, 2, 512]); o2 += 2 * KiB
        iters = [(qb, c) for qb in range(NB) for c in range(4)]
        steps = [(k, sc) for k in range(len(iters)) for sc in range(NCH)]
        pt_slot = {}
        pstate = {"pti": 0}

        def acc_of(k):
            r = k % 2
            return [(ps[2 * r], ("ps", 2 * r)), (ps[2 * r + 1], ("ps", 2 * r + 1))]

        def st_of(i):
            sp_ = 4 + 2 * (i % 2)
            return [(ps[sp_], ("ps", sp_)), (ps[sp_ + 1], ("ps", sp_ + 1))]

        def emit_mm1(i):
            k, sc = steps[i]
            qb, c = iters[k]
            r = k % 2
            if sc == 0:
                def ld_(kk):
                    qb_, c_ = iters[kk]
                    dma(QTt[:, kk % 2, :], QS[c_][:, qb_ * 512:qb_ * 512 + 512], writes=[("QTt", kk % 2)])
                if k == 0:
                    ld_(0)
                if k + 1 < len(iters):
                    ld_(k + 1)
            st = st_of(i)
            for hf in range(2):
                lo, hi = 64 * hf, 64 * hf + 64
                P.op("pe", lambda e, hf=hf, lo=lo, hi=hi, sc=sc, r=r, st=st: e.matmul(st[hf][0][:], lhsT=KT[lo:hi, sc * 128:(sc + 1) * 128],
                                                                                    rhs=QTt[lo:hi, r, :], start=True, stop=True),
                     reads=["KTall", ("QTt", r)], writes=[st[hf][1]])

        def emit_exp(i):
            st = st_of(i)
            sl = []
            for hf in range(2):
                s_ = pstate["pti"] % 8
                pstate["pti"] += 1
                sl.append(s_)
                P.op("act", lambda e, hf=hf, s_=s_, st=st: e.activation(out=PT[:, s_, :], in_=st[hf][0][:], func=AF.Exp, scale=0.125),
                     reads=[st[hf][1]], writes=[("PT", s_)])
            pt_slot[i] = sl

        def emit_mm2(i):
            k, sc = steps[i]
            acc = acc_of(k)
            for hf in range(2):
                s_ = pt_slot[i][hf]
                P.op("pe", lambda e, hf=hf, s_=s_, sc=sc, acc=acc: e.matmul(acc[hf][0][:], lhsT=VO[:, sc, hf, :], rhs=PT[:, s_, :],
                                                                           start=(sc == 0), stop=(sc == NCH - 1)),
                     reads=["VOall", ("PT", s_)], writes=[acc[hf][1]])

        def fin_a(k):
            acc = acc_of(k)
            for hf in range(2):
                ap_, ak = acc[hf]
                P.op("dve", lambda e, hf=hf, ap_=ap_: e.tensor_copy(out=ACCS[:, hf, :], in_=ap_[:]), reads=[ak], writes=[("ACCS", hf)])
                P.op("dve", lambda e, hf=hf: e.reciprocal(out=RCP[64:128, hf, :], in_=ACCS[64:128, hf, :]), reads=[("ACCS", hf)], writes=[("RCP", hf)])

        def fin_b(k):
            qb, c = iters[k]
            acc = acc_of(k)
            for hf in range(2):
                head = c + 4 * hf
                ap_, ak = acc[hf]
                P.op("pe", lambda e, hf=hf, ap_=ap_: e.matmul(ap_[0:64, :], lhsT=ID32[64:128, 64:128], rhs=RCP[64:128, hf, :], start=True, stop=True),
                     reads=[("RCP", hf), "ID32"], writes=[ak])
                P.op("dve", lambda e, hf=hf, ap_=ap_: e.tensor_tensor(out=OAb[0:64, hf, :], in0=ACCS[0:64, hf, :], in1=ap_[0:64, :], op=ALU.mult),
                     reads=[("ACCS", hf), ak], writes=[("OAb", hf)])
                po_ = (head % 2) * 64
                dma(OAS[head // 2][po_:po_ + 64, qb * 512:qb * 512 + 512], OAb[0:64, hf, :], reads=[("OAb", hf)], writes=["OAS"])

        nst = len(steps)
        emit_mm1(0)
        pending_fin = None
        for i in range(nst):
            k, sc = steps[i]
            if i + 1 < nst:
                emit_mm1(i + 1)
            emit_exp(i)
            emit_mm2(i)
            if pending_fin is not None and sc == min(8, NCH - 1):
                fin_b(pending_fin)
                pending_fin = None
            if sc == NCH - 1:
                fin_a(k)
                pending_fin = k
        if pending_fin is not None:
            fin_b(pending_fin)

    def gla(S, l):
        NB, NCH = S // 512, S // 128
        o2 = O_MAIN
        if 6 * NCH * 128 <= O_DEC:
            DSB = A.view(0, F32, [128, NCH, 128])
            STATE16 = A.view(4 * NCH * 128, BF16, [128, NCH, 128])
        else:
            DSB = A.view(o2, F32, [128, NCH, 128]); o2 += 4 * NCH * 128
            STATE16 = A.view(o2, BF16, [128, NCH, 128]); o2 += 2 * NCH * 128
        GQh = A.view(o2, BF16, [128, S]); o2 += 2 * S
        GKh = A.view(o2, BF16, [128, S]); o2 += 2 * S
        GKTh = A.view(o2, BF16, [128, NCH, 128]); o2 += 2 * S
        GVh = A.view(o2, BF16, [128, NCH, 128]); o2 += 2 * S
        SALL = A.view(o2, F32, [128, NCH, 128]); o2 += 4 * NCH * 128
        AT2 = A.view(o2, BF16, [128, 3, 2, 512]); o2 += 6 * KiB
        OSQ = A.view(o2, BF16, [128, 2, 512]); o2 += 2 * KiB
        SRb = A.view(o2, F32, [128, 3, 512]); o2 += 6 * KiB
        T1 = A.view(o2, F32, [128, 512]); o2 += 2 * KiB
        OGb = A.view(o2, BF16, [128, 2, 512]); o2 += 2 * KiB
        TN2 = A.view(o2, F32, [128, 512]); o2 += 2 * KiB
        RS2 = A.view(o2, F32, [128, 512]); o2 += 2 * KiB
        assert o2 <= ARENA_BYTES, o2
        for h in range(4):
            cst = min(16, NCH)
            for c0 in range(0, NCH, cst):
                dma(GKTh[:, c0:c0 + cst, :], GKT[h][c0:c0 + cst].rearrange("c p n -> p c n"), writes=[("GKTh", c0 // cst)])
                dma(GVh[:, c0:c0 + cst, :], GV[c0:c0 + cst, :, h * 128:(h + 1) * 128].rearrange("c p n -> p c n"), writes=[("GVh", c0 // cst)])
            step = min(2048, S)
            for c0 in range(0, S, step):
                dma(GQh[:, c0:c0 + step], GQ[h][:, c0:c0 + step], writes=[("GQh", c0 // step)])
                dma(GKh[:, c0:c0 + step], GK[h][:, c0:c0 + step], writes=[("GKh", c0 // step)])
            P.op("pool", lambda e: e.memset(SALL[0:64, 0, :], 0.0), writes=[("SA", 0)])
            P.op("pool", lambda e: e.memset(SALL[64:128, NCH - 1, :], 0.0), writes=[("SB", NCH - 1)])
            for cb in range(NB):
                pd, pdk = psn()
                for ci in range(4):
                    c = 4 * cb + ci
                    P.op("pe", lambda e, c=c, ci=ci, pd=pd: e.matmul(pd[:, ci * 128:(ci + 1) * 128], lhsT=GKTh[:, c, :], rhs=GVh[:, c, :], start=True, stop=True),
                         reads=[("GKTh", c // cst), ("GVh", c // cst)], writes=[pdk])
                P.op("act", lambda e, cb=cb, pd=pd: e.activation(out=DSB[:, 4 * cb:4 * cb + 4, :], in_=pd[:].rearrange("p (c n) -> p c n", c=4), func=AF.Copy),
                     reads=[pdk], writes=[("DSB", cb)])

            def fwd_step(c):
                P.op("dve", lambda e, c=c, h=h: e.scalar_tensor_tensor(out=SALL[0:64, c + 1, :], in0=SALL[0:64, c, :], scalar=DEC[0:64, h, c, :],
                                                                     in1=DSB[0:64, c, :], op0=ALU.mult, op1=ALU.add),
                     reads=[("SA", c), ("DSB", c // 4), "DECall"], writes=[("SA", c + 1)])

            def bwd_step(c):
                P.op("dve", lambda e, c=c, h=h: e.scalar_tensor_tensor(out=SALL[64:128, c - 1, :], in0=SALL[64:128, c, :], scalar=DEC[64:128, h, c, :],
                                                                     in1=DSB[64:128, c, :], op0=ALU.mult, op1=ALU.add),
                     reads=[("SB", c), ("DSB", c // 4), "DECall"], writes=[("SB", c - 1)])

            LEAD = min(12, max(1, (NCH - 1) // 2))
            fl = list(range(0, NCH - 1))
            bl = list(range(NCH - 1, 0, -1))
            for k in range(len(fl) + LEAD):
                if k < len(fl):
                    fwd_step(fl[k])
                if 0 <= k - LEAD < len(bl):
                    bwd_step(bl[k - LEAD])
            cst2 = min(16, NCH)
            for c0 in range(0, NCH, cst2):
                en_ = "act"
                rk = [("SA", c) for c in range(c0, c0 + cst2)] + [("SB", c) for c in range(c0, c0 + cst2)]
                wk_ = [("ST16", c) for c in range(c0, c0 + cst2)]
                if en_ == "act":
                    P.op("act", lambda e, c0=c0: e.activation(out=STATE16[:, c0:c0 + cst2, :], in_=SALL[:, c0:c0 + cst2, :], func=AF.Copy), reads=rk, writes=wk_)
                else:
                    P.op("pool", lambda e, c0=c0: e.tensor_copy(out=STATE16[:, c0:c0 + cst2, :], in_=SALL[:, c0:c0 + cst2, :]), reads=rk, writes=wk_)
            m2f = M2[:, 0:128].rearrange("p (o n) -> p o n", o=1).to_broadcast([128, 4, 128])
            m2b = M2[:, 128:256].rearrange("p (o n) -> p o n", o=1).to_broadcast([128, 4, 128])
            v4_ = lambda t: t.rearrange("p (c n) -> p c n", c=4)
            gc = OFF_GLA + l
            pos = {}

            def st3_A(cb):
                paF, pafk = psn()
                paB, pabk = psn()
                r3 = cb % 3
                r2 = cb % 2
                dma(SRb[:, r3, :], GR[h][:, cb * 512:cb * 512 + 512], writes=[("SRb", r3)])
                for ci in range(4):
                    c = 4 * cb + ci
                    cs = slice(c * 128, (c + 1) * 128)
                    osl = slice(ci * 128, (ci + 1) * 128)
                    qk_ = [("GKh", (c * 128) // step), ("GQh", (c * 128) // step)]
                    P.op("pe", lambda e, cs=cs, osl=osl: e.matmul(paF[:, osl], lhsT=GKh[0:64, cs], rhs=GQh[0:64, cs], start=True, stop=True),
                         reads=qk_, writes=[pafk])
                    P.op("pe", lambda e, cs=cs, osl=osl: e.matmul(paB[:, osl], lhsT=GKh[64:128, cs], rhs=GQh[64:128, cs], start=True, stop=True),
                         reads=qk_, writes=[pabk])
                P.op("dve", lambda e: e.tensor_tensor(out=v4_(AT2[:, r3, 0, :]), in0=v4_(paF[:]), in1=m2f, op=ALU.mult),
                     reads=[pafk, "M2"], writes=[("AT2", r3, 0)])
                P.op("dve", lambda e: e.tensor_tensor(out=v4_(AT2[:, r3, 1, :]), in0=v4_(paB[:]), in1=m2b, op=ALU.mult),
                     reads=[pabk, "M2"], writes=[("AT2", r3, 1)])

            def st3_O(cb):
                po, pok = psn()
                pos[cb] = (po, pok)
                r3 = cb % 3
                r2 = cb % 2
                for ci in range(4):
                    c = 4 * cb + ci
                    cs = slice(c * 128, (c + 1) * 128)
                    osl = slice(ci * 128, (ci + 1) * 128)
                    P.op("pe", lambda e, c=c, osl=osl: e.matmul(po[:, osl], lhsT=GVh[:, c, :], rhs=AT2[:, r3, 0, osl], start=True, stop=False),
                         reads=[("GVh", c // cst), ("AT2", r3, 0)], writes=[pok])
                    P.op("pe", lambda e, c=c, osl=osl: e.matmul(po[:, osl], lhsT=GVh[:, c, :], rhs=AT2[:, r3, 1, osl], start=False, stop=False),
                         reads=[("GVh", c // cst), ("AT2", r3, 1)], writes=[pok])
                    P.op("pe", lambda e, c=c, cs=cs, osl=osl: e.matmul(po[:, osl], lhsT=STATE16[:, c, :], rhs=GQh[:, cs], start=False, stop=True),
                         reads=[("ST16", c), ("GQh", (c * 128) // step)], writes=[pok])
                P.op("act", lambda e: e.activation(out=OSQ[:, r2, :], in_=po[:], func=AF.Square), reads=[pok], writes=[("OSQ", r2)])

            def st3_N(cb):
                po, pok = pos.pop(cb)
                r2 = cb % 2
                pn, pnk = psn()
                P.op("pe", lambda e: e.matmul(pn[:], lhsT=ONES16[:], rhs=OSQ[:, r2, :], start=True, stop=True), reads=[("OSQ", r2), "ONES16"], writes=[pnk])
                P.op("act", lambda e: e.activation(out=TN2, in_=pn[:], func=AF.Ln, scale=1.0 / 128, bias=EPSC[:]), reads=[pnk], writes=["TN2"])
                P.op("act", lambda e: e.activation(out=RS2, in_=TN2, func=AF.Exp, scale=-0.5), reads=["TN2"], writes=["RS2"])
                P.op("dve", lambda e: e.scalar_tensor_tensor(out=T1, in0=po[:], scalar=SV[:, gc:gc + 1], in1=RS2, op0=ALU.mult, op1=ALU.mult),
                     reads=[pok, "RS2", "SV"], writes=["T1"])
                r3 = cb % 3
                P.op("dve", lambda e: e.tensor_tensor(out=OGb[:, r2, :], in0=T1, in1=SRb[:, r3, :], op=ALU.mult),
                     reads=["T1", ("SRb", r3)], writes=[("OGb", r2)])
                dma(OG[h][:, cb * 512:cb * 512 + 512], OGb[:, r2, :], reads=[("OGb", r2)], writes=["OG"])

            for stp in range(NB + 2):
                if stp < NB:
                    st3_A(stp)
                if 0 <= stp - 1 < NB:
                    st3_O(stp - 1)
                if 0 <= stp - 2 < NB:
                    st3_N(stp - 2)

    for name, S in seqs:
        NB = S // 512
        P.op("pool", lambda e: e.memset(VO[:, :, :, 64:128], 1.0), writes=["VOones"])
        ws = WS()
        for b in range(NB):
            plan_front(ws, 0)
        load_x_dma(xin[name], 0)
        for b in range(NB):
            load_x(xin[name], b * 512)
            run_carry()
            hk = (lambda b=b: load_x_dma(xin[name], (b + 1) * 512)) if b + 1 < NB else None
            front(ws, 0, b, hook=hk)
        run_carry()
        P.flush()
        if stop_after == "front":
            return nc
        for l in range(L):
            attention(S)
            P.flush()
            if stop_after == "att":
                return nc
            gla(S, l)
            P.flush()
            if stop_after == "gla":
                return nc
            ws = WS()
            for b in range(NB):
                plan_back(ws, l)
                if l + 1 < L:
                    plan_front(ws, l + 1)
            if l + 1 == L:
                pass
            else:
                P.op("pool", lambda e: e.memset(VO[:, :, :, 64:128], 1.0), writes=["VOones"])
            back_loads(0)
            for b in range(NB):
                if l + 1 < L:
                    back(ws, l, b)
                    hk = (lambda b=b: back_loads(b + 1)) if b + 1 < NB else None
                    front(ws, l + 1, b, hook=hk)
                else:
                    hk = (lambda b=b: back_loads(b + 1, oc=True, xt=False)) if b + 1 < NB else None
                    back(ws, l, b, hook=hk)
                    store_y(yout[name], b * 512)
                    if b + 1 < NB:
                        back_loads(b + 1, oc=False, xt=True)
            run_carry()
            P.flush()
    return nc


def make_consts(SMAX):
    ident = np.eye(128, dtype=np.float32)
    rot = np.zeros((128, 128), np.float32)
    for hb in (0, 64):
        for i in range(16):
            rot[hb + 16 + i, hb + i] = -1.0
            rot[hb + i, hb + 16 + i] = 1.0
            rot[hb + 48 + i, hb + 32 + i] = -1.0
            rot[hb + 32 + i, hb + 48 + i] = 1.0
    blk = np.zeros((128, 128), np.float32)
    blk[0:64, 0:64] = 1.0
    blk[64:128, 64:128] = 1.0
    j = np.arange(128)[:, None]
    i = np.arange(128)[None, :]
    m2 = np.concatenate([(j <= i), (j > i)], axis=1).astype(np.float32)
    reset = np.ones((128, 512), np.float32)
    reset[:, ::128] = 0.0
    t = np.arange(SMAX)
    row = (t // 64).astype(np.float32)
    col = (t % 64).astype(np.float32)
    inv_freq = (1.0 / (np.float32(10000.0) ** (np.arange(0, 32, 2, dtype=np.float32) / np.float32(32)))).astype(np.float32)
    ang_r = row[:, None] * inv_freq[None, :]
    ang_c = col[:, None] * inv_freq[None, :]
    ang = np.concatenate([ang_r, ang_r, ang_c, ang_c], axis=-1).astype(np.float32)
    cos = np.cos(ang).astype(np.float32).T
    sin = np.sin(ang).astype(np.float32).T
    cosT = np.ascontiguousarray(np.concatenate([cos, cos], axis=0))
    sinT = np.ascontiguousarray(np.concatenate([sin, sin], axis=0))
    return {"c_ident": ident, "c_rot": rot, "c_blk": blk, "c_m2": m2, "c_reset": reset, "c_cos": cosT, "c_sin": sinT}


WNAMES = ["norm_ffn1", "w_ffn1_gu", "w_ffn1_down", "norm_mix", "w_in", "q_norm", "k_norm", "w_gate_f", "b_gate_f",
          "w_gate_b", "b_gate_b", "gla_norm", "w_out", "norm_ffn2", "w_ffn2_gu", "w_ffn2_down", "norm_out"]


def kernel(**inputs):
    xp = np.asarray(inputs["x_prompt"], dtype=np.float32)
    xs = np.asarray(inputs["x_sample"], dtype=np.float32)
    nb = xp.shape[0]
    SP, SS = xp.shape[1], xs.shape[1]
    depth = np.asarray(inputs["w_in"]).shape[0]
    nc = build([("p", SP), ("s", SS)], depth=depth)
    consts = make_consts(max(SP, SS))
    wts = {k: np.ascontiguousarray(np.asarray(inputs[k], dtype=np.float32)) for k in WNAMES}
    in_maps = []
    for i in range(nb):
        m = {"x_p": np.ascontiguousarray(xp[i]), "x_s": np.ascontiguousarray(xs[i])}
        m.update(wts)
        m.update(consts)
        in_maps.append(m)
    res = run_bass_kernel_spmd(nc, in_maps, core_ids=list(range(nb)))
    yp = np.stack([np.asarray(r["y_p"], dtype=np.float32) for r in res.results], axis=0)
    ys = np.stack([np.asarray(r["y_s"], dtype=np.float32) for r in res.results], axis=0)
    return (yp, ys)
```
